# Optimizing a Trainium2 kernel written in Bass

```python
import jax, jax.numpy as jnp
from jax import lax
import numpy as np

D_MODEL = 2048
BATCH = 2
SEQ = 4096
DEPTH = 1

GLA_HEADS = 4
GLA_DK = D_MODEL // (2 * GLA_HEADS)
GLA_DV = D_MODEL // GLA_HEADS
GLA_RANK = 16
GLA_TAU = 16.0
GLA_CHUNK = 64
FOX_HEADS = 8
FOX_DH = 128
FOX_BLOCK = 128
D_FF = 5632
CONV_W = 3
PLE_DIM = 256
LN_EPS = 1e-5
RMS_EPS = 1e-6
ALPHA = (2 * DEPTH) ** 0.25
BETA = (8 * DEPTH) ** -0.25

GLA_QK = GLA_HEADS * GLA_DK
GLA_V = GLA_HEADS * GLA_DV
FOX_W = FOX_HEADS * FOX_DH
SPLITS = (GLA_QK, GLA_QK, GLA_V, GLA_V, GLA_RANK,
          FOX_W, FOX_W, FOX_W, FOX_HEADS,
          D_MODEL, D_MODEL)
N_IN = sum(SPLITS)

kernel_name = "hybrid_gla_fox_gated_merge_deepnorm"


def layer_norm(x, g, b):
    xf = x.astype(jnp.float32)
    mu = jnp.mean(xf, -1, keepdims=True)
    var = jnp.mean(jnp.square(xf - mu), -1, keepdims=True)
    return ((xf - mu) * lax.rsqrt(var + LN_EPS) * g.astype(jnp.float32)
            + b.astype(jnp.float32)).astype(x.dtype)


def gla_chunked(q, k, v, log_a):
    B, S, H, DK = q.shape
    DV = v.shape[-1]
    n = S // GLA_CHUNK

    def to_chunks(t):
        return t.reshape(B, n, GLA_CHUNK, H, t.shape[-1]).transpose(1, 0, 3, 2, 4)

    qc, kc, vc, gc = (to_chunks(t) for t in (q, k, v, log_a))
    causal = jnp.tril(jnp.ones((GLA_CHUNK, GLA_CHUNK), bool))[:, :, None]

    def step(state, inp):
        qb, kb, vb, gb = inp
        bcum = jnp.cumsum(gb, axis=-2)
        inter = jnp.einsum('bhck,bhkv->bhcv', qb * jnp.exp(bcum), state)
        diff = bcum[..., :, None, :] - bcum[..., None, :, :]
        decay = jnp.exp(jnp.where(causal, diff, -jnp.inf))
        attn = jnp.einsum('bhtk,bhsk,bhtsk->bhts', qb, kb, decay)
        intra = jnp.einsum('bhts,bhsv->bhtv', attn, vb)
        blast = bcum[..., -1:, :]
        state = (jnp.exp(blast[..., 0, :])[..., None] * state
                 + jnp.einsum('bhsk,bhsv->bhkv', kb * jnp.exp(blast - bcum), vb))
        return state, inter + intra

    s0 = jnp.zeros((B, H, DK, DV), jnp.float32)
    _, o = lax.scan(step, s0, (qc, kc, vc, gc))
    return o.transpose(1, 0, 3, 2, 4).reshape(B, S, H, DV)


def fox_attention(q, k, v, log_f):
    B, S, H, Dh = q.shape
    nb = S // FOX_BLOCK
    c = jnp.cumsum(log_f, axis=1).transpose(0, 2, 1)
    qh = q.transpose(0, 2, 1, 3) * (Dh ** -0.5)
    kh = k.transpose(0, 2, 1, 3)
    vh = v.transpose(0, 2, 1, 3)
    qb = qh.reshape(B, H, nb, FOX_BLOCK, Dh).transpose(2, 0, 1, 3, 4)
    cb = c.reshape(B, H, nb, FOX_BLOCK).transpose(2, 0, 1, 3)
    pos_k = jnp.arange(S)

    def block(args):
        i, q_i, c_i = args
        pos_q = i * FOX_BLOCK + jnp.arange(FOX_BLOCK)
        s = (jnp.einsum('bhqd,bhkd->bhqk', q_i, kh)
             + c_i[..., :, None] - c[..., None, :])
        s = jnp.where(pos_k[None, :] <= pos_q[:, None], s, -jnp.inf)
        pr = jax.nn.softmax(s, axis=-1)
        return jnp.einsum('bhqk,bhkd->bhqd', pr, vh)

    o = lax.map(block, (jnp.arange(nb), qb, cb))
    return o.transpose(1, 0, 3, 2, 4).reshape(B, S, H, Dh)


def token_mixers(x, w_in, w_gla_lr, b_gla_lr, gla_norm_g, b_forget,
                 w_branch_gla, w_branch_fox, w_out):
    B, S, _ = x.shape
    f32 = jnp.float32
    proj = x @ w_in
    offsets = np.cumsum(SPLITS)[:-1].tolist()
    gq, gk, gv, gr, glr, fq, fk, fv, ff, ma, mb = jnp.split(proj, offsets, axis=-1)

    q = gq.reshape(B, S, GLA_HEADS, GLA_DK).astype(f32) * (GLA_DK ** -0.5)
    k = gk.reshape(B, S, GLA_HEADS, GLA_DK).astype(f32)
    v = gv.reshape(B, S, GLA_HEADS, GLA_DV).astype(f32)
    log_a = (jax.nn.log_sigmoid((glr @ w_gla_lr + b_gla_lr).astype(f32))
             / GLA_TAU).reshape(B, S, GLA_HEADS, GLA_DK)
    o = gla_chunked(q, k, v, log_a)
    o = o * lax.rsqrt(jnp.mean(o * o, -1, keepdims=True) + RMS_EPS) * gla_norm_g.astype(f32)
    o = (o.reshape(B, S, GLA_V) * jax.nn.silu(gr.astype(f32))).astype(x.dtype)
    y_gla = o @ w_branch_gla

    fqh = fq.reshape(B, S, FOX_HEADS, FOX_DH).astype(f32)
    fkh = fk.reshape(B, S, FOX_HEADS, FOX_DH).astype(f32)
    fvh = fv.reshape(B, S, FOX_HEADS, FOX_DH).astype(f32)
    log_f = jax.nn.log_sigmoid((ff + b_forget).astype(f32))
    of = fox_attention(fqh, fkh, fvh, log_f).reshape(B, S, FOX_W).astype(x.dtype)
    y_fox = of @ w_branch_fox

    merged = jax.nn.sigmoid(ma) * y_gla + jax.nn.sigmoid(mb) * y_fox
    return merged @ w_out


def causal_dwconv(h, w, b):
    S = h.shape[1]
    hp = jnp.pad(h, ((0, 0), (CONV_W - 1, 0), (0, 0)))
    y = b
    for j in range(CONV_W):
        y = y + hp[:, j:j + S, :] * w[j]
    return y


def conv_gated_mlp(x, w_gate, w_up, conv_w, conv_b, w_down):
    g = causal_dwconv(x @ w_gate, conv_w, conv_b)
    h = jax.nn.gelu(g) * (x @ w_up)
    return h @ w_down


def setup_inputs(seed: int = 0) -> dict:
    key = jax.random.key(seed)
    ks = jax.random.split(key, 24)
    f32 = jnp.float32

    def nrm(k, shape, scale):
        return jax.random.normal(k, shape, f32) * scale

    L = DEPTH
    return {
        "x": nrm(ks[0], (BATCH, SEQ, D_MODEL), 1.0),
        "p": nrm(ks[1], (DEPTH, BATCH, SEQ, PLE_DIM), 1.0),
        "w_in": nrm(ks[2], (L, D_MODEL, N_IN), D_MODEL ** -0.5),
        "w_gla_lr": nrm(ks[3], (L, GLA_RANK, GLA_QK), GLA_RANK ** -0.5),
        "b_gla_lr": nrm(ks[4], (L, GLA_QK), 0.1),
        "gla_norm_g": 1.0 + nrm(ks[5], (L, GLA_DV), 0.02),
        "b_forget": jax.random.uniform(ks[6], (L, FOX_HEADS), f32, 1.0, 5.0),
        "w_branch_gla": nrm(ks[7], (L, GLA_V, D_MODEL), GLA_V ** -0.5),
        "w_branch_fox": nrm(ks[8], (L, FOX_W, D_MODEL), FOX_W ** -0.5),
        "w_out": nrm(ks[9], (L, D_MODEL, D_MODEL), BETA * D_MODEL ** -0.5),
        "ln1_g": 1.0 + nrm(ks[10], (L, D_MODEL), 0.02),
        "ln1_b": nrm(ks[11], (L, D_MODEL), 0.02),
        "w_gate": nrm(ks[12], (L, D_MODEL, D_FF), D_MODEL ** -0.5),
        "w_up": nrm(ks[13], (L, D_MODEL, D_FF), D_MODEL ** -0.5),
        "conv_w": nrm(ks[14], (L, CONV_W, D_FF), CONV_W ** -0.5),
        "conv_b": nrm(ks[15], (L, D_FF), 0.02),
        "w_down": nrm(ks[16], (L, D_FF, D_MODEL), BETA * D_FF ** -0.5),
        "ln2_g": 1.0 + nrm(ks[17], (L, D_MODEL), 0.02),
        "ln2_b": nrm(ks[18], (L, D_MODEL), 0.02),
        "w_ple_gate": nrm(ks[19], (L, D_MODEL, D_MODEL), D_MODEL ** -0.5),
        "w_ple_proj": nrm(ks[20], (L, PLE_DIM, D_MODEL), BETA * PLE_DIM ** -0.5),
    }


def reference(x, p, w_in, w_gla_lr, b_gla_lr, gla_norm_g, b_forget,
              w_branch_gla, w_branch_fox, w_out, ln1_g, ln1_b,
              w_gate, w_up, conv_w, conv_b, w_down, ln2_g, ln2_b,
              w_ple_gate, w_ple_proj):
    for i in range(DEPTH):
        mix = token_mixers(x, w_in[i], w_gla_lr[i], b_gla_lr[i], gla_norm_g[i],
                           b_forget[i], w_branch_gla[i], w_branch_fox[i], w_out[i])
        x = layer_norm(ALPHA * x + mix, ln1_g[i], ln1_b[i])
        ffn = conv_gated_mlp(x, w_gate[i], w_up[i], conv_w[i], conv_b[i], w_down[i])
        x = layer_norm(ALPHA * x + ffn, ln2_g[i], ln2_b[i])
        x = x + jax.nn.sigmoid(x @ w_ple_gate[i]) * (p[i] @ w_ple_proj[i])
    return x
```

```python
import numpy as np
import concourse.bass as bass
import concourse.mybir as mybir
from concourse.bass_utils import run_bass_kernel_spmd

F32 = mybir.dt.float32
BF16 = mybir.dt.bfloat16
AF = mybir.ActivationFunctionType
ALU = mybir.AluOpType

D = 2048
S_LEN = 4096
NWIN = 4224
NOWN = 1152
NREAL = 1024
OWN0 = NWIN - NOWN
NBLK = NWIN // 128
DC = 16
DFF = 5632
FC = 44
ALPHA = 2.0 ** 0.25
C_GQ, C_GK, C_GV, C_GR, C_GLR = 0, 1024, 2048, 4096, 6144
C_FQ, C_FK, C_FV, C_FF, C_MA, C_MB = 6160, 7184, 8208, 9232, 9240, 11288


class Buf:
    __slots__ = ("w", "r")

    def __init__(self):
        self.w = None
        self.r = []


class Sync:
    def __init__(self, nc, n_dma_sems=20):
        self.nc = nc
        self.engs = {"pe": nc.tensor, "act": nc.scalar, "dve": nc.vector, "pool": nc.gpsimd, "sp": nc.sync}
        self.semobj = {}
        self.cnt = {}
        for k in self.engs:
            self.semobj[k] = nc.alloc_semaphore("sem_" + k)
            self.cnt[k] = 0
        self.waited = {k: {} for k in self.engs}
        self.npe = 0
        self.marks = []
        self.dq = {}
        for q in ("sp", "pool"):
            keys = []
            for i in range(n_dma_sems):
                key = "d_%s_%d" % (q, i)
                self.semobj[key] = nc.alloc_semaphore(key)
                self.cnt[key] = 0
                keys.append(key)
            self.dq[q] = [keys, 0]

    def _wait(self, eng, deps):
        need = {}
        for tok in deps:
            if tok is None:
                continue
            k, v = tok
            if k == "pe" and eng == "pe":
                continue
            if v > need.get(k, 0):
                need[k] = v
        for k, v in need.items():
            if self.waited[eng].get(k, 0) >= v:
                continue
            self.engs[eng].wait_ge(self.semobj[k], v)
            self.waited[eng][k] = v

    def _deps(self, reads, writes):
        deps = []
        for b in reads:
            deps.append(b.w)
        for b in writes:
            deps.append(b.w)
            deps.extend(b.r)
        return deps

    def op(self, eng, fn, reads=(), writes=(), inc=True):
        self._wait(eng, self._deps(reads, writes))
        inst = fn(self.engs[eng])
        if eng == "pe":
            self.npe += 1
        tok = (eng, self.cnt[eng] + 1)
        if inc:
            inst.then_inc(self.semobj[eng], 1)
            self.cnt[eng] += 1
        for b in reads:
            b.r.append(tok)
        for b in writes:
            b.w = tok
            b.r = []
        return inst

    def dma(self, q, out, in_, reads=(), writes=()):
        keys, idx = self.dq[q]
        key = keys[idx % len(keys)]
        self.dq[q][1] = idx + 1
        deps = self._deps(reads, writes)
        deps.append((key, self.cnt[key]) if self.cnt[key] else None)
        self._wait(q, deps)
        inst = self.engs[q].dma_start(out=out, in_=in_)
        inst.then_inc(self.semobj[key], 16)
        self.cnt[key] += 16
        tok = (key, self.cnt[key])
        for b in reads:
            b.r.append(tok)
        for b in writes:
            b.w = tok
            b.r = []
        return inst

    def barrier(self):
        self.marks.append(self.npe)
        toks = [(k, v) for k, v in self.cnt.items() if v > 0]
        for e in self.engs:
            self._wait(e, [t for t in toks if not (t[0] == e)])


def build_program():
    import os
    KSTOP = int(os.environ.get("KSTOP", "99"))
    nc = bass.Bass("TRN2", target_bir_lowering=False)

    def din(name, shape, dt=F32):
        return nc.dram_tensor(name, list(shape), dt, kind="ExternalInput").ap()

    xT = din("xT", [D, NWIN])
    pT = din("pT", [256, NREAL])
    kmask_d = din("kmask", [128, NBLK])
    halo_d = din("halo", [128, 1])
    ident_d = din("ident", [128, 128])
    tri_d = din("tri", [128, 128])
    reset_d = din("reset", [128, NOWN])
    w_in = din("w_in", [D, 13336])
    w_lr = din("w_gla_lr", [16, 1024])
    blr_d = din("blr", [128, 8])
    gn_d = din("gn", [128, 4])
    bf_d = din("bfor", [128, 8])
    w_bg = din("w_branch_gla", [D, D])
    w_bf = din("w_branch_fox", [1024, D])
    w_out = din("w_out", [D, D])
    ln1g_d = din("ln1g", [128, DC])
    ln1b_d = din("ln1b", [128, DC])
    w_gate = din("w_gate", [D, DFF])
    w_up = din("w_up", [D, DFF])
    convw_d = din("convw", [128, FC, 3])
    convb_d = din("convb", [128, FC])
    w_down = din("w_down", [DFF, D])
    ln2g_d = din("ln2g", [128, DC])
    ln2b_d = din("ln2b", [128, DC])
    w_pg = din("w_ple_gate", [D, D])
    w_pp = din("w_ple_proj", [256, D])
    yT = nc.dram_tensor("yT", [D, NREAL], F32, kind="ExternalOutput").ap()
    x1s = nc.dram_tensor("x1s", [D, NREAL], F32, kind="Internal").ap()

    S = Sync(nc)
    NA_BYTES = 212736
    arena = nc.alloc_sbuf_tensor("arena", [128, NA_BYTES // 4], F32)
    ar = {"off": 0}

    def at(off):
        ar["off"] = off

    def sb(name, shape, dt):
        n = 1
        for s_ in shape[1:]:
            n *= s_
        nbytes = n * (2 if dt == BF16 else 4)
        off = (ar["off"] + 31) // 32 * 32
        assert off + nbytes <= NA_BYTES, (name, off, nbytes)
        ar["off"] = off + nbytes
        v = arena[0:shape[0], off // 4:(off + nbytes) // 4]
        if dt == BF16:
            v = v.bitcast(BF16)
        if len(shape) == 3:
            v = v.rearrange("p (a b) -> p a b", a=shape[1])
        elif len(shape) == 4:
            v = v.rearrange("p (a b c) -> p a b c", a=shape[1], b=shape[2])
        return v
    NPS = 7
    ps_t = [nc.alloc_psum_tensor("ps%d" % i, [128, 512], F32) for i in range(NPS)]
    ps_b = [Buf() for _ in range(NPS)]
    pst = nc.alloc_psum_tensor("pst", [128, 1024], BF16)
    pst_b = [Buf() for _ in range(8)]
    st = {"ps": 0, "psa": 0, "pst": 0, "ev": 0}

    def PS():
        i = st["ps"] % 5
        st["ps"] += 1
        return ps_t[i], ps_b[i]

    def PSA():
        i = 5 + st["psa"] % 2
        st["psa"] += 1
        return ps_t[i], ps_b[i]

    def PST():
        i = st["ps"] % 5
        st["ps"] += 1
        return ps_t[i][:, 0:64].bitcast(BF16), ps_b[i]

    def evac(out, in_, reads, writes):
        st["ev"] += 1
        if st["ev"] % 2:
            S.op("act", lambda e: e.copy(out=out, in_=in_), reads=reads, writes=writes)
        else:
            S.op("dve", lambda e: e.tensor_copy(out=out, in_=in_), reads=reads, writes=writes)

    ident = sb("ident", [128, 128], BF16); ident_B = Buf()
    tri = sb("tri", [128, 128], F32); tri_B = Buf()
    trib = sb("trib", [128, 128], BF16); trib_B = Buf()
    ones = sb("ones", [128, 128], F32); ones_B = Buf()
    reset = sb("reset", [128, NOWN], F32); reset_B = Buf()
    kmask = sb("kmask", [128, NBLK], F32); kmask_B = Buf()
    halo = sb("halo", [128, 1], F32); halo_B = Buf()
    cst = sb("cst", [128, 4], F32); cst_B = Buf()
    nblr = sb("nblr", [128, 8], F32); nblr_B = Buf()
    gn = sb("gn", [128, 4], F32); gn_B = Buf()
    bfor = sb("bfor", [128, 8], F32); bfor_B = Buf()
    ln1g = sb("ln1g", [128, DC], F32); ln1b = sb("ln1b", [128, DC], F32)
    ln2g = sb("ln2g", [128, DC], F32); ln2b = sb("ln2b", [128, DC], F32)
    convw = sb("convw", [128, FC, 3], F32); convb = sb("convb", [128, FC], F32)
    prm_B = Buf()
    wlr = sb("wlr", [32, 1024], BF16); wlr_B = Buf()
    ones33 = sb("ones33", [128, NBLK], F32)

    S.dma("pool", ident[:], ident_d, writes=[ident_B])
    S.dma("pool", trib[:], tri_d, writes=[trib_B])
    S.op("dve", lambda e: e.memset(wlr[:], 0.0), writes=[wlr_B])
    S.dma("pool", wlr[0:16, :], w_lr, writes=[wlr_B])
    S.dma("sp", tri[:], tri_d, writes=[tri_B])
    S.dma("sp", reset[:], reset_d, writes=[reset_B])
    S.dma("sp", kmask[:], kmask_d, writes=[kmask_B])
    S.dma("sp", halo[:], halo_d, writes=[halo_B])
    S.dma("sp", nblr[:], blr_d, writes=[nblr_B])
    S.dma("sp", gn[:], gn_d, writes=[gn_B])
    S.dma("sp", bfor[:], bf_d, writes=[bfor_B])
    for t_, d_ in ((ln1g, ln1g_d), (ln1b, ln1b_d), (ln2g, ln2g_d), (ln2b, ln2b_d), (convw, convw_d), (convb, convb_d)):
        S.dma("sp", t_[:], d_, writes=[prm_B])
    S.op("dve", lambda e: e.memset(ones[:], 1.0), writes=[ones_B])
    S.op("dve", lambda e: e.memset(ones33[:], 1.0), writes=[ones_B])
    S.op("dve", lambda e: e.memset(cst[:, 0:1], 0.0), writes=[cst_B])
    S.op("dve", lambda e: e.memset(cst[:, 1:2], 1.0), writes=[cst_B])
    S.op("dve", lambda e: e.memset(cst[:, 2:3], 1e-5), writes=[cst_B])
    S.op("dve", lambda e: e.memset(cst[:, 3:4], 1e-6), writes=[cst_B])
    S.op("dve", lambda e: e.tensor_scalar(out=nblr[:], in0=nblr[:], scalar1=-1.0, scalar2=None, op0=ALU.mult),
         reads=[nblr_B], writes=[nblr_B])
    S.barrier()
    C0, C1, CEPS_LN, CEPS_RMS = cst[:, 0:1], cst[:, 1:2], cst[:, 2:3], cst[:, 3:4]
    if KSTOP == 1:
        return nc


    CONST_END = ar["off"]
    assert CONST_END <= 12288, CONST_END
    OFF_XT = 12288
    OFF_OFT = OFF_XT + 36864
    OFF_OGT = OFF_OFT + 18432
    OFF_LOCAL = OFF_OGT + 36864
    at(OFF_XT)
    XT = sb("XT", [128, DC, NOWN], BF16)
    XT_B = [Buf() for _ in range(4)]
    OFT = sb("OFT", [128, 8, NOWN], BF16); OFT_B = Buf()
    OGT = sb("OGT", [128, 16, NOWN], BF16); OGT_B = Buf()
    wp = {"bufs": [], "B": [], "i": 0}

    def wp_setup(nbuf, nbytes):
        wp["bufs"] = [sb("WP%d" % i, [128, nbytes // 2], BF16) for i in range(nbuf)]
        wp["B"] = [Buf() for _ in range(nbuf)]
        wp["i"] = 0

    def wpanel(wsrc, c0, ncols, kc=DC, kc0=0):
        i = wp["i"] % len(wp["bufs"])
        wp["i"] += 1
        src = wsrc.rearrange("(kc p) n -> p kc n", p=128)[:, kc0:kc0 + kc, c0:c0 + ncols]
        v = wp["bufs"][i][:, 0:kc * ncols].rearrange("p (k n) -> p k n", k=kc)
        S.dma("pool", v, src, writes=[wp["B"][i]])
        return v, wp["B"][i]

    def load_xt(t0, ntok):
        for g in range(4):
            src = xT.rearrange("(dc p) t -> p dc t", p=128)[:, 4 * g:4 * g + 4, t0:t0 + ntok]
            S.dma("pool", XT[:, 4 * g:4 * g + 4, 0:ntok], src, writes=[XT_B[g]])

    def ttiles(ntok):
        n = 512 if ntok % 512 == 0 else 384
        return [(a, n) for a in range(0, ntok, n)]

    def proj_fm(w, wB, wcol, act, actB, t0, n, cb, kc=DC):
        p, pB = PS()
        for dc in range(kc):
            rb = actB[dc * len(actB) // kc] if isinstance(actB, list) else actB
            S.op("pe", lambda e, dc=dc: e.matmul(p[:, 0:n], lhsT=w[:, dc, wcol:wcol + 128], rhs=act[:, dc, t0:t0 + n],
                                                  start=(dc == 0), stop=(dc == kc - 1)),
                 reads=[wB, rb], writes=[pB], inc=(dc == kc - 1))
        cb(p, pB)

    def proj_tm(w, wB, ncols, act, actB, tb, cb, kc=DC):
        p, pB = PS()
        for dc in range(kc):
            rb = actB[dc * len(actB) // kc] if isinstance(actB, list) else actB
            S.op("pe", lambda e, dc=dc: e.matmul(p[:, 0:ncols], lhsT=act[:, dc, tb * 128:(tb + 1) * 128], rhs=w[:, dc, 0:ncols],
                                                  start=(dc == 0), stop=(dc == kc - 1)),
                 reads=[wB, rb], writes=[pB], inc=(dc == kc - 1))
        cb(p, pB)

    SUPER = [(0, 1024), (1024, 1024), (2048, 1024), (OWN0, NOWN)]
    w_in_v = w_in.rearrange("(kc p) n -> p kc n", p=128)

    at(OFF_OGT)
    KT = sb("KT", [128, 4, NWIN], BF16); KT_B = Buf()
    VA = sb("VA", [128, NBLK, 4, 130], BF16); VA_B = Buf()
    QT = sb("QT", [128, 4, NOWN], BF16); QT_B = Buf()
    LF = sb("LF", [128, NBLK, 8], F32); LF_B = Buf()
    CK = sb("CK", [128, NBLK, 8], F32); CK_B = Buf()
    TOT = sb("TOT", [128, NBLK, 8], F32); TOT_B = Buf()
    INC = sb("INC", [128, NBLK, 8], F32); INC_B = Buf()
    WFF = sb("WFF", [128, DC, 8], BF16); WFF_B = Buf()
    BI = [sb("BI%d" % i, [128, NBLK], F32) for i in range(2)]; BI_B = [Buf(), Buf()]
    PT4 = [sb("PT4%d" % i, [128, 512], BF16) for i in range(4)]; PT4_B = [Buf() for _ in range(4)]
    TMP = [sb("TMP%d" % i, [128, 512], F32) for i in range(2)]; TMP_B = [Buf(), Buf()]
    ON = [sb("ON%d" % i, [128, 128], BF16) for i in range(2)]; ON_B = [Buf(), Buf()]
    sm = sb("sm", [128, 8], F32); sm_B = Buf()
    t8 = sb("t8", [128, 8], F32); t8_B = Buf()
    wp_setup(3, 16384)
    S.dma("pool", WFF, w_in_v[:, :, C_FF:C_FF + 8], writes=[WFF_B])
    S.op("dve", lambda e: e.memset(VA[:, :, :, 128:130], 1.0), writes=[VA_B])
    SCL = 128.0 ** -0.5
    cnt = {"pt": 0, "on": 0, "bi": 0, "tm": 0, "cur_bi": 0, "po": None}

    XTF = [XT.rearrange("p a b -> p (a b)")[:, i * DC * 512:(i + 1) * DC * 512].rearrange("p (a b) -> p a b", a=DC) for i in range(2)]
    XTF_B = [[Buf() for _ in range(4)] for _ in range(2)]
    FT = [(512 * k_, 512) for k_ in range(8)] + [(4096, 128)]
    xT_v = xT.rearrange("(dc p) t -> p dc t", p=128)
    for hp in range(2):
        wK, wKB = wpanel(w_in, C_FK + hp * 512, 512)
        wV, wVB = wpanel(w_in, C_FV + hp * 512, 512)
        wQ, wQB = wpanel(w_in, C_FQ + hp * 512, 512)
        for ti, (t0, n) in enumerate(FT):
            b = ti % 2
            X_, XB_ = XTF[b], XTF_B[b]
            for g in range(4):
                S.dma("pool", X_[:, 4 * g:4 * g + 4, 0:n], xT_v[:, 4 * g:4 * g + 4, t0:t0 + n], writes=[XB_[g]])
            for h in range(4):
                proj_fm(wK, wKB, h * 128, X_, XB_, 0, n,
                        lambda p, pB, h=h, n=n, t0=t0: evac(KT[:, h, t0:t0 + n], p[:, 0:n], [pB], [KT_B]))
            for tb in range(n // 128):
                blk = t0 // 128 + tb
                proj_tm(wV, wVB, 512, X_, XB_, tb,
                        lambda p, pB, blk=blk: evac(VA[:, blk, :, 0:128], p[:, 0:512].rearrange("p (h d) -> p h d", h=4), [pB], [VA_B]))
            if hp == 0:
                for tb in range(n // 128):
                    blk = t0 // 128 + tb

                    def ffcb(p, pB, blk=blk):
                        S.op("dve", lambda e: e.tensor_tensor(out=t8, in0=p[:, 0:8], in1=bfor, op=ALU.add),
                             reads=[pB, bfor_B], writes=[t8_B])
                        S.op("act", lambda e: e.activation(out=t8, in_=t8, func=AF.Exp, scale=-1.0, bias=C0),
                             reads=[t8_B, cst_B], writes=[t8_B])
                        S.op("act", lambda e: e.activation(out=LF[:, blk, :], in_=t8, func=AF.Ln, scale=1.0, bias=C1),
                             reads=[t8_B, cst_B], writes=[LF_B])
                    proj_tm(WFF, WFF_B, 8, X_, XB_, tb, ffcb)
            if t0 >= OWN0:
                for h in range(4):
                    proj_fm(wQ, wQB, h * 128, X_, XB_, 0, n,
                            lambda p, pB, h=h, n=n, t0=t0: evac(QT[:, h, t0 - OWN0:t0 - OWN0 + n], p[:, 0:n], [pB], [QT_B]))
        if hp == 0:
            LFf = LF.rearrange("p b h -> p (b h)")
            p1, p1B = PS()
            S.op("pe", lambda e: e.matmul(p1[:, 0:NBLK * 8], lhsT=tri, rhs=LFf, start=True, stop=True),
                 reads=[tri_B, LF_B], writes=[p1B])
            p2, p2B = PS()
            S.op("pe", lambda e: e.matmul(p2[:, 0:NBLK * 8], lhsT=ones, rhs=LFf, start=True, stop=True),
                 reads=[ones_B, LF_B], writes=[p2B])
            S.op("dve", lambda e: e.tensor_copy(out=TOT.rearrange("p b h -> p (b h)"), in_=p2[:, 0:NBLK * 8]),
                 reads=[p2B], writes=[TOT_B])
            for h in range(8):
                S.op("dve", lambda e, h=h: e.tensor_tensor_scan(out=INC[:, :, h], data0=ones33, data1=TOT[:, :, h],
                                                                 initial=0.0, op0=ALU.mult, op1=ALU.add),
                     reads=[TOT_B, ones_B], writes=[INC_B])
            S.op("dve", lambda e: e.tensor_tensor(out=CK, in0=INC, in1=TOT, op=ALU.subtract),
                 reads=[INC_B, TOT_B], writes=[CK_B])
            S.op("dve", lambda e: e.tensor_tensor(out=CK.rearrange("p b h -> p (b h)"), in0=CK.rearrange("p b h -> p (b h)"),
                                                  in1=p1[:, 0:NBLK * 8], op=ALU.add),
                 reads=[CK_B, p1B], writes=[CK_B])
            S.op("dve", lambda e: e.tensor_tensor(out=CK, in0=CK, in1=kmask.unsqueeze(2).broadcast_to([128, NBLK, 8]), op=ALU.add),
                 reads=[CK_B, kmask_B], writes=[CK_B])
        if KSTOP == 3:
            S.barrier()
            return nc
        items = []
        for h in range(4):
            for i in range(NOWN // 128):
                qb = OWN0 // 128 + i
                js = list(range(0, qb + 1, 4))
                for gi, j0 in enumerate(js):
                    items.append({"h": h, "i": i, "qb": qb, "j0": j0, "nj": min(4, qb + 1 - j0),
                                  "first": gi == 0, "last": gi == len(js) - 1})

        def stageA(it):
            p, pB = PS()
            it["p"], it["pB"] = p, pB
            h, i, j0, nj = it["h"], it["i"], it["j0"], it["nj"]
            for jj in range(nj):
                S.op("pe", lambda e, jj=jj: e.matmul(p[:, jj * 128:(jj + 1) * 128], lhsT=KT[:, h, (j0 + jj) * 128:(j0 + jj + 1) * 128],
                                                     rhs=QT[:, h, i * 128:(i + 1) * 128], start=True, stop=True),
                     reads=[KT_B, QT_B], writes=[pB], inc=(jj == nj - 1))

        def stageB(it):
            h, i, j0, nj, qb = it["h"], it["i"], it["j0"], it["nj"], it["qb"]
            hg = hp * 4 + h
            if it["first"]:
                bi = cnt["bi"] % 2; cnt["bi"] += 1
                cnt["cur_bi"] = bi
                S.op("dve", lambda e: e.tensor_scalar(out=BI[bi], in0=CK[:, :, hg], scalar1=INC[:, qb, hg:hg + 1],
                                                      scalar2=None, op0=ALU.subtract),
                     reads=[CK_B, INC_B], writes=[BI_B[bi]])
            bi = cnt["cur_bi"]
            p, pB = it["p"], it["pB"]
            t = cnt["tm"] % 2; cnt["tm"] += 1
            k = cnt["pt"] % 4; cnt["pt"] += 1
            it["k"] = k
            w_ = nj * 128
            S.op("dve", lambda e: e.scalar_tensor_tensor(out=TMP[t][:, 0:w_].rearrange("p (j t) -> p j t", j=nj),
                                                         in0=p[:, 0:w_].rearrange("p (j t) -> p j t", j=nj), scalar=SCL,
                                                         in1=BI[bi][:, j0:j0 + nj].unsqueeze(2).broadcast_to([128, nj, 128]),
                                                         op0=ALU.mult, op1=ALU.add),
                 reads=[pB, BI_B[bi]], writes=[TMP_B[t]])
            S.op("act", lambda e: e.activation(out=PT4[k][:, 0:w_], in_=TMP[t][:, 0:w_], func=AF.Exp, scale=1.0, bias=C0),
                 reads=[TMP_B[t], cst_B], writes=[PT4_B[k]])
            if it["last"]:
                S.op("dve", lambda e: e.tensor_tensor(out=PT4[k][:, w_ - 128:w_], in0=PT4[k][:, w_ - 128:w_], in1=trib, op=ALU.mult),
                     reads=[PT4_B[k], trib_B], writes=[PT4_B[k]])

        def stageC(it):
            h, i, j0, nj = it["h"], it["i"], it["j0"], it["nj"]
            hg = hp * 4 + h
            k = it["k"]
            if it["first"]:
                cnt["po"] = PSA()
            po, poB = cnt["po"]
            for jj in range(nj):
                S.op("pe", lambda e, jj=jj: e.matmul(po[:, 0:130], lhsT=PT4[k][:, jj * 128:(jj + 1) * 128], rhs=VA[:, j0 + jj, h, :],
                                                     start=(it["first"] and jj == 0), stop=(it["last"] and jj == nj - 1)),
                     reads=[PT4_B[k], VA_B], writes=[poB], inc=(jj == nj - 1))
            if it["last"]:
                S.op("dve", lambda e: e.tensor_scalar(out=sm[:, 0:1], in0=po[:, 128:129], scalar1=1e-30, scalar2=None, op0=ALU.max),
                     reads=[poB], writes=[sm_B])
                S.op("dve", lambda e: e.reciprocal(out=sm[:, 1:2], in_=sm[:, 0:1]), reads=[sm_B], writes=[sm_B])
                o = cnt["on"] % 2; cnt["on"] += 1
                S.op("dve", lambda e: e.tensor_scalar(out=ON[o], in0=po[:, 0:128], scalar1=sm[:, 1:2], scalar2=None, op0=ALU.mult),
                     reads=[poB, sm_B], writes=[ON_B[o]])
                pt_, ptB = PST()
                S.op("pe", lambda e: e.transpose(pt_, ON[o], ident), reads=[ON_B[o], ident_B], writes=[ptB])
                evac(OFT[:, hg, i * 128:(i + 1) * 128], pt_, [ptB], [OFT_B])

        n_it = len(items)
        for kk in range(n_it + 3):
            if kk < n_it:
                stageA(items[kk])
            if 0 <= kk - 1 < n_it:
                stageB(items[kk - 1])
            if 0 <= kk - 3 < n_it:
                stageC(items[kk - 3])
    S.barrier()

    if KSTOP == 4:
        return nc
    at(OFF_LOCAL)
    wp_setup(2, 8192)
    Sst = [sb("Sst%d" % h, [128, 2, 512], F32) for h in range(4)]; Sst_B = [[Buf(), Buf()] for _ in range(4)]
    Sbf = sb("Sbf", [128, 2, 512], BF16); Sbf_B = [Buf(), Buf()]
    GLR = sb("GLR", [32, NOWN], BF16); GLR_B = Buf()
    S.op("dve", lambda e: e.memset(GLR, 0.0), writes=[GLR_B])
    WG16 = sb("WG16", [128, DC, 16], BF16); WG16_B = Buf()
    S.dma("pool", WG16, w_in_v[:, :, C_GLR:C_GLR + 16], writes=[WG16_B])
    for h in range(4):
        S.op("dve", lambda e, h=h: e.memset(Sst[h], 0.0), writes=Sst_B[h])
    Lb = sb("Lb", [128, NOWN], F32); Lb_B = Buf()
    Bb = sb("Bb", [128, NOWN], F32); Bb_B = Buf()
    Db = sb("Db", [128, NOWN], F32); Db_B = Buf()
    EN = sb("EN", [128, NOWN], F32); EN_B = Buf()
    EP = sb("EP", [128, NOWN], F32); EP_B = Buf()
    EBL = sb("EBL", [128, 2, 16], F32); EBL_B = Buf()
    KHT = sb("KHT", [128, 2, NOWN], BF16); KHT_B = Buf()
    KNT = sb("KNT", [128, 2, NOWN], BF16); KNT_B = Buf()
    QPT = sb("QPT", [128, 2, NOWN], BF16); QPT_B = Buf()
    KHt = sb("KHt", [128, 9, 256], BF16); KHt_B = Buf()
    Vh = sb("Vh", [128, 9, 512], BF16); Vh_B = Buf()
    AT = sb("AT", [128, 128], BF16); AT_B = Buf()
    SG = sb("SG", [128, 4, NOWN], BF16); SG_B = Buf()
    SQ4 = sb("SQ4", [128, 512], BF16); SQ4_B = Buf()
    TT4 = sb("TT4", [128, 512], F32); TT4_B = Buf()
    onesb = sb("onesb", [128, 128], BF16); onesb_B = Buf()
    S.op("dve", lambda e: e.memset(onesb, 1.0), writes=[onesb_B])
    RR = sb("RR", [128, 128], F32); RR_B = Buf()

    for (t0, ntok) in SUPER:
        load_xt(t0, ntok)
        own = (t0 == OWN0)
        nb = ntok // 128
        for (a, n) in ttiles(ntok):
            p, pB = PS()
            for dc in range(DC):
                S.op("pe", lambda e, dc=dc, p=p, a=a, n=n: e.matmul(p[0:16, 0:n], lhsT=WG16[:, dc, :], rhs=XT[:, dc, a:a + n],
                                                                  start=(dc == 0), stop=(dc == DC - 1)),
                     reads=[WG16_B, XT_B[dc // 4]], writes=[pB], inc=(dc == DC - 1))
            evac(GLR[0:16, a:a + n], p[0:16, 0:n], [pB], [GLR_B])
        if KSTOP == 51:
            S.barrier()
            return nc
        if KSTOP == 55 and own:
            S.barrier()
            return nc
        for h in range(4):
            wk, wkB = wpanel(w_in, C_GK + h * 256, 256)
            if own:
                wq, wqB = wpanel(w_in, C_GQ + h * 256, 256)
            for kc in range(2):
                col = h * 256 + kc * 128
                for (a, n) in ttiles(ntok):
                    p, pB = PS()
                    S.op("pe", lambda e, p=p, a=a, n=n, col=col: e.matmul(p[:, 0:n], lhsT=wlr[:, col:col + 128], rhs=GLR[:, a:a + n],
                                                                        start=True, stop=True),
                         reads=[wlr_B, GLR_B], writes=[pB])
                    S.op("act", lambda e, p=p, a=a, n=n, h=h, kc=kc: e.activation(out=Lb[:, a:a + n], in_=p[:, 0:n], func=AF.Exp, scale=-1.0,
                                                                                bias=nblr[:, h * 2 + kc:h * 2 + kc + 1]),
                         reads=[pB, nblr_B], writes=[Lb_B])
                if KSTOP == 511:
                    S.barrier()
                    return nc
                S.op("act", lambda e: e.activation(out=Lb[:, 0:ntok], in_=Lb[:, 0:ntok], func=AF.Ln, scale=1.0, bias=C1),
                     reads=[Lb_B, cst_B], writes=[Lb_B])
                if KSTOP == 512:
                    S.barrier()
                    return nc
                S.op("dve", lambda e: e.tensor_tensor_scan(out=Bb[:, 0:ntok], data0=reset[:, 0:ntok], data1=Lb[:, 0:ntok],
                                                           initial=0.0, op0=ALU.mult, op1=ALU.add),
                     reads=[reset_B, Lb_B], writes=[Bb_B])
                if KSTOP == 513:
                    S.barrier()
                    return nc
                B3 = Bb[:, 0:ntok].rearrange("p (b t) -> p b t", t=128)
                S.op("dve", lambda e: e.tensor_tensor(out=Db[:, 0:ntok].rearrange("p (b t) -> p b t", t=128), in0=B3,
                                                      in1=B3[:, :, 127:128].broadcast_to([128, nb, 128]), op=ALU.subtract),
                     reads=[Bb_B], writes=[Db_B])
                S.op("act", lambda e: e.activation(out=Db[:, 0:ntok], in_=Db[:, 0:ntok], func=AF.Exp, scale=1.0 / 16, bias=C0),
                     reads=[Db_B, cst_B], writes=[Db_B])
                S.op("act", lambda e, kc=kc: e.activation(out=EBL[:, kc, 0:nb], in_=B3[:, :, 127], func=AF.Exp, scale=-1.0 / 16, bias=C0),
                     reads=[Bb_B, cst_B], writes=[EBL_B])
                if KSTOP == 514:
                    S.barrier()
                    return nc
                if own:
                    S.op("act", lambda e: e.activation(out=EN[:, 0:ntok], in_=Bb[:, 0:ntok], func=AF.Exp, scale=1.0 / 16, bias=C0),
                         reads=[Bb_B, cst_B], writes=[EN_B])
                    S.op("act", lambda e: e.activation(out=EP[:, 0:ntok], in_=Bb[:, 0:ntok], func=AF.Exp, scale=-1.0 / 16, bias=C0),
                         reads=[Bb_B, cst_B], writes=[EP_B])
                for (a, n) in ttiles(ntok):
                    def kcb(p, pB, a=a, n=n, kc=kc):
                        S.op("dve", lambda e: e.tensor_tensor(out=KHT[:, kc, a:a + n], in0=p[:, 0:n], in1=Db[:, a:a + n], op=ALU.mult),
                             reads=[pB, Db_B], writes=[KHT_B])
                        if own:
                            S.op("dve", lambda e: e.tensor_tensor(out=KNT[:, kc, a:a + n], in0=p[:, 0:n], in1=EN[:, a:a + n], op=ALU.mult),
                                 reads=[pB, EN_B], writes=[KNT_B])
                    proj_fm(wk, wkB, kc * 128, XT, XT_B, a, n, kcb)
                    if own:
                        def qcb(p, pB, a=a, n=n, kc=kc):
                            S.op("dve", lambda e: e.scalar_tensor_tensor(out=QPT[:, kc, a:a + n], in0=p[:, 0:n], scalar=256.0 ** -0.5,
                                                                         in1=EP[:, a:a + n], op0=ALU.mult, op1=ALU.mult),
                                 reads=[pB, EP_B], writes=[QPT_B])
                        proj_fm(wq, wqB, kc * 128, XT, XT_B, a, n, qcb)
                if KSTOP == 515:
                    S.barrier()
                    return nc
                for tb in range(nb):
                    pt_, ptB = PST()
                    S.op("pe", lambda e, pt_=pt_, tb=tb, kc=kc: e.transpose(pt_, KHT[:, kc, tb * 128:(tb + 1) * 128], ident),
                         reads=[KHT_B, ident_B], writes=[ptB])
                    evac(KHt[:, tb, kc * 128:(kc + 1) * 128], pt_, [ptB], [KHt_B])
            if KSTOP == 52:
                S.barrier()
                return nc
            for half in range(2):
                wv, wvB = wpanel(w_in, C_GV + h * 512 + half * 256, 256)
                for tb in range(nb):
                    proj_tm(wv, wvB, 256, XT, XT_B, tb,
                            lambda p, pB, tb=tb, half=half: evac(Vh[:, tb, half * 256:(half + 1) * 256], p[:, 0:256], [pB], [Vh_B]))
            if own:
                for half in range(2):
                    wr, wrB = wpanel(w_in, C_GR + h * 512 + half * 256, 256)
                    for v2 in range(2):
                        vc = half * 2 + v2
                        for (a, n) in ttiles(ntok):
                            proj_fm(wr, wrB, v2 * 128, XT, XT_B, a, n,
                                    lambda p, pB, vc=vc, a=a, n=n: (
                                        S.op("act", lambda e: e.activation(out=SG[:, vc, a:a + n], in_=p[:, 0:n], func=AF.Silu, scale=1.0, bias=C0),
                                             reads=[pB, cst_B], writes=[SG_B]),
                                        S.op("dve", lambda e: e.tensor_scalar(out=SG[:, vc, a:a + n], in0=SG[:, vc, a:a + n], scalar1=gn[:, vc:vc + 1],
                                                                              scalar2=None, op0=ALU.mult),
                                             reads=[gn_B], writes=[SG_B])))
                S.op("act", lambda e, h=h: e.copy(out=Sbf, in_=Sst[h]), reads=Sst_B[h], writes=Sbf_B)
            if KSTOP == 53:
                S.barrier()
                return nc
            for tb in range(nb):
                if own:
                    pa, paB = PS()
                    for kc in range(2):
                        S.op("pe", lambda e, kc=kc, tb=tb, pa=pa: e.matmul(pa[:, 0:128], lhsT=KNT[:, kc, tb * 128:(tb + 1) * 128],
                                                                         rhs=QPT[:, kc, tb * 128:(tb + 1) * 128], start=(kc == 0), stop=(kc == 1)),
                             reads=[KNT_B, QPT_B], writes=[paB], inc=(kc == 1))
                    S.op("dve", lambda e, pa=pa: e.tensor_tensor(out=AT, in0=pa[:, 0:128], in1=tri, op=ALU.mult),
                         reads=[paB, tri_B], writes=[AT_B])
                    po, poB = PSA()
                    for vc in range(4):
                        for kc in range(2):
                            S.op("pe", lambda e, kc=kc, vc=vc, tb=tb, po=po: e.matmul(po[:, vc * 128:(vc + 1) * 128],
                                                                                    lhsT=Sbf[:, kc, vc * 128:(vc + 1) * 128],
                                                                                    rhs=QPT[:, kc, tb * 128:(tb + 1) * 128],
                                                                                    start=(kc == 0), stop=False),
                                 reads=[Sbf_B[kc], QPT_B], writes=[poB], inc=False)
                        S.op("pe", lambda e, vc=vc, tb=tb, po=po: e.matmul(po[:, vc * 128:(vc + 1) * 128], lhsT=Vh[:, tb, vc * 128:(vc + 1) * 128],
                                                                         rhs=AT, start=False, stop=True),
                             reads=[Vh_B, AT_B], writes=[poB], inc=(vc == 3))
                if not (own and tb == nb - 1):
                    for kc in range(2):
                        pu, puB = PS()
                        S.op("pe", lambda e, kc=kc, tb=tb, pu=pu: e.matmul(pu[:, 0:512], lhsT=KHt[:, tb, kc * 128:(kc + 1) * 128], rhs=Vh[:, tb, :],
                                                                         start=True, stop=True),
                             reads=[KHt_B, Vh_B], writes=[puB])
                        S.op("dve", lambda e, kc=kc, tb=tb, pu=pu, h=h: e.scalar_tensor_tensor(out=Sst[h][:, kc, :], in0=Sst[h][:, kc, :],
                                                                                             scalar=EBL[:, kc, tb:tb + 1], in1=pu[:, 0:512],
                                                                                             op0=ALU.mult, op1=ALU.add),
                             reads=[EBL_B, puB], writes=[Sst_B[h][kc]])
                        if own:
                            S.op("act", lambda e, h=h, kc=kc: e.copy(out=Sbf[:, kc, :], in_=Sst[h][:, kc, :]),
                                 reads=[Sst_B[h][kc]], writes=[Sbf_B[kc]])
                if own:
                    pr, prB = PS()
                    S.op("act", lambda e, po=po: e.activation(out=SQ4, in_=po[:, 0:512], func=AF.Square, scale=1.0, bias=C0),
                         reads=[poB, cst_B], writes=[SQ4_B])
                    for vc in range(4):
                        S.op("pe", lambda e, vc=vc, pr=pr: e.matmul(pr[:, 0:128], lhsT=onesb, rhs=SQ4[:, vc * 128:(vc + 1) * 128],
                                                                    start=(vc == 0), stop=(vc == 3)),
                             reads=[onesb_B, SQ4_B], writes=[prB], inc=(vc == 3))
                    S.op("act", lambda e, pr=pr: e.activation(out=RR, in_=pr[:, 0:128], func=AF.Sqrt, scale=1.0 / 512, bias=CEPS_RMS),
                         reads=[prB, cst_B], writes=[RR_B])
                    S.op("dve", lambda e: e.reciprocal(out=RR, in_=RR), reads=[RR_B], writes=[RR_B])
                    S.op("dve", lambda e, po=po: e.tensor_tensor(out=TT4.rearrange("p (v t) -> p v t", v=4),
                                                                 in0=po[:, 0:512].rearrange("p (v t) -> p v t", v=4),
                                                                 in1=RR.unsqueeze(1).broadcast_to([128, 4, 128]), op=ALU.mult),
                         reads=[poB, RR_B], writes=[TT4_B])
                    S.op("dve", lambda e, tb=tb, h=h: e.tensor_tensor(out=OGT[:, h * 4:(h + 1) * 4, tb * 128:(tb + 1) * 128],
                                                                      in0=TT4.rearrange("p (v t) -> p v t", v=4),
                                                                      in1=SG[:, :, tb * 128:(tb + 1) * 128], op=ALU.mult),
                         reads=[TT4_B, SG_B], writes=[OGT_B])
            if KSTOP == 54:
                S.barrier()
                return nc
    S.barrier()

    if KSTOP == 5:
        return nc
    OFF_MG = OFF_LOCAL
    OFF_X1B = OFF_MG + 36864
    OFF_WP3 = OFF_X1B + 36864
    at(OFF_MG)
    MG = sb("MG", [128, DC, NOWN], BF16); MG_B = Buf()
    X1b = sb("X1b", [128, DC, NOWN], BF16); X1b_B = Buf()
    wp_setup(6, 4096)
    assert ar["off"] <= NA_BYTES
    at(OFF_X1B)
    SA = sb("SA", [128, 512], F32); SA_B = Buf()
    SB_ = sb("SB_", [128, 512], F32); SB_B = Buf()
    T1 = sb("T1", [128, 384], F32); T1_B = Buf()
    T2 = sb("T2", [128, 384], F32); T2_B = Buf()
    for dch in range(DC):
        c0 = dch * 128
        wa, waB = wpanel(w_in, C_MA + c0, 128)
        wb, wbB = wpanel(w_in, C_MB + c0, 128)
        wg_, wgB = wpanel(w_bg, c0, 128)
        wf_, wfB = wpanel(w_bf, c0, 128, kc=8)
        for (a, n) in ttiles(NOWN):
            proj_fm(wa, waB, 0, XT, XT_B, a, n,
                    lambda p, pB, n=n: S.op("act", lambda e: e.activation(out=SA[:, 0:n], in_=p[:, 0:n], func=AF.Sigmoid, scale=1.0, bias=C0),
                                            reads=[pB, cst_B], writes=[SA_B]))
            proj_fm(wb, wbB, 0, XT, XT_B, a, n,
                    lambda p, pB, n=n: S.op("act", lambda e: e.activation(out=SB_[:, 0:n], in_=p[:, 0:n], func=AF.Sigmoid, scale=1.0, bias=C0),
                                            reads=[pB, cst_B], writes=[SB_B]))
            proj_fm(wg_, wgB, 0, OGT, OGT_B, a, n,
                    lambda p, pB, n=n: S.op("dve", lambda e: e.tensor_tensor(out=T1[:, 0:n], in0=p[:, 0:n], in1=SA[:, 0:n], op=ALU.mult),
                                            reads=[pB, SA_B], writes=[T1_B]))
            proj_fm(wf_, wfB, 0, OFT, OFT_B, a, n,
                    lambda p, pB, n=n: S.op("dve", lambda e: e.tensor_tensor(out=T2[:, 0:n], in0=p[:, 0:n], in1=SB_[:, 0:n], op=ALU.mult),
                                            reads=[pB, SB_B], writes=[T2_B]), kc=8)
            S.op("dve", lambda e, a=a, n=n, dch=dch: e.tensor_tensor(out=MG[:, dch, a:a + n], in0=T1[:, 0:n], in1=T2[:, 0:n], op=ALU.add),
                 reads=[T1_B, T2_B], writes=[MG_B])
    S.barrier()

    if KSTOP == 6:
        return nc
    at(OFF_XT)
    R1 = sb("R1", [128, DC, NOWN], F32); R1_B = Buf()
    OFF_T = ar["off"]
    XF = [sb("XF%d" % i, [128, 512], F32) for i in range(2)]; XF_B = [Buf(), Buf()]
    MU = sb("MU", [128, 512], F32); MU_B = Buf()
    VR = sb("VR", [128, 512], F32); VR_B = Buf()
    SQ2 = [sb("SQ2%d" % i, [128, 512], F32) for i in range(2)]; SQ2_B = [Buf(), Buf()]
    TT2 = sb("TT2", [128, 512], F32); TT2_B = Buf()
    assert ar["off"] <= OFF_MG, ar["off"]
    xfc = {"i": 0}
    xTv = xT.rearrange("(dc p) t -> p dc t", p=128)
    for dch in range(DC):
        w, wB = wpanel(w_out, dch * 128, 128)
        for (a, n) in ttiles(NOWN):
            i = xfc["i"] % 2; xfc["i"] += 1
            S.dma("sp", XF[i][:, 0:n], xTv[:, dch, OWN0 + a:OWN0 + a + n], writes=[XF_B[i]])
            proj_fm(w, wB, 0, MG, MG_B, a, n,
                    lambda p, pB, i=i, a=a, n=n, dch=dch: S.op("dve", lambda e: e.scalar_tensor_tensor(out=R1[:, dch, a:a + n], in0=XF[i][:, 0:n], scalar=ALPHA,
                                                                                                  in1=p[:, 0:n], op0=ALU.mult, op1=ALU.add),
                                                               reads=[pB, XF_B[i]], writes=[R1_B]))

    def layer_norm(R, RB, ntok, gam, bet, outb, outbB):
        rbs = []
        for (a, n) in ttiles(ntok):
            p1, p1B = PS()
            for dc in range(DC):
                S.op("pe", lambda e, dc=dc: e.matmul(p1[:, 0:n], lhsT=ones, rhs=R[:, dc, a:a + n], start=(dc == 0), stop=(dc == DC - 1)),
                     reads=[ones_B, RB], writes=[p1B], inc=(dc == DC - 1))
            p2, p2B = PS()
            for dc in range(DC):
                i = dc % 2
                S.op("act", lambda e, dc=dc, i=i: e.activation(out=SQ2[i][:, 0:n], in_=R[:, dc, a:a + n], func=AF.Square, scale=1.0, bias=C0),
                     reads=[RB, cst_B], writes=[SQ2_B[i]])
                S.op("pe", lambda e, dc=dc, i=i: e.matmul(p2[:, 0:n], lhsT=ones, rhs=SQ2[i][:, 0:n], start=(dc == 0), stop=(dc == DC - 1)),
                     reads=[ones_B, SQ2_B[i]], writes=[p2B])
            S.op("dve", lambda e: e.tensor_scalar(out=MU[:, 0:n], in0=p1[:, 0:n], scalar1=1.0 / D, scalar2=None, op0=ALU.mult),
                 reads=[p1B], writes=[MU_B])
            S.op("dve", lambda e: e.tensor_tensor(out=TT2[:, 0:n], in0=MU[:, 0:n], in1=MU[:, 0:n], op=ALU.mult),
                 reads=[MU_B], writes=[TT2_B])
            S.op("dve", lambda e: e.scalar_tensor_tensor(out=VR[:, 0:n], in0=p2[:, 0:n], scalar=1.0 / D, in1=TT2[:, 0:n],
                                                         op0=ALU.mult, op1=ALU.subtract),
                 reads=[p2B, TT2_B], writes=[VR_B])
            S.op("act", lambda e: e.activation(out=VR[:, 0:n], in_=VR[:, 0:n], func=AF.Sqrt, scale=1.0, bias=CEPS_LN),
                 reads=[VR_B, cst_B], writes=[VR_B])
            S.op("dve", lambda e: e.reciprocal(out=VR[:, 0:n], in_=VR[:, 0:n]), reads=[VR_B], writes=[VR_B])
            for dc in range(DC):
                rb = Buf()
                rbs.append(rb)
                S.op("dve", lambda e, dc=dc: e.tensor_tensor(out=R[:, dc, a:a + n], in0=R[:, dc, a:a + n], in1=MU[:, 0:n], op=ALU.subtract),
                     reads=[RB, MU_B, VR_B], writes=[rb])
                S.op("dve", lambda e, dc=dc: e.tensor_tensor(out=R[:, dc, a:a + n], in0=R[:, dc, a:a + n], in1=VR[:, 0:n], op=ALU.mult),
                     reads=[VR_B], writes=[rb])
                S.op("act", lambda e, dc=dc: e.activation(out=outb[:, dc, a:a + n], in_=R[:, dc, a:a + n], func=AF.Identity,
                                                          scale=gam[:, dc:dc + 1], bias=bet[:, dc:dc + 1]),
                     reads=[rb, prm_B], writes=[outbB])
                S.op("act", lambda e, dc=dc: e.activation(out=R[:, dc, a:a + n], in_=R[:, dc, a:a + n], func=AF.Identity,
                                                          scale=gam[:, dc:dc + 1], bias=bet[:, dc:dc + 1]),
                     reads=[prm_B], writes=[rb])
        S.op("act", lambda e: e.copy(out=VR[:, 0:1], in_=VR[:, 0:1]), reads=rbs, writes=[RB, VR_B])

    layer_norm(R1, R1_B, NOWN, ln1g, ln1b, X1b, X1b_B)
    x1sv = x1s.rearrange("(dc p) t -> p dc t", p=128)
    x1s_B = Buf()
    for g in range(4):
        S.dma("sp", x1sv[:, 4 * g:4 * g + 4, :], R1[:, 4 * g:4 * g + 4, 128:NOWN], reads=[R1_B], writes=[x1s_B])
    S.barrier()

    if KSTOP == 7:
        return nc
    at(OFF_WP3)
    wp_setup(4, 6144)
    at(OFF_XT)
    R2 = sb("R2", [128, DC, NREAL], F32); R2_B = Buf()
    GE = sb("GE", [128, NREAL], F32); GE_B = Buf()
    XF = [sb("XFb%d" % i, [128, 512], F32) for i in range(2)]; XF_B = [Buf(), Buf()]
    HB = sb("HB", [128, FC // 2, NREAL], BF16); HB_B = Buf()
    GG = sb("GG", [128, NOWN], F32); GG_B = Buf()
    CV = sb("CV", [128, NREAL], F32); CV_B = Buf()
    assert ar["off"] <= OFF_X1B, ar["off"]
    for grp in range(2):
        for fl in range(FC // 2):
            fc = grp * (FC // 2) + fl
            wg_, wgB = wpanel(w_gate, fc * 128, 128)
            wu_, wuB = wpanel(w_up, fc * 128, 128)
            for (a, n) in ttiles(NOWN):
                proj_fm(wg_, wgB, 0, X1b, X1b_B, a, n, lambda p, pB, a=a, n=n: evac(GG[:, a:a + n], p[:, 0:n], [pB], [GG_B]))
            S.op("dve", lambda e: e.tensor_scalar(out=GG[:, 126:128], in0=GG[:, 126:128], scalar1=halo[:, 0:1], scalar2=None, op0=ALU.mult),
                 reads=[GG_B, halo_B], writes=[GG_B])
            S.op("dve", lambda e, fc=fc: e.tensor_scalar(out=CV, in0=GG[:, 126:126 + NREAL], scalar1=convw[:, fc, 0:1], scalar2=convb[:, fc:fc + 1],
                                                         op0=ALU.mult, op1=ALU.add),
                 reads=[GG_B, prm_B], writes=[CV_B])
            S.op("dve", lambda e, fc=fc: e.scalar_tensor_tensor(out=CV, in0=GG[:, 127:127 + NREAL], scalar=convw[:, fc, 1:2], in1=CV,
                                                                op0=ALU.mult, op1=ALU.add),
                 reads=[GG_B, prm_B, CV_B], writes=[CV_B])
            S.op("dve", lambda e, fc=fc: e.scalar_tensor_tensor(out=CV, in0=GG[:, 128:128 + NREAL], scalar=convw[:, fc, 2:3], in1=CV,
                                                                op0=ALU.mult, op1=ALU.add),
                 reads=[GG_B, prm_B, CV_B], writes=[CV_B])
            S.op("act", lambda e: e.activation(out=GE, in_=CV, func=AF.Gelu, scale=1.0, bias=C0), reads=[CV_B, cst_B], writes=[GE_B])
            for (a, n) in ttiles(NREAL):
                proj_fm(wu_, wuB, 0, X1b[:, :, 128:NOWN], X1b_B, a, n,
                        lambda p, pB, a=a, n=n, fl=fl: S.op("dve", lambda e: e.tensor_tensor(out=HB[:, fl, a:a + n], in0=p[:, 0:n], in1=GE[:, a:a + n], op=ALU.mult),
                                                            reads=[pB, GE_B], writes=[HB_B]))
        for dch in range(DC):
            w, wB = wpanel(w_down, dch * 128, 128, kc=FC // 2, kc0=grp * (FC // 2))
            for (a, n) in ttiles(NREAL):
                if grp == 0:
                    k = xfc["i"] % 2; xfc["i"] += 1
                    S.dma("sp", XF[k][:, 0:n], x1sv[:, dch, a:a + n], reads=[x1s_B], writes=[XF_B[k]])
                    proj_fm(w, wB, 0, HB, HB_B, a, n,
                            lambda p, pB, k=k, a=a, n=n, dch=dch: S.op("dve", lambda e: e.scalar_tensor_tensor(out=R2[:, dch, a:a + n], in0=XF[k][:, 0:n], scalar=ALPHA,
                                                                                                          in1=p[:, 0:n], op0=ALU.mult, op1=ALU.add),
                                                                       reads=[pB, XF_B[k]], writes=[R2_B]), kc=FC // 2)
                else:
                    proj_fm(w, wB, 0, HB, HB_B, a, n,
                            lambda p, pB, a=a, n=n, dch=dch: S.op("dve", lambda e: e.tensor_tensor(out=R2[:, dch, a:a + n], in0=R2[:, dch, a:a + n],
                                                                                                 in1=p[:, 0:n], op=ALU.add),
                                                                  reads=[pB, R2_B], writes=[R2_B]), kc=FC // 2)
    S.barrier()
    if KSTOP == 8:
        return nc
    at(86016)
    MU = sb("MUb", [128, 512], F32); MU_B = Buf()
    VR = sb("VRb", [128, 512], F32); VR_B = Buf()
    SQ2 = [sb("SQ2b%d" % i, [128, 512], F32) for i in range(2)]; SQ2_B = [Buf(), Buf()]
    TT2 = sb("TT2b", [128, 512], F32); TT2_B = Buf()
    X2b = X1b; X2b_B = X1b_B
    layer_norm(R2, R2_B, NREAL, ln2g, ln2b, X2b, X2b_B)

    PTb = sb("PTb", [128, 2, NREAL], BF16); PTb_B = Buf()
    S.dma("pool", PTb, pT.rearrange("(kc p) t -> p kc t", p=128), writes=[PTb_B])
    YO = [sb("YO%d" % i, [128, 512], F32) for i in range(2)]; YO_B = [Buf(), Buf()]
    SGT = sb("SGT", [128, 512], F32); SGT_B = Buf()
    yTv = yT.rearrange("(dc p) t -> p dc t", p=128)
    yc = {"i": 0}
    for dch in range(DC):
        w, wB = wpanel(w_pg, dch * 128, 128)
        w2, w2B = wpanel(w_pp, dch * 128, 128, kc=2)
        for (a, n) in ttiles(NREAL):
            proj_fm(w, wB, 0, X2b, X2b_B, a, n,
                    lambda p, pB, n=n: S.op("act", lambda e: e.activation(out=SGT[:, 0:n], in_=p[:, 0:n], func=AF.Sigmoid, scale=1.0, bias=C0),
                                            reads=[pB, cst_B], writes=[SGT_B]))
            k = yc["i"] % 2; yc["i"] += 1
            proj_fm(w2, w2B, 0, PTb, PTb_B, a, n,
                    lambda p, pB, n=n, k=k: S.op("dve", lambda e: e.tensor_tensor(out=YO[k][:, 0:n], in0=p[:, 0:n], in1=SGT[:, 0:n], op=ALU.mult),
                                                 reads=[pB, SGT_B], writes=[YO_B[k]]), kc=2)
            S.op("dve", lambda e, k=k, a=a, n=n, dch=dch: e.tensor_tensor(out=YO[k][:, 0:n], in0=YO[k][:, 0:n], in1=R2[:, dch, a:a + n], op=ALU.add),
                 reads=[YO_B[k], R2_B], writes=[YO_B[k]])
            S.dma("sp", yTv[:, dch, a:a + n], YO[k][:, 0:n], reads=[YO_B[k]])
    S.barrier()
    nc._marks = S.marks
    return nc


_CACHE = {}


def kernel(x, p, w_in, w_gla_lr, b_gla_lr, gla_norm_g, b_forget, w_branch_gla, w_branch_fox, w_out,
           ln1_g, ln1_b, w_gate, w_up, conv_w, conv_b, w_down, ln2_g, ln2_b, w_ple_gate, w_ple_proj):
    f = np.float32
    x = np.asarray(x, f); p = np.asarray(p, f)
    B = x.shape[0]

    def pc(v, n):
        return np.ascontiguousarray(np.asarray(v, f).reshape(n, 128).T)

    shared = {
        "ident": np.eye(128, dtype=f),
        "tri": np.ascontiguousarray(np.triu(np.ones((128, 128), f))),
        "reset": np.ascontiguousarray(np.tile((np.arange(NOWN) % 128 != 0).astype(f)[None, :], (128, 1))),
        "w_in": np.ascontiguousarray(np.asarray(w_in, f)[0]),
        "w_gla_lr": np.ascontiguousarray(np.asarray(w_gla_lr, f)[0]),
        "blr": pc(np.asarray(b_gla_lr)[0], 8),
        "gn": pc(np.asarray(gla_norm_g)[0], 4),
        "bfor": np.ascontiguousarray(np.tile(np.asarray(b_forget, f)[0][None, :], (128, 1))),
        "w_branch_gla": np.ascontiguousarray(np.asarray(w_branch_gla, f)[0]),
        "w_branch_fox": np.ascontiguousarray(np.asarray(w_branch_fox, f)[0]),
        "w_out": np.ascontiguousarray(np.asarray(w_out, f)[0]),
        "ln1g": pc(np.asarray(ln1_g)[0], DC), "ln1b": pc(np.asarray(ln1_b)[0], DC),
        "w_gate": np.ascontiguousarray(np.asarray(w_gate, f)[0]),
        "w_up": np.ascontiguousarray(np.asarray(w_up, f)[0]),
        "convw": np.ascontiguousarray(np.asarray(conv_w, f)[0].reshape(3, FC, 128).transpose(2, 1, 0)),
        "convb": pc(np.asarray(conv_b)[0], FC),
        "w_down": np.ascontiguousarray(np.asarray(w_down, f)[0]),
        "ln2g": pc(np.asarray(ln2_g)[0], DC), "ln2b": pc(np.asarray(ln2_b)[0], DC),
        "w_ple_gate": np.ascontiguousarray(np.asarray(w_ple_gate, f)[0]),
        "w_ple_proj": np.ascontiguousarray(np.asarray(w_ple_proj, f)[0]),
    }
    in_maps = []
    for c in range(8):
        b, j = c // 4, c % 4
        g0 = 1024 * j + 1024 - NWIN
        xw = np.zeros((NWIN, D), f)
        lo = max(g0, 0)
        xw[lo - g0:, :] = x[b, lo:1024 * j + 1024, :]
        valid = (np.arange(NWIN) + g0) >= 0
        km = np.where(valid, 0.0, -30000.0).astype(f).reshape(NBLK, 128).T
        m = dict(shared)
        m["xT"] = np.ascontiguousarray(xw.T)
        m["pT"] = np.ascontiguousarray(p[0, b, 1024 * j:1024 * j + 1024, :].T)
        m["kmask"] = np.ascontiguousarray(km)
        m["halo"] = np.full((128, 1), 0.0 if j == 0 else 1.0, f)
        in_maps.append(m)
    if "nc" not in _CACHE:
        _CACHE["nc"] = build_program()
    res = run_bass_kernel_spmd(_CACHE["nc"], in_maps, core_ids=list(range(8)))
    out = np.empty((B, S_LEN, D), f)
    for c in range(8):
        b, j = c // 4, c % 4
        out[b, 1024 * j:1024 * j + 1024, :] = res.results[c]["yT"].T
    return out
```

```python
import numpy as np
import concourse.bass as bass
import concourse.mybir as mybir
from concourse.bass_utils import run_bass_kernel_spmd

F32 = mybir.dt.float32
BF16 = mybir.dt.bfloat16
AF = mybir.ActivationFunctionType
ALU = mybir.AluOpType

D = 2048
S_LEN = 4096
NWIN = 4224
NOWN = 1152
NREAL = 1024
OWN0 = NWIN - NOWN
NBLK = NWIN // 128
DC = 16
DFF = 5632
FC = 44
ALPHA = 2.0 ** 0.25
C_GQ, C_GK, C_GV, C_GR, C_GLR = 0, 1024, 2048, 4096, 6144
C_FQ, C_FK, C_FV, C_FF, C_MA, C_MB = 6160, 7184, 8208, 9232, 9240, 11288


class Buf:
    __slots__ = ("w", "r")

    def __init__(self):
        self.w = None
        self.r = []


class Sync:
    def __init__(self, nc, n_dma_sems=20):
        self.nc = nc
        self.engs = {"pe": nc.tensor, "act": nc.scalar, "dve": nc.vector, "pool": nc.gpsimd, "sp": nc.sync}
        self.semobj = {}
        self.cnt = {}
        for k in self.engs:
            self.semobj[k] = nc.alloc_semaphore("sem_" + k)
            self.cnt[k] = 0
        self.waited = {k: {} for k in self.engs}
        self.npe = 0
        self.marks = []
        self.dq = {}
        for q in ("sp", "pool"):
            keys = []
            for i in range(n_dma_sems):
                key = "d_%s_%d" % (q, i)
                self.semobj[key] = nc.alloc_semaphore(key)
                self.cnt[key] = 0
                keys.append(key)
            self.dq[q] = [keys, 0]

    def _wait(self, eng, deps):
        need = {}
        for tok in deps:
            if tok is None:
                continue
            k, v = tok
            if k == "pe" and eng == "pe":
                continue
            if v > need.get(k, 0):
                need[k] = v
        for k, v in need.items():
            if self.waited[eng].get(k, 0) >= v:
                continue
            self.engs[eng].wait_ge(self.semobj[k], v)
            self.waited[eng][k] = v

    def _deps(self, reads, writes):
        deps = []
        for b in reads:
            deps.append(b.w)
        for b in writes:
            deps.append(b.w)
            deps.extend(b.r)
        return deps

    def op(self, eng, fn, reads=(), writes=(), inc=True):
        self._wait(eng, self._deps(reads, writes))
        inst = fn(self.engs[eng])
        if eng == "pe":
            self.npe += 1
        tok = (eng, self.cnt[eng] + 1)
        if inc:
            inst.then_inc(self.semobj[eng], 1)
            self.cnt[eng] += 1
        for b in reads:
            b.r.append(tok)
        for b in writes:
            b.w = tok
            b.r = []
        return inst

    def dma(self, q, out, in_, reads=(), writes=()):
        keys, idx = self.dq[q]
        key = keys[idx % len(keys)]
        self.dq[q][1] = idx + 1
        deps = self._deps(reads, writes)
        deps.append((key, self.cnt[key]) if self.cnt[key] else None)
        self._wait(q, deps)
        inst = self.engs[q].dma_start(out=out, in_=in_)
        inst.then_inc(self.semobj[key], 16)
        self.cnt[key] += 16
        tok = (key, self.cnt[key])
        for b in reads:
            b.r.append(tok)
        for b in writes:
            b.w = tok
            b.r = []
        return inst

    def barrier(self):
        self.marks.append(self.npe)
        toks = [(k, v) for k, v in self.cnt.items() if v > 0]
        for e in self.engs:
            self._wait(e, [t for t in toks if not (t[0] == e)])


def build_program():
    import os
    KSTOP = int(os.environ.get("KSTOP", "99"))
    nc = bass.Bass("TRN2", target_bir_lowering=False)

    def din(name, shape, dt=F32):
        return nc.dram_tensor(name, list(shape), dt, kind="ExternalInput").ap()

    xT = din("xT", [D, NWIN])
    pT = din("pT", [256, NREAL])
    kmask_d = din("kmask", [128, NBLK])
    halo_d = din("halo", [128, 1])
    ident_d = din("ident", [128, 128])
    tri_d = din("tri", [128, 128])
    reset_d = din("reset", [128, NOWN])
    w_in = din("w_in", [D, 13336])
    w_lr = din("w_gla_lr", [16, 1024])
    blr_d = din("blr", [128, 8])
    gn_d = din("gn", [128, 4])
    bf_d = din("bfor", [128, 8])
    w_bg = din("w_branch_gla", [D, D])
    w_bf = din("w_branch_fox", [1024, D])
    w_out = din("w_out", [D, D])
    ln1g_d = din("ln1g", [128, DC])
    ln1b_d = din("ln1b", [128, DC])
    w_gate = din("w_gate", [D, DFF])
    w_up = din("w_up", [D, DFF])
    convw_d = din("convw", [128, FC, 3])
    convb_d = din("convb", [128, FC])
    w_down = din("w_down", [DFF, D])
    ln2g_d = din("ln2g", [128, DC])
    ln2b_d = din("ln2b", [128, DC])
    w_pg = din("w_ple_gate", [D, D])
    w_pp = din("w_ple_proj", [256, D])
    yT = nc.dram_tensor("yT", [D, NREAL], F32, kind="ExternalOutput").ap()
    x1s = nc.dram_tensor("x1s", [D, NREAL], F32, kind="Internal").ap()

    S = Sync(nc)
    NA_BYTES = 212736
    arena = nc.alloc_sbuf_tensor("arena", [128, NA_BYTES // 4], F32)
    ar = {"off": 0}

    def at(off):
        ar["off"] = off

    def sb(name, shape, dt):
        n = 1
        for s_ in shape[1:]:
            n *= s_
        nbytes = n * (2 if dt == BF16 else 4)
        off = (ar["off"] + 31) // 32 * 32
        assert off + nbytes <= NA_BYTES, (name, off, nbytes)
        ar["off"] = off + nbytes
        v = arena[0:shape[0], off // 4:(off + nbytes) // 4]
        if dt == BF16:
            v = v.bitcast(BF16)
        if len(shape) == 3:
            v = v.rearrange("p (a b) -> p a b", a=shape[1])
        elif len(shape) == 4:
            v = v.rearrange("p (a b c) -> p a b c", a=shape[1], b=shape[2])
        return v
    NPS = 7
    ps_t = [nc.alloc_psum_tensor("ps%d" % i, [128, 512], F32) for i in range(NPS)]
    ps_b = [Buf() for _ in range(NPS)]
    pst = nc.alloc_psum_tensor("pst", [128, 1024], BF16)
    pst_b = [Buf() for _ in range(8)]
    st = {"ps": 0, "psa": 0, "pst": 0, "ev": 0}

    def PS():
        i = st["ps"] % 5
        st["ps"] += 1
        return ps_t[i], ps_b[i]

    def PSA():
        i = 5 + st["psa"] % 2
        st["psa"] += 1
        return ps_t[i], ps_b[i]

    def PST():
        i = st["ps"] % 5
        st["ps"] += 1
        return ps_t[i][:, 0:64].bitcast(BF16), ps_b[i]

    def evac(out, in_, reads, writes):
        st["ev"] += 1
        if st["ev"] % 2:
            S.op("act", lambda e: e.copy(out=out, in_=in_), reads=reads, writes=writes)
        else:
            S.op("dve", lambda e: e.tensor_copy(out=out, in_=in_), reads=reads, writes=writes)

    ident = sb("ident", [128, 128], BF16); ident_B = Buf()
    tri = sb("tri", [128, 128], F32); tri_B = Buf()
    trib = sb("trib", [128, 128], BF16); trib_B = Buf()
    ones = sb("ones", [128, 128], F32); ones_B = Buf()
    reset = sb("reset", [128, NOWN], F32); reset_B = Buf()
    kmask = sb("kmask", [128, NBLK], F32); kmask_B = Buf()
    halo = sb("halo", [128, 1], F32); halo_B = Buf()
    cst = sb("cst", [128, 4], F32); cst_B = Buf()
    nblr = sb("nblr", [128, 8], F32); nblr_B = Buf()
    gn = sb("gn", [128, 4], F32); gn_B = Buf()
    bfor = sb("bfor", [128, 8], F32); bfor_B = Buf()
    ln1g = sb("ln1g", [128, DC], F32); ln1b = sb("ln1b", [128, DC], F32)
    ln2g = sb("ln2g", [128, DC], F32); ln2b = sb("ln2b", [128, DC], F32)
    convw = sb("convw", [128, FC, 3], F32); convb = sb("convb", [128, FC], F32)
    prm_B = Buf()
    wlr = sb("wlr", [32, 1024], BF16); wlr_B = Buf()
    ones33 = sb("ones33", [128, NBLK], F32)

    S.dma("pool", ident[:], ident_d, writes=[ident_B])
    S.dma("pool", trib[:], tri_d, writes=[trib_B])
    S.op("dve", lambda e: e.memset(wlr[:], 0.0), writes=[wlr_B])
    S.dma("pool", wlr[0:16, :], w_lr, writes=[wlr_B])
    S.dma("sp", tri[:], tri_d, writes=[tri_B])
    S.dma("sp", reset[:], reset_d, writes=[reset_B])
    S.dma("sp", kmask[:], kmask_d, writes=[kmask_B])
    S.dma("sp", halo[:], halo_d, writes=[halo_B])
    S.dma("sp", nblr[:], blr_d, writes=[nblr_B])
    S.dma("sp", gn[:], gn_d, writes=[gn_B])
    S.dma("sp", bfor[:], bf_d, writes=[bfor_B])
    for t_, d_ in ((ln1g, ln1g_d), (ln1b, ln1b_d), (ln2g, ln2g_d), (ln2b, ln2b_d), (convw, convw_d), (convb, convb_d)):
        S.dma("sp", t_[:], d_, writes=[prm_B])
    S.op("dve", lambda e: e.memset(ones[:], 1.0), writes=[ones_B])
    S.op("dve", lambda e: e.memset(ones33[:], 1.0), writes=[ones_B])
    S.op("dve", lambda e: e.memset(cst[:, 0:1], 0.0), writes=[cst_B])
    S.op("dve", lambda e: e.memset(cst[:, 1:2], 1.0), writes=[cst_B])
    S.op("dve", lambda e: e.memset(cst[:, 2:3], 1e-5), writes=[cst_B])
    S.op("dve", lambda e: e.memset(cst[:, 3:4], 1e-6), writes=[cst_B])
    S.op("dve", lambda e: e.tensor_scalar(out=nblr[:], in0=nblr[:], scalar1=-1.0, scalar2=None, op0=ALU.mult),
         reads=[nblr_B], writes=[nblr_B])
    S.barrier()
    C0, C1, CEPS_LN, CEPS_RMS = cst[:, 0:1], cst[:, 1:2], cst[:, 2:3], cst[:, 3:4]
    if KSTOP == 1:
        return nc


    CONST_END = ar["off"]
    assert CONST_END <= 12288, CONST_END
    OFF_XT = 12288
    OFF_OFT = OFF_XT + 36864
    OFF_OGT = OFF_OFT + 18432
    OFF_LOCAL = OFF_OGT + 36864
    at(OFF_XT)
    XT = sb("XT", [128, DC, NOWN], BF16)
    XT_B = [Buf() for _ in range(4)]
    OFT = sb("OFT", [128, 8, NOWN], BF16); OFT_B = Buf()
    OGT = sb("OGT", [128, 16, NOWN], BF16); OGT_B = Buf()
    wp = {"bufs": [], "B": [], "i": 0}

    def wp_setup(nbuf, nbytes):
        wp["bufs"] = [sb("WP%d" % i, [128, nbytes // 2], BF16) for i in range(nbuf)]
        wp["B"] = [Buf() for _ in range(nbuf)]
        wp["i"] = 0

    def wpanel(wsrc, c0, ncols, kc=DC, kc0=0):
        i = wp["i"] % len(wp["bufs"])
        wp["i"] += 1
        src = wsrc.rearrange("(kc p) n -> p kc n", p=128)[:, kc0:kc0 + kc, c0:c0 + ncols]
        v = wp["bufs"][i][:, 0:kc * ncols].rearrange("p (k n) -> p k n", k=kc)
        S.dma("pool", v, src, writes=[wp["B"][i]])
        return v, wp["B"][i]

    def load_xt(t0, ntok):
        for g in range(4):
            src = xT.rearrange("(dc p) t -> p dc t", p=128)[:, 4 * g:4 * g + 4, t0:t0 + ntok]
            S.dma("pool", XT[:, 4 * g:4 * g + 4, 0:ntok], src, writes=[XT_B[g]])

    def ttiles(ntok):
        n = 512 if ntok % 512 == 0 else 384
        return [(a, n) for a in range(0, ntok, n)]

    def proj_fm(w, wB, wcol, act, actB, t0, n, cb, kc=DC):
        p, pB = PS()
        for dc in range(kc):
            rb = actB[dc * len(actB) // kc] if isinstance(actB, list) else actB
            S.op("pe", lambda e, dc=dc: e.matmul(p[:, 0:n], lhsT=w[:, dc, wcol:wcol + 128], rhs=act[:, dc, t0:t0 + n],
                                                  start=(dc == 0), stop=(dc == kc - 1)),
                 reads=[wB, rb], writes=[pB], inc=(dc == kc - 1))
        cb(p, pB)

    def proj_tm(w, wB, ncols, act, actB, tb, cb, kc=DC):
        p, pB = PS()
        for dc in range(kc):
            rb = actB[dc * len(actB) // kc] if isinstance(actB, list) else actB
            S.op("pe", lambda e, dc=dc: e.matmul(p[:, 0:ncols], lhsT=act[:, dc, tb * 128:(tb + 1) * 128], rhs=w[:, dc, 0:ncols],
                                                  start=(dc == 0), stop=(dc == kc - 1)),
                 reads=[wB, rb], writes=[pB], inc=(dc == kc - 1))
        cb(p, pB)

    SUPER = [(0, 1024), (1024, 1024), (2048, 1024), (OWN0, NOWN)]
    w_in_v = w_in.rearrange("(kc p) n -> p kc n", p=128)

    at(OFF_OGT)
    KT = sb("KT", [128, 4, NWIN], BF16); KT_B = Buf()
    VA = sb("VA", [128, NBLK, 4, 130], BF16); VA_B = Buf()
    QT = sb("QT", [128, 4, NOWN], BF16); QT_B = Buf()
    LF = sb("LF", [128, NBLK, 8], F32); LF_B = Buf()
    CK = sb("CK", [128, NBLK, 8], F32); CK_B = Buf()
    TOT = sb("TOT", [128, NBLK, 8], F32); TOT_B = Buf()
    INC = sb("INC", [128, NBLK, 8], F32); INC_B = Buf()
    WFF = sb("WFF", [128, DC, 8], BF16); WFF_B = Buf()
    BI = [sb("BI%d" % i, [128, NBLK], F32) for i in range(2)]; BI_B = [Buf(), Buf()]
    PT4 = [sb("PT4%d" % i, [128, 512], BF16) for i in range(4)]; PT4_B = [Buf() for _ in range(4)]
    TMP = [sb("TMP%d" % i, [128, 512], F32) for i in range(2)]; TMP_B = [Buf(), Buf()]
    ON = [sb("ON%d" % i, [128, 128], BF16) for i in range(2)]; ON_B = [Buf(), Buf()]
    sm = sb("sm", [128, 8], F32); sm_B = Buf()
    t8 = sb("t8", [128, 8], F32); t8_B = Buf()
    wp_setup(3, 16384)
    S.dma("pool", WFF, w_in_v[:, :, C_FF:C_FF + 8], writes=[WFF_B])
    S.op("dve", lambda e: e.memset(VA[:, :, :, 128:130], 1.0), writes=[VA_B])
    SCL = 128.0 ** -0.5
    cnt = {"pt": 0, "on": 0, "bi": 0, "tm": 0, "cur_bi": 0, "po": None}

    XTF = [XT.rearrange("p a b -> p (a b)")[:, i * DC * 512:(i + 1) * DC * 512].rearrange("p (a b) -> p a b", a=DC) for i in range(2)]
    XTF_B = [[Buf() for _ in range(4)] for _ in range(2)]
    FT = [(512 * k_, 512) for k_ in range(8)] + [(4096, 128)]
    xT_v = xT.rearrange("(dc p) t -> p dc t", p=128)
    for hp in range(2):
        wK, wKB = wpanel(w_in, C_FK + hp * 512, 512)
        wV, wVB = wpanel(w_in, C_FV + hp * 512, 512)
        wQ, wQB = wpanel(w_in, C_FQ + hp * 512, 512)
        for ti, (t0, n) in enumerate(FT):
            b = ti % 2
            X_, XB_ = XTF[b], XTF_B[b]
            for g in range(4):
                S.dma("pool", X_[:, 4 * g:4 * g + 4, 0:n], xT_v[:, 4 * g:4 * g + 4, t0:t0 + n], writes=[XB_[g]])
            for h in range(4):
                proj_fm(wK, wKB, h * 128, X_, XB_, 0, n,
                        lambda p, pB, h=h, n=n, t0=t0: evac(KT[:, h, t0:t0 + n], p[:, 0:n], [pB], [KT_B]))
            for tb in range(n // 128):
                blk = t0 // 128 + tb
                proj_tm(wV, wVB, 512, X_, XB_, tb,
                        lambda p, pB, blk=blk: evac(VA[:, blk, :, 0:128], p[:, 0:512].rearrange("p (h d) -> p h d", h=4), [pB], [VA_B]))
            if hp == 0:
                for tb in range(n // 128):
                    blk = t0 // 128 + tb

                    def ffcb(p, pB, blk=blk):
                        S.op("dve", lambda e: e.tensor_tensor(out=t8, in0=p[:, 0:8], in1=bfor, op=ALU.add),
                             reads=[pB, bfor_B], writes=[t8_B])
                        S.op("act", lambda e: e.activation(out=t8, in_=t8, func=AF.Exp, scale=-1.0, bias=C0),
                             reads=[t8_B, cst_B], writes=[t8_B])
                        S.op("act", lambda e: e.activation(out=LF[:, blk, :], in_=t8, func=AF.Ln, scale=1.0, bias=C1),
                             reads=[t8_B, cst_B], writes=[LF_B])
                    proj_tm(WFF, WFF_B, 8, X_, XB_, tb, ffcb)
            if t0 >= OWN0:
                for h in range(4):
                    proj_fm(wQ, wQB, h * 128, X_, XB_, 0, n,
                            lambda p, pB, h=h, n=n, t0=t0: evac(QT[:, h, t0 - OWN0:t0 - OWN0 + n], p[:, 0:n], [pB], [QT_B]))
        if hp == 0:
            LFf = LF.rearrange("p b h -> p (b h)")
            p1, p1B = PS()
            S.op("pe", lambda e: e.matmul(p1[:, 0:NBLK * 8], lhsT=tri, rhs=LFf, start=True, stop=True),
                 reads=[tri_B, LF_B], writes=[p1B])
            p2, p2B = PS()
            S.op("pe", lambda e: e.matmul(p2[:, 0:NBLK * 8], lhsT=ones, rhs=LFf, start=True, stop=True),
                 reads=[ones_B, LF_B], writes=[p2B])
            S.op("dve", lambda e: e.tensor_copy(out=TOT.rearrange("p b h -> p (b h)"), in_=p2[:, 0:NBLK * 8]),
                 reads=[p2B], writes=[TOT_B])
            for h in range(8):
                S.op("dve", lambda e, h=h: e.tensor_tensor_scan(out=INC[:, :, h], data0=ones33, data1=TOT[:, :, h],
                                                                 initial=0.0, op0=ALU.mult, op1=ALU.add),
                     reads=[TOT_B, ones_B], writes=[INC_B])
            S.op("dve", lambda e: e.tensor_tensor(out=CK, in0=INC, in1=TOT, op=ALU.subtract),
                 reads=[INC_B, TOT_B], writes=[CK_B])
            S.op("dve", lambda e: e.tensor_tensor(out=CK.rearrange("p b h -> p (b h)"), in0=CK.rearrange("p b h -> p (b h)"),
                                                  in1=p1[:, 0:NBLK * 8], op=ALU.add),
                 reads=[CK_B, p1B], writes=[CK_B])
            S.op("dve", lambda e: e.tensor_tensor(out=CK, in0=CK, in1=kmask.unsqueeze(2).broadcast_to([128, NBLK, 8]), op=ALU.add),
                 reads=[CK_B, kmask_B], writes=[CK_B])
        if KSTOP == 3:
            S.barrier()
            return nc
        items = []
        for h in range(4):
            for i in range(NOWN // 128):
                qb = OWN0 // 128 + i
                js = list(range(0, qb + 1, 4))
                for gi, j0 in enumerate(js):
                    items.append({"h": h, "i": i, "qb": qb, "j0": j0, "nj": min(4, qb + 1 - j0),
                                  "first": gi == 0, "last": gi == len(js) - 1})

        def stageA(it):
            p, pB = PS()
            it["p"], it["pB"] = p, pB
            h, i, j0, nj = it["h"], it["i"], it["j0"], it["nj"]
            for jj in range(nj):
                S.op("pe", lambda e, jj=jj: e.matmul(p[:, jj * 128:(jj + 1) * 128], lhsT=KT[:, h, (j0 + jj) * 128:(j0 + jj + 1) * 128],
                                                     rhs=QT[:, h, i * 128:(i + 1) * 128], start=True, stop=True),
                     reads=[KT_B, QT_B], writes=[pB], inc=(jj == nj - 1))

        def stageB(it):
            h, i, j0, nj, qb = it["h"], it["i"], it["j0"], it["nj"], it["qb"]
            hg = hp * 4 + h
            if it["first"]:
                bi = cnt["bi"] % 2; cnt["bi"] += 1
                cnt["cur_bi"] = bi
                S.op("dve", lambda e: e.tensor_scalar(out=BI[bi], in0=CK[:, :, hg], scalar1=INC[:, qb, hg:hg + 1],
                                                      scalar2=None, op0=ALU.subtract),
                     reads=[CK_B, INC_B], writes=[BI_B[bi]])
            bi = cnt["cur_bi"]
            p, pB = it["p"], it["pB"]
            t = cnt["tm"] % 2; cnt["tm"] += 1
            k = cnt["pt"] % 4; cnt["pt"] += 1
            it["k"] = k
            w_ = nj * 128
            S.op("dve", lambda e: e.scalar_tensor_tensor(out=TMP[t][:, 0:w_].rearrange("p (j t) -> p j t", j=nj),
                                                         in0=p[:, 0:w_].rearrange("p (j t) -> p j t", j=nj), scalar=SCL,
                                                         in1=BI[bi][:, j0:j0 + nj].unsqueeze(2).broadcast_to([128, nj, 128]),
                                                         op0=ALU.mult, op1=ALU.add),
                 reads=[pB, BI_B[bi]], writes=[TMP_B[t]])
            S.op("act", lambda e: e.activation(out=PT4[k][:, 0:w_], in_=TMP[t][:, 0:w_], func=AF.Exp, scale=1.0, bias=C0),
                 reads=[TMP_B[t], cst_B], writes=[PT4_B[k]])
            if it["last"]:
                S.op("dve", lambda e: e.tensor_tensor(out=PT4[k][:, w_ - 128:w_], in0=PT4[k][:, w_ - 128:w_], in1=trib, op=ALU.mult),
                     reads=[PT4_B[k], trib_B], writes=[PT4_B[k]])

        def stageC(it):
            h, i, j0, nj = it["h"], it["i"], it["j0"], it["nj"]
            hg = hp * 4 + h
            k = it["k"]
            if it["first"]:
                cnt["po"] = PSA()
            po, poB = cnt["po"]
            for jj in range(nj):
                S.op("pe", lambda e, jj=jj: e.matmul(po[:, 0:130], lhsT=PT4[k][:, jj * 128:(jj + 1) * 128], rhs=VA[:, j0 + jj, h, :],
                                                     start=(it["first"] and jj == 0), stop=(it["last"] and jj == nj - 1)),
                     reads=[PT4_B[k], VA_B], writes=[poB], inc=(jj == nj - 1))
            if it["last"]:
                S.op("dve", lambda e: e.tensor_scalar(out=sm[:, 0:1], in0=po[:, 128:129], scalar1=1e-30, scalar2=None, op0=ALU.max),
                     reads=[poB], writes=[sm_B])
                S.op("dve", lambda e: e.reciprocal(out=sm[:, 1:2], in_=sm[:, 0:1]), reads=[sm_B], writes=[sm_B])
                o = cnt["on"] % 2; cnt["on"] += 1
                S.op("dve", lambda e: e.tensor_scalar(out=ON[o], in0=po[:, 0:128], scalar1=sm[:, 1:2], scalar2=None, op0=ALU.mult),
                     reads=[poB, sm_B], writes=[ON_B[o]])
                pt_, ptB = PST()
                S.op("pe", lambda e: e.transpose(pt_, ON[o], ident), reads=[ON_B[o], ident_B], writes=[ptB])
                evac(OFT[:, hg, i * 128:(i + 1) * 128], pt_, [ptB], [OFT_B])

        n_it = len(items)
        for kk in range(n_it + 3):
            if kk < n_it:
                stageA(items[kk])
            if 0 <= kk - 1 < n_it:
                stageB(items[kk - 1])
            if 0 <= kk - 3 < n_it:
                stageC(items[kk - 3])
    S.barrier()

    if KSTOP == 4:
        return nc
    at(OFF_LOCAL)
    wp_setup(2, 8192)
    Sst = [sb("Sst%d" % h, [128, 2, 512], F32) for h in range(4)]; Sst_B = [[Buf(), Buf()] for _ in range(4)]
    Sbf = sb("Sbf", [128, 2, 512], BF16); Sbf_B = [Buf(), Buf()]
    GLR = sb("GLR", [32, NOWN], BF16); GLR_B = Buf()
    S.op("dve", lambda e: e.memset(GLR, 0.0), writes=[GLR_B])
    WG16 = sb("WG16", [128, DC, 16], BF16); WG16_B = Buf()
    S.dma("pool", WG16, w_in_v[:, :, C_GLR:C_GLR + 16], writes=[WG16_B])
    for h in range(4):
        S.op("dve", lambda e, h=h: e.memset(Sst[h], 0.0), writes=Sst_B[h])
    Lb = sb("Lb", [128, NOWN], F32); Lb_B = Buf()
    Bb = sb("Bb", [128, NOWN], F32); Bb_B = Buf()
    Db = sb("Db", [128, NOWN], F32); Db_B = Buf()
    EN = sb("EN", [128, NOWN], F32); EN_B = Buf()
    EP = sb("EP", [128, NOWN], F32); EP_B = Buf()
    EBL = sb("EBL", [128, 2, 16], F32); EBL_B = Buf()
    KHT = sb("KHT", [128, 2, NOWN], BF16); KHT_B = Buf()
    KNT = sb("KNT", [128, 2, NOWN], BF16); KNT_B = Buf()
    QPT = sb("QPT", [128, 2, NOWN], BF16); QPT_B = Buf()
    KHt = sb("KHt", [128, 9, 256], BF16); KHt_B = Buf()
    Vh = sb("Vh", [128, 9, 512], BF16); Vh_B = Buf()
    AT = sb("AT", [128, 128], BF16); AT_B = Buf()
    SG = sb("SG", [128, 4, NOWN], BF16); SG_B = Buf()
    SQ4 = sb("SQ4", [128, 512], BF16); SQ4_B = Buf()
    TT4 = sb("TT4", [128, 512], F32); TT4_B = Buf()
    onesb = sb("onesb", [128, 128], BF16); onesb_B = Buf()
    S.op("dve", lambda e: e.memset(onesb, 1.0), writes=[onesb_B])
    RR2 = [sb("RR%d" % i, [128, 128], F32) for i in range(2)]; RR2_B = [Buf(), Buf()]
    ATA = sb("ATA", [128, 9, 128], BF16); ATA_B = [Buf() for _ in range(9)]

    for (t0, ntok) in SUPER:
        load_xt(t0, ntok)
        own = (t0 == OWN0)
        nb = ntok // 128
        for (a, n) in ttiles(ntok):
            p, pB = PS()
            for dc in range(DC):
                S.op("pe", lambda e, dc=dc, p=p, a=a, n=n: e.matmul(p[0:16, 0:n], lhsT=WG16[:, dc, :], rhs=XT[:, dc, a:a + n],
                                                                  start=(dc == 0), stop=(dc == DC - 1)),
                     reads=[WG16_B, XT_B[dc // 4]], writes=[pB], inc=(dc == DC - 1))
            evac(GLR[0:16, a:a + n], p[0:16, 0:n], [pB], [GLR_B])
        if KSTOP == 51:
            S.barrier()
            return nc
        if KSTOP == 55 and own:
            S.barrier()
            return nc
        for h in range(4):
            wk, wkB = wpanel(w_in, C_GK + h * 256, 256)
            if own:
                wq, wqB = wpanel(w_in, C_GQ + h * 256, 256)
            for kc in range(2):
                col = h * 256 + kc * 128
                for (a, n) in ttiles(ntok):
                    p, pB = PS()
                    S.op("pe", lambda e, p=p, a=a, n=n, col=col: e.matmul(p[:, 0:n], lhsT=wlr[:, col:col + 128], rhs=GLR[:, a:a + n],
                                                                        start=True, stop=True),
                         reads=[wlr_B, GLR_B], writes=[pB])
                    S.op("act", lambda e, p=p, a=a, n=n, h=h, kc=kc: e.activation(out=Lb[:, a:a + n], in_=p[:, 0:n], func=AF.Exp, scale=-1.0,
                                                                                bias=nblr[:, h * 2 + kc:h * 2 + kc + 1]),
                         reads=[pB, nblr_B], writes=[Lb_B])
                if KSTOP == 511:
                    S.barrier()
                    return nc
                S.op("act", lambda e: e.activation(out=Lb[:, 0:ntok], in_=Lb[:, 0:ntok], func=AF.Ln, scale=1.0, bias=C1),
                     reads=[Lb_B, cst_B], writes=[Lb_B])
                if KSTOP == 512:
                    S.barrier()
                    return nc
                S.op("dve", lambda e: e.tensor_tensor_scan(out=Bb[:, 0:ntok], data0=reset[:, 0:ntok], data1=Lb[:, 0:ntok],
                                                           initial=0.0, op0=ALU.mult, op1=ALU.add),
                     reads=[reset_B, Lb_B], writes=[Bb_B])
                if KSTOP == 513:
                    S.barrier()
                    return nc
                B3 = Bb[:, 0:ntok].rearrange("p (b t) -> p b t", t=128)
                S.op("dve", lambda e: e.tensor_tensor(out=Db[:, 0:ntok].rearrange("p (b t) -> p b t", t=128), in0=B3,
                                                      in1=B3[:, :, 127:128].broadcast_to([128, nb, 128]), op=ALU.subtract),
                     reads=[Bb_B], writes=[Db_B])
                S.op("act", lambda e: e.activation(out=Db[:, 0:ntok], in_=Db[:, 0:ntok], func=AF.Exp, scale=1.0 / 16, bias=C0),
                     reads=[Db_B, cst_B], writes=[Db_B])
                S.op("act", lambda e, kc=kc: e.activation(out=EBL[:, kc, 0:nb], in_=B3[:, :, 127], func=AF.Exp, scale=-1.0 / 16, bias=C0),
                     reads=[Bb_B, cst_B], writes=[EBL_B])
                if KSTOP == 514:
                    S.barrier()
                    return nc
                if own:
                    S.op("act", lambda e: e.activation(out=EN[:, 0:ntok], in_=Bb[:, 0:ntok], func=AF.Exp, scale=1.0 / 16, bias=C0),
                         reads=[Bb_B, cst_B], writes=[EN_B])
                    S.op("act", lambda e: e.activation(out=EP[:, 0:ntok], in_=Bb[:, 0:ntok], func=AF.Exp, scale=-1.0 / 16, bias=C0),
                         reads=[Bb_B, cst_B], writes=[EP_B])
                for (a, n) in ttiles(ntok):
                    def kcb(p, pB, a=a, n=n, kc=kc):
                        S.op("dve", lambda e: e.tensor_tensor(out=KHT[:, kc, a:a + n], in0=p[:, 0:n], in1=Db[:, a:a + n], op=ALU.mult),
                             reads=[pB, Db_B], writes=[KHT_B])
                        if own:
                            S.op("dve", lambda e: e.tensor_tensor(out=KNT[:, kc, a:a + n], in0=p[:, 0:n], in1=EN[:, a:a + n], op=ALU.mult),
                                 reads=[pB, EN_B], writes=[KNT_B])
                    proj_fm(wk, wkB, kc * 128, XT, XT_B, a, n, kcb)
                    if own:
                        def qcb(p, pB, a=a, n=n, kc=kc):
                            S.op("dve", lambda e: e.scalar_tensor_tensor(out=QPT[:, kc, a:a + n], in0=p[:, 0:n], scalar=256.0 ** -0.5,
                                                                         in1=EP[:, a:a + n], op0=ALU.mult, op1=ALU.mult),
                                 reads=[pB, EP_B], writes=[QPT_B])
                        proj_fm(wq, wqB, kc * 128, XT, XT_B, a, n, qcb)
                if KSTOP == 515:
                    S.barrier()
                    return nc
                for tb in range(nb):
                    pt_, ptB = PST()
                    S.op("pe", lambda e, pt_=pt_, tb=tb, kc=kc: e.transpose(pt_, KHT[:, kc, tb * 128:(tb + 1) * 128], ident),
                         reads=[KHT_B, ident_B], writes=[ptB])
                    evac(KHt[:, tb, kc * 128:(kc + 1) * 128], pt_, [ptB], [KHt_B])
            if KSTOP == 52:
                S.barrier()
                return nc
            for half in range(2):
                wv, wvB = wpanel(w_in, C_GV + h * 512 + half * 256, 256)
                for tb in range(nb):
                    proj_tm(wv, wvB, 256, XT, XT_B, tb,
                            lambda p, pB, tb=tb, half=half: evac(Vh[:, tb, half * 256:(half + 1) * 256], p[:, 0:256], [pB], [Vh_B]))
            if own:
                for half in range(2):
                    wr, wrB = wpanel(w_in, C_GR + h * 512 + half * 256, 256)
                    for v2 in range(2):
                        vc = half * 2 + v2
                        for (a, n) in ttiles(ntok):
                            proj_fm(wr, wrB, v2 * 128, XT, XT_B, a, n,
                                    lambda p, pB, vc=vc, a=a, n=n: (
                                        S.op("act", lambda e: e.activation(out=SG[:, vc, a:a + n], in_=p[:, 0:n], func=AF.Silu, scale=1.0, bias=C0),
                                             reads=[pB, cst_B], writes=[SG_B]),
                                        S.op("dve", lambda e: e.tensor_scalar(out=SG[:, vc, a:a + n], in0=SG[:, vc, a:a + n], scalar1=gn[:, vc:vc + 1],
                                                                              scalar2=None, op0=ALU.mult),
                                             reads=[gn_B], writes=[SG_B])))
                S.op("act", lambda e, h=h: e.copy(out=Sbf, in_=Sst[h]), reads=Sst_B[h], writes=Sbf_B)
            if KSTOP == 53:
                S.barrier()
                return nc
            if not own:
                for tb in range(nb):
                    for kc in range(2):
                        pu, puB = PS()
                        S.op("pe", lambda e, kc=kc, tb=tb, pu=pu: e.matmul(pu[:, 0:512], lhsT=KHt[:, tb, kc * 128:(kc + 1) * 128], rhs=Vh[:, tb, :],
                                                                         start=True, stop=True),
                             reads=[KHt_B, Vh_B], writes=[puB])
                        S.op("dve", lambda e, kc=kc, tb=tb, pu=pu, h=h: e.scalar_tensor_tensor(out=Sst[h][:, kc, :], in0=Sst[h][:, kc, :],
                                                                                             scalar=EBL[:, kc, tb:tb + 1], in1=pu[:, 0:512],
                                                                                             op0=ALU.mult, op1=ALU.add),
                             reads=[EBL_B, puB], writes=[Sst_B[h][kc]])
            else:
                for tb in range(nb):
                    pa, paB = PS()
                    for kc in range(2):
                        S.op("pe", lambda e, kc=kc, tb=tb, pa=pa: e.matmul(pa[:, 0:128], lhsT=KNT[:, kc, tb * 128:(tb + 1) * 128],
                                                                         rhs=QPT[:, kc, tb * 128:(tb + 1) * 128], start=(kc == 0), stop=(kc == 1)),
                             reads=[KNT_B, QPT_B], writes=[paB], inc=(kc == 1))
                    S.op("dve", lambda e, pa=pa, tb=tb: e.tensor_tensor(out=ATA[:, tb, :], in0=pa[:, 0:128], in1=tri, op=ALU.mult),
                         reads=[paB, tri_B], writes=[ATA_B[tb]])

                def part_b(tb, po, poB):
                    r = tb % 2
                    S.op("dve", lambda e: e.reciprocal(out=RR2[r], in_=RR2[r]), reads=[RR2_B[r]], writes=[RR2_B[r]])
                    S.op("dve", lambda e: e.tensor_tensor(out=TT4.rearrange("p (v t) -> p v t", v=4),
                                                          in0=po[:, 0:512].rearrange("p (v t) -> p v t", v=4),
                                                          in1=RR2[r].unsqueeze(1).broadcast_to([128, 4, 128]), op=ALU.mult),
                         reads=[poB, RR2_B[r]], writes=[TT4_B])
                    S.op("dve", lambda e: e.tensor_tensor(out=OGT[:, h * 4:(h + 1) * 4, tb * 128:(tb + 1) * 128],
                                                          in0=TT4.rearrange("p (v t) -> p v t", v=4),
                                                          in1=SG[:, :, tb * 128:(tb + 1) * 128], op=ALU.mult),
                         reads=[TT4_B, SG_B], writes=[OGT_B])

                pend = None
                for tb in range(nb):
                    last = (tb == nb - 1)
                    if not last:
                        for kc in range(2):
                            pu, puB = PS()
                            S.op("pe", lambda e, kc=kc, tb=tb, pu=pu: e.matmul(pu[:, 0:512], lhsT=KHt[:, tb, kc * 128:(kc + 1) * 128], rhs=Vh[:, tb, :],
                                                                             start=True, stop=True),
                                 reads=[KHt_B, Vh_B], writes=[puB])
                            S.op("dve", lambda e, kc=kc, tb=tb, pu=pu: e.scalar_tensor_tensor(out=Sst[h][:, kc, :], in0=Sst[h][:, kc, :],
                                                                                            scalar=EBL[:, kc, tb:tb + 1], in1=pu[:, 0:512],
                                                                                            op0=ALU.mult, op1=ALU.add),
                                 reads=[EBL_B, puB], writes=[Sst_B[h][kc]])
                    if pend is not None:
                        part_b(*pend)
                    po, poB = PSA()
                    for vc in range(4):
                        for kc in range(2):
                            S.op("pe", lambda e, kc=kc, vc=vc, tb=tb, po=po: e.matmul(po[:, vc * 128:(vc + 1) * 128],
                                                                                    lhsT=Sbf[:, kc, vc * 128:(vc + 1) * 128],
                                                                                    rhs=QPT[:, kc, tb * 128:(tb + 1) * 128],
                                                                                    start=(kc == 0), stop=False),
                                 reads=[Sbf_B[kc], QPT_B], writes=[poB], inc=False)
                        S.op("pe", lambda e, vc=vc, tb=tb, po=po: e.matmul(po[:, vc * 128:(vc + 1) * 128], lhsT=Vh[:, tb, vc * 128:(vc + 1) * 128],
                                                                         rhs=ATA[:, tb, :], start=False, stop=True),
                             reads=[Vh_B, ATA_B[tb]], writes=[poB], inc=(vc == 3))
                    if not last:
                        for kc in range(2):
                            S.op("act", lambda e, kc=kc: e.copy(out=Sbf[:, kc, :], in_=Sst[h][:, kc, :]),
                                 reads=[Sst_B[h][kc]], writes=[Sbf_B[kc]])
                    r = tb % 2
                    pr, prB = PS()
                    S.op("act", lambda e, po=po: e.activation(out=SQ4, in_=po[:, 0:512], func=AF.Square, scale=1.0, bias=C0),
                         reads=[poB, cst_B], writes=[SQ4_B])
                    for vc in range(4):
                        S.op("pe", lambda e, vc=vc, pr=pr: e.matmul(pr[:, 0:128], lhsT=onesb, rhs=SQ4[:, vc * 128:(vc + 1) * 128],
                                                                    start=(vc == 0), stop=(vc == 3)),
                             reads=[onesb_B, SQ4_B], writes=[prB], inc=(vc == 3))
                    S.op("act", lambda e, pr=pr, r=r: e.activation(out=RR2[r], in_=pr[:, 0:128], func=AF.Sqrt, scale=1.0 / 512, bias=CEPS_RMS),
                         reads=[prB, cst_B], writes=[RR2_B[r]])
                    pend = (tb, po, poB)
                part_b(*pend)
            if KSTOP == 54:
                S.barrier()
                return nc
    S.barrier()

    if KSTOP == 5:
        return nc
    OFF_MG = OFF_LOCAL
    OFF_X1B = OFF_MG + 36864
    OFF_WP3 = OFF_X1B + 36864
    at(OFF_MG)
    MG = sb("MG", [128, DC, NOWN], BF16); MG_B = Buf()
    X1b = sb("X1b", [128, DC, NOWN], BF16); X1b_B = Buf()
    wp_setup(6, 4096)
    assert ar["off"] <= NA_BYTES
    at(OFF_X1B)
    SA = sb("SA", [128, 512], F32); SA_B = Buf()
    SB_ = sb("SB_", [128, 512], F32); SB_B = Buf()
    T1 = sb("T1", [128, 384], F32); T1_B = Buf()
    T2 = sb("T2", [128, 384], F32); T2_B = Buf()
    for dch in range(DC):
        c0 = dch * 128
        wa, waB = wpanel(w_in, C_MA + c0, 128)
        wb, wbB = wpanel(w_in, C_MB + c0, 128)
        wg_, wgB = wpanel(w_bg, c0, 128)
        wf_, wfB = wpanel(w_bf, c0, 128, kc=8)
        for (a, n) in ttiles(NOWN):
            proj_fm(wa, waB, 0, XT, XT_B, a, n,
                    lambda p, pB, n=n: S.op("act", lambda e: e.activation(out=SA[:, 0:n], in_=p[:, 0:n], func=AF.Sigmoid, scale=1.0, bias=C0),
                                            reads=[pB, cst_B], writes=[SA_B]))
            proj_fm(wb, wbB, 0, XT, XT_B, a, n,
                    lambda p, pB, n=n: S.op("act", lambda e: e.activation(out=SB_[:, 0:n], in_=p[:, 0:n], func=AF.Sigmoid, scale=1.0, bias=C0),
                                            reads=[pB, cst_B], writes=[SB_B]))
            proj_fm(wg_, wgB, 0, OGT, OGT_B, a, n,
                    lambda p, pB, n=n: S.op("dve", lambda e: e.tensor_tensor(out=T1[:, 0:n], in0=p[:, 0:n], in1=SA[:, 0:n], op=ALU.mult),
                                            reads=[pB, SA_B], writes=[T1_B]))
            proj_fm(wf_, wfB, 0, OFT, OFT_B, a, n,
                    lambda p, pB, n=n: S.op("dve", lambda e: e.tensor_tensor(out=T2[:, 0:n], in0=p[:, 0:n], in1=SB_[:, 0:n], op=ALU.mult),
                                            reads=[pB, SB_B], writes=[T2_B]), kc=8)
            S.op("dve", lambda e, a=a, n=n, dch=dch: e.tensor_tensor(out=MG[:, dch, a:a + n], in0=T1[:, 0:n], in1=T2[:, 0:n], op=ALU.add),
                 reads=[T1_B, T2_B], writes=[MG_B])
    S.barrier()

    if KSTOP == 6:
        return nc
    at(OFF_XT)
    R1 = sb("R1", [128, DC, NOWN], F32); R1_B = Buf()
    OFF_T = ar["off"]
    XF = [sb("XF%d" % i, [128, 512], F32) for i in range(2)]; XF_B = [Buf(), Buf()]
    MU = sb("MU", [128, 512], F32); MU_B = Buf()
    VR = sb("VR", [128, 512], F32); VR_B = Buf()
    SQ2 = [sb("SQ2%d" % i, [128, 512], F32) for i in range(2)]; SQ2_B = [Buf(), Buf()]
    TT2 = sb("TT2", [128, 512], F32); TT2_B = Buf()
    assert ar["off"] <= OFF_MG, ar["off"]
    xfc = {"i": 0}
    xTv = xT.rearrange("(dc p) t -> p dc t", p=128)
    for dch in range(DC):
        w, wB = wpanel(w_out, dch * 128, 128)
        for (a, n) in ttiles(NOWN):
            i = xfc["i"] % 2; xfc["i"] += 1
            S.dma("sp", XF[i][:, 0:n], xTv[:, dch, OWN0 + a:OWN0 + a + n], writes=[XF_B[i]])
            proj_fm(w, wB, 0, MG, MG_B, a, n,
                    lambda p, pB, i=i, a=a, n=n, dch=dch: S.op("dve", lambda e: e.scalar_tensor_tensor(out=R1[:, dch, a:a + n], in0=XF[i][:, 0:n], scalar=ALPHA,
                                                                                                  in1=p[:, 0:n], op0=ALU.mult, op1=ALU.add),
                                                               reads=[pB, XF_B[i]], writes=[R1_B]))

    def layer_norm(R, RB, ntok, gam, bet, outb, outbB):
        rbs = []
        for (a, n) in ttiles(ntok):
            p1, p1B = PS()
            for dc in range(DC):
                S.op("pe", lambda e, dc=dc: e.matmul(p1[:, 0:n], lhsT=ones, rhs=R[:, dc, a:a + n], start=(dc == 0), stop=(dc == DC - 1)),
                     reads=[ones_B, RB], writes=[p1B], inc=(dc == DC - 1))
            p2, p2B = PS()
            for dc in range(DC):
                i = dc % 2
                S.op("act", lambda e, dc=dc, i=i: e.activation(out=SQ2[i][:, 0:n], in_=R[:, dc, a:a + n], func=AF.Square, scale=1.0, bias=C0),
                     reads=[RB, cst_B], writes=[SQ2_B[i]])
                S.op("pe", lambda e, dc=dc, i=i: e.matmul(p2[:, 0:n], lhsT=ones, rhs=SQ2[i][:, 0:n], start=(dc == 0), stop=(dc == DC - 1)),
                     reads=[ones_B, SQ2_B[i]], writes=[p2B])
            S.op("dve", lambda e: e.tensor_scalar(out=MU[:, 0:n], in0=p1[:, 0:n], scalar1=1.0 / D, scalar2=None, op0=ALU.mult),
                 reads=[p1B], writes=[MU_B])
            S.op("dve", lambda e: e.tensor_tensor(out=TT2[:, 0:n], in0=MU[:, 0:n], in1=MU[:, 0:n], op=ALU.mult),
                 reads=[MU_B], writes=[TT2_B])
            S.op("dve", lambda e: e.scalar_tensor_tensor(out=VR[:, 0:n], in0=p2[:, 0:n], scalar=1.0 / D, in1=TT2[:, 0:n],
                                                         op0=ALU.mult, op1=ALU.subtract),
                 reads=[p2B, TT2_B], writes=[VR_B])
            S.op("act", lambda e: e.activation(out=VR[:, 0:n], in_=VR[:, 0:n], func=AF.Sqrt, scale=1.0, bias=CEPS_LN),
                 reads=[VR_B, cst_B], writes=[VR_B])
            S.op("dve", lambda e: e.reciprocal(out=VR[:, 0:n], in_=VR[:, 0:n]), reads=[VR_B], writes=[VR_B])
            for dc in range(DC):
                rb = Buf()
                rbs.append(rb)
                S.op("dve", lambda e, dc=dc: e.tensor_tensor(out=R[:, dc, a:a + n], in0=R[:, dc, a:a + n], in1=MU[:, 0:n], op=ALU.subtract),
                     reads=[RB, MU_B, VR_B], writes=[rb])
                S.op("dve", lambda e, dc=dc: e.tensor_tensor(out=R[:, dc, a:a + n], in0=R[:, dc, a:a + n], in1=VR[:, 0:n], op=ALU.mult),
                     reads=[VR_B], writes=[rb])
                S.op("act", lambda e, dc=dc: e.activation(out=outb[:, dc, a:a + n], in_=R[:, dc, a:a + n], func=AF.Identity,
                                                          scale=gam[:, dc:dc + 1], bias=bet[:, dc:dc + 1]),
                     reads=[rb, prm_B], writes=[outbB])
                S.op("act", lambda e, dc=dc: e.activation(out=R[:, dc, a:a + n], in_=R[:, dc, a:a + n], func=AF.Identity,
                                                          scale=gam[:, dc:dc + 1], bias=bet[:, dc:dc + 1]),
                     reads=[prm_B], writes=[rb])
        S.op("act", lambda e: e.copy(out=VR[:, 0:1], in_=VR[:, 0:1]), reads=rbs, writes=[RB, VR_B])

    layer_norm(R1, R1_B, NOWN, ln1g, ln1b, X1b, X1b_B)
    x1sv = x1s.rearrange("(dc p) t -> p dc t", p=128)
    x1s_B = Buf()
    for g in range(4):
        S.dma("sp", x1sv[:, 4 * g:4 * g + 4, :], R1[:, 4 * g:4 * g + 4, 128:NOWN], reads=[R1_B], writes=[x1s_B])
    S.barrier()

    if KSTOP == 7:
        return nc
    at(OFF_WP3)
    wp_setup(4, 6144)
    at(OFF_XT)
    R2 = sb("R2", [128, DC, NREAL], F32); R2_B = Buf()
    GE = sb("GE", [128, NREAL], F32); GE_B = Buf()
    XF = [sb("XFb%d" % i, [128, 512], F32) for i in range(2)]; XF_B = [Buf(), Buf()]
    HB = sb("HB", [128, FC // 2, NREAL], BF16); HB_B = Buf()
    GG = sb("GG", [128, NOWN], F32); GG_B = Buf()
    CV = sb("CV", [128, NREAL], F32); CV_B = Buf()
    assert ar["off"] <= OFF_X1B, ar["off"]
    for grp in range(2):
        for fl in range(FC // 2):
            fc = grp * (FC // 2) + fl
            wg_, wgB = wpanel(w_gate, fc * 128, 128)
            wu_, wuB = wpanel(w_up, fc * 128, 128)
            for (a, n) in ttiles(NOWN):
                proj_fm(wg_, wgB, 0, X1b, X1b_B, a, n, lambda p, pB, a=a, n=n: evac(GG[:, a:a + n], p[:, 0:n], [pB], [GG_B]))
            S.op("dve", lambda e: e.tensor_scalar(out=GG[:, 126:128], in0=GG[:, 126:128], scalar1=halo[:, 0:1], scalar2=None, op0=ALU.mult),
                 reads=[GG_B, halo_B], writes=[GG_B])
            S.op("dve", lambda e, fc=fc: e.tensor_scalar(out=CV, in0=GG[:, 126:126 + NREAL], scalar1=convw[:, fc, 0:1], scalar2=convb[:, fc:fc + 1],
                                                         op0=ALU.mult, op1=ALU.add),
                 reads=[GG_B, prm_B], writes=[CV_B])
            S.op("dve", lambda e, fc=fc: e.scalar_tensor_tensor(out=CV, in0=GG[:, 127:127 + NREAL], scalar=convw[:, fc, 1:2], in1=CV,
                                                                op0=ALU.mult, op1=ALU.add),
                 reads=[GG_B, prm_B, CV_B], writes=[CV_B])
            S.op("dve", lambda e, fc=fc: e.scalar_tensor_tensor(out=CV, in0=GG[:, 128:128 + NREAL], scalar=convw[:, fc, 2:3], in1=CV,
                                                                op0=ALU.mult, op1=ALU.add),
                 reads=[GG_B, prm_B, CV_B], writes=[CV_B])
            S.op("act", lambda e: e.activation(out=GE, in_=CV, func=AF.Gelu, scale=1.0, bias=C0), reads=[CV_B, cst_B], writes=[GE_B])
            for (a, n) in ttiles(NREAL):
                proj_fm(wu_, wuB, 0, X1b[:, :, 128:NOWN], X1b_B, a, n,
                        lambda p, pB, a=a, n=n, fl=fl: S.op("dve", lambda e: e.tensor_tensor(out=HB[:, fl, a:a + n], in0=p[:, 0:n], in1=GE[:, a:a + n], op=ALU.mult),
                                                            reads=[pB, GE_B], writes=[HB_B]))
        for dch in range(DC):
            w, wB = wpanel(w_down, dch * 128, 128, kc=FC // 2, kc0=grp * (FC // 2))
            for (a, n) in ttiles(NREAL):
                if grp == 0:
                    k = xfc["i"] % 2; xfc["i"] += 1
                    S.dma("sp", XF[k][:, 0:n], x1sv[:, dch, a:a + n], reads=[x1s_B], writes=[XF_B[k]])
                    proj_fm(w, wB, 0, HB, HB_B, a, n,
                            lambda p, pB, k=k, a=a, n=n, dch=dch: S.op("dve", lambda e: e.scalar_tensor_tensor(out=R2[:, dch, a:a + n], in0=XF[k][:, 0:n], scalar=ALPHA,
                                                                                                          in1=p[:, 0:n], op0=ALU.mult, op1=ALU.add),
                                                                       reads=[pB, XF_B[k]], writes=[R2_B]), kc=FC // 2)
                else:
                    proj_fm(w, wB, 0, HB, HB_B, a, n,
                            lambda p, pB, a=a, n=n, dch=dch: S.op("dve", lambda e: e.tensor_tensor(out=R2[:, dch, a:a + n], in0=R2[:, dch, a:a + n],
                                                                                                 in1=p[:, 0:n], op=ALU.add),
                                                                  reads=[pB, R2_B], writes=[R2_B]), kc=FC // 2)
    S.barrier()
    if KSTOP == 8:
        return nc
    at(86016)
    MU = sb("MUb", [128, 512], F32); MU_B = Buf()
    VR = sb("VRb", [128, 512], F32); VR_B = Buf()
    SQ2 = [sb("SQ2b%d" % i, [128, 512], F32) for i in range(2)]; SQ2_B = [Buf(), Buf()]
    TT2 = sb("TT2b", [128, 512], F32); TT2_B = Buf()
    X2b = X1b; X2b_B = X1b_B
    layer_norm(R2, R2_B, NREAL, ln2g, ln2b, X2b, X2b_B)

    PTb = sb("PTb", [128, 2, NREAL], BF16); PTb_B = Buf()
    S.dma("pool", PTb, pT.rearrange("(kc p) t -> p kc t", p=128), writes=[PTb_B])
    YO = [sb("YO%d" % i, [128, 512], F32) for i in range(2)]; YO_B = [Buf(), Buf()]
    SGT = sb("SGT", [128, 512], F32); SGT_B = Buf()
    yTv = yT.rearrange("(dc p) t -> p dc t", p=128)
    yc = {"i": 0}
    for dch in range(DC):
        w, wB = wpanel(w_pg, dch * 128, 128)
        w2, w2B = wpanel(w_pp, dch * 128, 128, kc=2)
        for (a, n) in ttiles(NREAL):
            proj_fm(w, wB, 0, X2b, X2b_B, a, n,
                    lambda p, pB, n=n: S.op("act", lambda e: e.activation(out=SGT[:, 0:n], in_=p[:, 0:n], func=AF.Sigmoid, scale=1.0, bias=C0),
                                            reads=[pB, cst_B], writes=[SGT_B]))
            k = yc["i"] % 2; yc["i"] += 1
            proj_fm(w2, w2B, 0, PTb, PTb_B, a, n,
                    lambda p, pB, n=n, k=k: S.op("dve", lambda e: e.tensor_tensor(out=YO[k][:, 0:n], in0=p[:, 0:n], in1=SGT[:, 0:n], op=ALU.mult),
                                                 reads=[pB, SGT_B], writes=[YO_B[k]]), kc=2)
            S.op("dve", lambda e, k=k, a=a, n=n, dch=dch: e.tensor_tensor(out=YO[k][:, 0:n], in0=YO[k][:, 0:n], in1=R2[:, dch, a:a + n], op=ALU.add),
                 reads=[YO_B[k], R2_B], writes=[YO_B[k]])
            S.dma("sp", yTv[:, dch, a:a + n], YO[k][:, 0:n], reads=[YO_B[k]])
    S.barrier()
    nc._marks = S.marks
    return nc


_CACHE = {}


def kernel(x, p, w_in, w_gla_lr, b_gla_lr, gla_norm_g, b_forget, w_branch_gla, w_branch_fox, w_out,
           ln1_g, ln1_b, w_gate, w_up, conv_w, conv_b, w_down, ln2_g, ln2_b, w_ple_gate, w_ple_proj):
    f = np.float32
    x = np.asarray(x, f); p = np.asarray(p, f)
    B = x.shape[0]

    def pc(v, n):
        return np.ascontiguousarray(np.asarray(v, f).reshape(n, 128).T)

    shared = {
        "ident": np.eye(128, dtype=f),
        "tri": np.ascontiguousarray(np.triu(np.ones((128, 128), f))),
        "reset": np.ascontiguousarray(np.tile((np.arange(NOWN) % 128 != 0).astype(f)[None, :], (128, 1))),
        "w_in": np.ascontiguousarray(np.asarray(w_in, f)[0]),
        "w_gla_lr": np.ascontiguousarray(np.asarray(w_gla_lr, f)[0]),
        "blr": pc(np.asarray(b_gla_lr)[0], 8),
        "gn": pc(np.asarray(gla_norm_g)[0], 4),
        "bfor": np.ascontiguousarray(np.tile(np.asarray(b_forget, f)[0][None, :], (128, 1))),
        "w_branch_gla": np.ascontiguousarray(np.asarray(w_branch_gla, f)[0]),
        "w_branch_fox": np.ascontiguousarray(np.asarray(w_branch_fox, f)[0]),
        "w_out": np.ascontiguousarray(np.asarray(w_out, f)[0]),
        "ln1g": pc(np.asarray(ln1_g)[0], DC), "ln1b": pc(np.asarray(ln1_b)[0], DC),
        "w_gate": np.ascontiguousarray(np.asarray(w_gate, f)[0]),
        "w_up": np.ascontiguousarray(np.asarray(w_up, f)[0]),
        "convw": np.ascontiguousarray(np.asarray(conv_w, f)[0].reshape(3, FC, 128).transpose(2, 1, 0)),
        "convb": pc(np.asarray(conv_b)[0], FC),
        "w_down": np.ascontiguousarray(np.asarray(w_down, f)[0]),
        "ln2g": pc(np.asarray(ln2_g)[0], DC), "ln2b": pc(np.asarray(ln2_b)[0], DC),
        "w_ple_gate": np.ascontiguousarray(np.asarray(w_ple_gate, f)[0]),
        "w_ple_proj": np.ascontiguousarray(np.asarray(w_ple_proj, f)[0]),
    }
    in_maps = []
    for c in range(8):
        b, j = c // 4, c % 4
        g0 = 1024 * j + 1024 - NWIN
        xw = np.zeros((NWIN, D), f)
        lo = max(g0, 0)
        xw[lo - g0:, :] = x[b, lo:1024 * j + 1024, :]
        valid = (np.arange(NWIN) + g0) >= 0
        km = np.where(valid, 0.0, -30000.0).astype(f).reshape(NBLK, 128).T
        m = dict(shared)
        m["xT"] = np.ascontiguousarray(xw.T)
        m["pT"] = np.ascontiguousarray(p[0, b, 1024 * j:1024 * j + 1024, :].T)
        m["kmask"] = np.ascontiguousarray(km)
        m["halo"] = np.full((128, 1), 0.0 if j == 0 else 1.0, f)
        in_maps.append(m)
    if "nc" not in _CACHE:
        _CACHE["nc"] = build_program()
    res = run_bass_kernel_spmd(_CACHE["nc"], in_maps, core_ids=list(range(8)))
    out = np.empty((B, S_LEN, D), f)
    for c in range(8):
        b, j = c // 4, c % 4
        out[b, 1024 * j:1024 * j + 1024, :] = res.results[c]["yT"].T
    return out
```

```python
import numpy as np
import concourse.bass as bass
import concourse.mybir as mybir
from concourse.bass_utils import run_bass_kernel_spmd

F32 = mybir.dt.float32
BF16 = mybir.dt.bfloat16
AF = mybir.ActivationFunctionType
ALU = mybir.AluOpType

D = 2048
S_LEN = 4096
NWIN = 4224
NOWN = 1152
NREAL = 1024
OWN0 = NWIN - NOWN
NBLK = NWIN // 128
DC = 16
DFF = 5632
FC = 44
ALPHA = 2.0 ** 0.25
C_GQ, C_GK, C_GV, C_GR, C_GLR = 0, 1024, 2048, 4096, 6144
C_FQ, C_FK, C_FV, C_FF, C_MA, C_MB = 6160, 7184, 8208, 9232, 9240, 11288


class Buf:
    __slots__ = ("w", "r")

    def __init__(self):
        self.w = None
        self.r = []


class Sync:
    def __init__(self, nc, n_dma_sems=20):
        self.nc = nc
        self.engs = {"pe": nc.tensor, "act": nc.scalar, "dve": nc.vector, "pool": nc.gpsimd, "sp": nc.sync}
        self.semobj = {}
        self.cnt = {}
        for k in self.engs:
            self.semobj[k] = nc.alloc_semaphore("sem_" + k)
            self.cnt[k] = 0
        self.waited = {k: {} for k in self.engs}
        self.npe = 0
        self.marks = []
        self.dq = {}
        for q in ("sp", "pool"):
            keys = []
            for i in range(n_dma_sems):
                key = "d_%s_%d" % (q, i)
                self.semobj[key] = nc.alloc_semaphore(key)
                self.cnt[key] = 0
                keys.append(key)
            self.dq[q] = [keys, 0]

    def _wait(self, eng, deps):
        need = {}
        for tok in deps:
            if tok is None:
                continue
            k, v = tok
            if k == "pe" and eng == "pe":
                continue
            if v > need.get(k, 0):
                need[k] = v
        for k, v in need.items():
            if self.waited[eng].get(k, 0) >= v:
                continue
            self.engs[eng].wait_ge(self.semobj[k], v)
            self.waited[eng][k] = v

    def _deps(self, reads, writes):
        deps = []
        for b in reads:
            deps.append(b.w)
        for b in writes:
            deps.append(b.w)
            deps.extend(b.r)
        return deps

    def op(self, eng, fn, reads=(), writes=(), inc=True):
        self._wait(eng, self._deps(reads, writes))
        inst = fn(self.engs[eng])
        if eng == "pe":
            self.npe += 1
        tok = (eng, self.cnt[eng] + 1)
        if inc:
            inst.then_inc(self.semobj[eng], 1)
            self.cnt[eng] += 1
        for b in reads:
            b.r.append(tok)
        for b in writes:
            b.w = tok
            b.r = []
        return inst

    def dma(self, q, out, in_, reads=(), writes=()):
        keys, idx = self.dq[q]
        key = keys[idx % len(keys)]
        self.dq[q][1] = idx + 1
        deps = self._deps(reads, writes)
        deps.append((key, self.cnt[key]) if self.cnt[key] else None)
        self._wait(q, deps)
        inst = self.engs[q].dma_start(out=out, in_=in_)
        inst.then_inc(self.semobj[key], 16)
        self.cnt[key] += 16
        tok = (key, self.cnt[key])
        for b in reads:
            b.r.append(tok)
        for b in writes:
            b.w = tok
            b.r = []
        return inst

    def barrier(self):
        self.marks.append(self.npe)
        toks = [(k, v) for k, v in self.cnt.items() if v > 0]
        for e in self.engs:
            self._wait(e, [t for t in toks if not (t[0] == e)])


def build_program():
    import os
    KSTOP = int(os.environ.get("KSTOP", "99"))
    nc = bass.Bass("TRN2", target_bir_lowering=False)

    def din(name, shape, dt=F32):
        return nc.dram_tensor(name, list(shape), dt, kind="ExternalInput").ap()

    xT = din("xT", [D, NWIN])
    pT = din("pT", [256, NREAL])
    kmask_d = din("kmask", [128, NBLK])
    halo_d = din("halo", [128, 1])
    ident_d = din("ident", [128, 128])
    tri_d = din("tri", [128, 128])
    reset_d = din("reset", [128, NOWN])
    w_in = din("w_in", [D, 13336])
    w_lr = din("w_gla_lr", [16, 1024])
    blr_d = din("blr", [128, 8])
    gn_d = din("gn", [128, 4])
    bf_d = din("bfor", [128, 8])
    w_bg = din("w_branch_gla", [D, D])
    w_bf = din("w_branch_fox", [1024, D])
    w_out = din("w_out", [D, D])
    ln1g_d = din("ln1g", [128, DC])
    ln1b_d = din("ln1b", [128, DC])
    w_gate = din("w_gate", [D, DFF])
    w_up = din("w_up", [D, DFF])
    convw_d = din("convw", [128, FC, 3])
    convb_d = din("convb", [128, FC])
    w_down = din("w_down", [DFF, D])
    ln2g_d = din("ln2g", [128, DC])
    ln2b_d = din("ln2b", [128, DC])
    w_pg = din("w_ple_gate", [D, D])
    w_pp = din("w_ple_proj", [256, D])
    yT = nc.dram_tensor("yT", [D, NREAL], F32, kind="ExternalOutput").ap()
    x1s = nc.dram_tensor("x1s", [D, NREAL], F32, kind="Internal").ap()

    S = Sync(nc)
    NA_BYTES = 212736
    arena = nc.alloc_sbuf_tensor("arena", [128, NA_BYTES // 4], F32)
    ar = {"off": 0}

    def at(off):
        ar["off"] = off

    def sb(name, shape, dt):
        n = 1
        for s_ in shape[1:]:
            n *= s_
        nbytes = n * (2 if dt == BF16 else 4)
        off = (ar["off"] + 31) // 32 * 32
        assert off + nbytes <= NA_BYTES, (name, off, nbytes)
        ar["off"] = off + nbytes
        v = arena[0:shape[0], off // 4:(off + nbytes) // 4]
        if dt == BF16:
            v = v.bitcast(BF16)
        if len(shape) == 3:
            v = v.rearrange("p (a b) -> p a b", a=shape[1])
        elif len(shape) == 4:
            v = v.rearrange("p (a b c) -> p a b c", a=shape[1], b=shape[2])
        return v
    NPS = 7
    ps_t = [nc.alloc_psum_tensor("ps%d" % i, [128, 512], F32) for i in range(NPS)]
    ps_b = [Buf() for _ in range(NPS)]
    pst = nc.alloc_psum_tensor("pst", [128, 1024], BF16)
    pst_b = [Buf() for _ in range(8)]
    st = {"ps": 0, "psa": 0, "pst": 0, "ev": 0}

    def PS():
        i = st["ps"] % 5
        st["ps"] += 1
        return ps_t[i], ps_b[i]

    def PSA():
        i = 5 + st["psa"] % 2
        st["psa"] += 1
        return ps_t[i], ps_b[i]

    def PST():
        i = st["ps"] % 5
        st["ps"] += 1
        return ps_t[i][:, 0:64].bitcast(BF16), ps_b[i]

    def evac(out, in_, reads, writes):
        st["ev"] += 1
        if st["ev"] % 2:
            S.op("act", lambda e: e.copy(out=out, in_=in_), reads=reads, writes=writes)
        else:
            S.op("dve", lambda e: e.tensor_copy(out=out, in_=in_), reads=reads, writes=writes)

    ident = sb("ident", [128, 128], BF16); ident_B = Buf()
    tri = sb("tri", [128, 128], F32); tri_B = Buf()
    trib = sb("trib", [128, 128], BF16); trib_B = Buf()
    ones = sb("ones", [128, 128], F32); ones_B = Buf()
    reset = sb("reset", [128, NOWN], F32); reset_B = Buf()
    kmask = sb("kmask", [128, NBLK], F32); kmask_B = Buf()
    halo = sb("halo", [128, 1], F32); halo_B = Buf()
    cst = sb("cst", [128, 4], F32); cst_B = Buf()
    nblr = sb("nblr", [128, 8], F32); nblr_B = Buf()
    gn = sb("gn", [128, 4], F32); gn_B = Buf()
    bfor = sb("bfor", [128, 8], F32); bfor_B = Buf()
    ln1g = sb("ln1g", [128, DC], F32); ln1b = sb("ln1b", [128, DC], F32)
    ln2g = sb("ln2g", [128, DC], F32); ln2b = sb("ln2b", [128, DC], F32)
    convw = sb("convw", [128, FC, 3], F32); convb = sb("convb", [128, FC], F32)
    prm_B = Buf()
    wlr = sb("wlr", [32, 1024], BF16); wlr_B = Buf()
    ones33 = sb("ones33", [128, NBLK], F32)

    S.dma("pool", ident[:], ident_d, writes=[ident_B])
    S.dma("pool", trib[:], tri_d, writes=[trib_B])
    S.op("dve", lambda e: e.memset(wlr[:], 0.0), writes=[wlr_B])
    S.dma("pool", wlr[0:16, :], w_lr, writes=[wlr_B])
    S.dma("sp", tri[:], tri_d, writes=[tri_B])
    S.dma("sp", reset[:], reset_d, writes=[reset_B])
    S.dma("sp", kmask[:], kmask_d, writes=[kmask_B])
    S.dma("sp", halo[:], halo_d, writes=[halo_B])
    S.dma("sp", nblr[:], blr_d, writes=[nblr_B])
    S.dma("sp", gn[:], gn_d, writes=[gn_B])
    S.dma("sp", bfor[:], bf_d, writes=[bfor_B])
    for t_, d_ in ((ln1g, ln1g_d), (ln1b, ln1b_d), (ln2g, ln2g_d), (ln2b, ln2b_d), (convw, convw_d), (convb, convb_d)):
        S.dma("sp", t_[:], d_, writes=[prm_B])
    S.op("dve", lambda e: e.memset(ones[:], 1.0), writes=[ones_B])
    S.op("dve", lambda e: e.memset(ones33[:], 1.0), writes=[ones_B])
    S.op("dve", lambda e: e.memset(cst[:, 0:1], 0.0), writes=[cst_B])
    S.op("dve", lambda e: e.memset(cst[:, 1:2], 1.0), writes=[cst_B])
    S.op("dve", lambda e: e.memset(cst[:, 2:3], 1e-5), writes=[cst_B])
    S.op("dve", lambda e: e.memset(cst[:, 3:4], 1e-6), writes=[cst_B])
    S.op("dve", lambda e: e.tensor_scalar(out=nblr[:], in0=nblr[:], scalar1=-1.0, scalar2=None, op0=ALU.mult),
         reads=[nblr_B], writes=[nblr_B])
    S.barrier()
    C0, C1, CEPS_LN, CEPS_RMS = cst[:, 0:1], cst[:, 1:2], cst[:, 2:3], cst[:, 3:4]
    if KSTOP == 1:
        return nc


    CONST_END = ar["off"]
    assert CONST_END <= 12288, CONST_END
    OFF_XT = 12288
    OFF_OFT = OFF_XT + 36864
    OFF_OGT = OFF_OFT + 18432
    OFF_LOCAL = OFF_OGT + 36864
    at(OFF_XT)
    XT = sb("XT", [128, DC, NOWN], BF16)
    XT_B = [Buf() for _ in range(4)]
    OFT = sb("OFT", [128, 8, NOWN], BF16); OFT_B = Buf()
    OGT = sb("OGT", [128, 16, NOWN], BF16); OGT_B = Buf()
    wp = {"bufs": [], "B": [], "i": 0}

    def wp_setup(nbuf, nbytes):
        wp["bufs"] = [sb("WP%d" % i, [128, nbytes // 2], BF16) for i in range(nbuf)]
        wp["B"] = [Buf() for _ in range(nbuf)]
        wp["i"] = 0

    def wpanel(wsrc, c0, ncols, kc=DC, kc0=0):
        i = wp["i"] % len(wp["bufs"])
        wp["i"] += 1
        src = wsrc.rearrange("(kc p) n -> p kc n", p=128)[:, kc0:kc0 + kc, c0:c0 + ncols]
        v = wp["bufs"][i][:, 0:kc * ncols].rearrange("p (k n) -> p k n", k=kc)
        S.dma("pool", v, src, writes=[wp["B"][i]])
        return v, wp["B"][i]

    def load_xt(t0, ntok):
        for g in range(4):
            src = xT.rearrange("(dc p) t -> p dc t", p=128)[:, 4 * g:4 * g + 4, t0:t0 + ntok]
            S.dma("pool", XT[:, 4 * g:4 * g + 4, 0:ntok], src, writes=[XT_B[g]])

    def ttiles(ntok):
        n = 512 if ntok % 512 == 0 else 384
        return [(a, n) for a in range(0, ntok, n)]

    def proj_fm(w, wB, wcol, act, actB, t0, n, cb, kc=DC):
        p, pB = PS()
        for dc in range(kc):
            rb = actB[dc * len(actB) // kc] if isinstance(actB, list) else actB
            S.op("pe", lambda e, dc=dc: e.matmul(p[:, 0:n], lhsT=w[:, dc, wcol:wcol + 128], rhs=act[:, dc, t0:t0 + n],
                                                  start=(dc == 0), stop=(dc == kc - 1)),
                 reads=[wB, rb], writes=[pB], inc=(dc == kc - 1))
        cb(p, pB)

    def proj_tm(w, wB, ncols, act, actB, tb, cb, kc=DC):
        p, pB = PS()
        for dc in range(kc):
            rb = actB[dc * len(actB) // kc] if isinstance(actB, list) else actB
            S.op("pe", lambda e, dc=dc: e.matmul(p[:, 0:ncols], lhsT=act[:, dc, tb * 128:(tb + 1) * 128], rhs=w[:, dc, 0:ncols],
                                                  start=(dc == 0), stop=(dc == kc - 1)),
                 reads=[wB, rb], writes=[pB], inc=(dc == kc - 1))
        cb(p, pB)

    SUPER = [(0, 1024), (1024, 1024), (2048, 1024), (OWN0, NOWN)]
    w_in_v = w_in.rearrange("(kc p) n -> p kc n", p=128)

    at(OFF_OGT)
    KT = sb("KT", [128, 4, NWIN], BF16); KT_B = Buf()
    VA = sb("VA", [128, NBLK, 4, 130], BF16); VA_B = Buf()
    QT = sb("QT", [128, 4, NOWN], BF16); QT_B = Buf()
    LF = sb("LF", [128, NBLK, 8], F32); LF_B = Buf()
    CK = sb("CK", [128, NBLK, 8], F32); CK_B = Buf()
    TOT = sb("TOT", [128, NBLK, 8], F32); TOT_B = Buf()
    INC = sb("INC", [128, NBLK, 8], F32); INC_B = Buf()
    WFF = sb("WFF", [128, DC, 8], BF16); WFF_B = Buf()
    BI = [sb("BI%d" % i, [128, NBLK], F32) for i in range(2)]; BI_B = [Buf(), Buf()]
    PT4 = [sb("PT4%d" % i, [128, 512], BF16) for i in range(4)]; PT4_B = [Buf() for _ in range(4)]
    TMP = [sb("TMP%d" % i, [128, 512], F32) for i in range(2)]; TMP_B = [Buf(), Buf()]
    ON = [sb("ON%d" % i, [128, 128], BF16) for i in range(2)]; ON_B = [Buf(), Buf()]
    sm = sb("sm", [128, 8], F32); sm_B = Buf()
    t8 = sb("t8", [128, 8], F32); t8_B = Buf()
    wp_setup(3, 16384)
    S.dma("pool", WFF, w_in_v[:, :, C_FF:C_FF + 8], writes=[WFF_B])
    S.op("dve", lambda e: e.memset(VA[:, :, :, 128:130], 1.0), writes=[VA_B])
    SCL = 128.0 ** -0.5
    cnt = {"pt": 0, "on": 0, "bi": 0, "tm": 0, "cur_bi": 0, "po": None}

    XTF = [XT.rearrange("p a b -> p (a b)")[:, i * DC * 512:(i + 1) * DC * 512].rearrange("p (a b) -> p a b", a=DC) for i in range(2)]
    XTF_B = [[Buf() for _ in range(4)] for _ in range(2)]
    FT = [(512 * k_, 512) for k_ in range(8)] + [(4096, 128)]
    xT_v = xT.rearrange("(dc p) t -> p dc t", p=128)
    for hp in range(2):
        wK, wKB = wpanel(w_in, C_FK + hp * 512, 512)
        wV, wVB = wpanel(w_in, C_FV + hp * 512, 512)
        wQ, wQB = wpanel(w_in, C_FQ + hp * 512, 512)
        for ti, (t0, n) in enumerate(FT):
            b = ti % 2
            X_, XB_ = XTF[b], XTF_B[b]
            for g in range(4):
                S.dma("pool", X_[:, 4 * g:4 * g + 4, 0:n], xT_v[:, 4 * g:4 * g + 4, t0:t0 + n], writes=[XB_[g]])
            for h in range(4):
                proj_fm(wK, wKB, h * 128, X_, XB_, 0, n,
                        lambda p, pB, h=h, n=n, t0=t0: evac(KT[:, h, t0:t0 + n], p[:, 0:n], [pB], [KT_B]))
            for tb in range(n // 128):
                blk = t0 // 128 + tb
                proj_tm(wV, wVB, 512, X_, XB_, tb,
                        lambda p, pB, blk=blk: evac(VA[:, blk, :, 0:128], p[:, 0:512].rearrange("p (h d) -> p h d", h=4), [pB], [VA_B]))
            if hp == 0:
                for tb in range(n // 128):
                    blk = t0 // 128 + tb

                    def ffcb(p, pB, blk=blk):
                        S.op("dve", lambda e: e.tensor_tensor(out=t8, in0=p[:, 0:8], in1=bfor, op=ALU.add),
                             reads=[pB, bfor_B], writes=[t8_B])
                        S.op("act", lambda e: e.activation(out=t8, in_=t8, func=AF.Exp, scale=-1.0, bias=C0),
                             reads=[t8_B, cst_B], writes=[t8_B])
                        S.op("act", lambda e: e.activation(out=LF[:, blk, :], in_=t8, func=AF.Ln, scale=1.0, bias=C1),
                             reads=[t8_B, cst_B], writes=[LF_B])
                    proj_tm(WFF, WFF_B, 8, X_, XB_, tb, ffcb)
            if t0 >= OWN0:
                for h in range(4):
                    proj_fm(wQ, wQB, h * 128, X_, XB_, 0, n,
                            lambda p, pB, h=h, n=n, t0=t0: evac(QT[:, h, t0 - OWN0:t0 - OWN0 + n], p[:, 0:n], [pB], [QT_B]))
        if hp == 0:
            LFf = LF.rearrange("p b h -> p (b h)")
            p1, p1B = PS()
            S.op("pe", lambda e: e.matmul(p1[:, 0:NBLK * 8], lhsT=tri, rhs=LFf, start=True, stop=True),
                 reads=[tri_B, LF_B], writes=[p1B])
            p2, p2B = PS()
            S.op("pe", lambda e: e.matmul(p2[:, 0:NBLK * 8], lhsT=ones, rhs=LFf, start=True, stop=True),
                 reads=[ones_B, LF_B], writes=[p2B])
            S.op("dve", lambda e: e.tensor_copy(out=TOT.rearrange("p b h -> p (b h)"), in_=p2[:, 0:NBLK * 8]),
                 reads=[p2B], writes=[TOT_B])
            for h in range(8):
                S.op("dve", lambda e, h=h: e.tensor_tensor_scan(out=INC[:, :, h], data0=ones33, data1=TOT[:, :, h],
                                                                 initial=0.0, op0=ALU.mult, op1=ALU.add),
                     reads=[TOT_B, ones_B], writes=[INC_B])
            S.op("dve", lambda e: e.tensor_tensor(out=CK, in0=INC, in1=TOT, op=ALU.subtract),
                 reads=[INC_B, TOT_B], writes=[CK_B])
            S.op("dve", lambda e: e.tensor_tensor(out=CK.rearrange("p b h -> p (b h)"), in0=CK.rearrange("p b h -> p (b h)"),
                                                  in1=p1[:, 0:NBLK * 8], op=ALU.add),
                 reads=[CK_B, p1B], writes=[CK_B])
            S.op("dve", lambda e: e.tensor_tensor(out=CK, in0=CK, in1=kmask.unsqueeze(2).broadcast_to([128, NBLK, 8]), op=ALU.add),
                 reads=[CK_B, kmask_B], writes=[CK_B])
        if KSTOP == 3:
            S.barrier()
            return nc
        items = []
        for h in range(4):
            for i in range(NOWN // 128):
                qb = OWN0 // 128 + i
                js = list(range(0, qb + 1, 4))
                for gi, j0 in enumerate(js):
                    items.append({"h": h, "i": i, "qb": qb, "j0": j0, "nj": min(4, qb + 1 - j0),
                                  "first": gi == 0, "last": gi == len(js) - 1})

        def stageA(it):
            p, pB = PS()
            it["p"], it["pB"] = p, pB
            h, i, j0, nj = it["h"], it["i"], it["j0"], it["nj"]
            for jj in range(nj):
                S.op("pe", lambda e, jj=jj: e.matmul(p[:, jj * 128:(jj + 1) * 128], lhsT=KT[:, h, (j0 + jj) * 128:(j0 + jj + 1) * 128],
                                                     rhs=QT[:, h, i * 128:(i + 1) * 128], start=True, stop=True),
                     reads=[KT_B, QT_B], writes=[pB], inc=(jj == nj - 1))

        def stageB(it):
            h, i, j0, nj, qb = it["h"], it["i"], it["j0"], it["nj"], it["qb"]
            hg = hp * 4 + h
            if it["first"]:
                bi = cnt["bi"] % 2; cnt["bi"] += 1
                cnt["cur_bi"] = bi
                S.op("dve", lambda e: e.tensor_scalar(out=BI[bi], in0=CK[:, :, hg], scalar1=INC[:, qb, hg:hg + 1],
                                                      scalar2=None, op0=ALU.subtract),
                     reads=[CK_B, INC_B], writes=[BI_B[bi]])
            bi = cnt["cur_bi"]
            p, pB = it["p"], it["pB"]
            t = cnt["tm"] % 2; cnt["tm"] += 1
            k = cnt["pt"] % 4; cnt["pt"] += 1
            it["k"] = k
            w_ = nj * 128
            S.op("dve", lambda e: e.scalar_tensor_tensor(out=TMP[t][:, 0:w_].rearrange("p (j t) -> p j t", j=nj),
                                                         in0=p[:, 0:w_].rearrange("p (j t) -> p j t", j=nj), scalar=SCL,
                                                         in1=BI[bi][:, j0:j0 + nj].unsqueeze(2).broadcast_to([128, nj, 128]),
                                                         op0=ALU.mult, op1=ALU.add),
                 reads=[pB, BI_B[bi]], writes=[TMP_B[t]])
            S.op("act", lambda e: e.activation(out=PT4[k][:, 0:w_], in_=TMP[t][:, 0:w_], func=AF.Exp, scale=1.0, bias=C0),
                 reads=[TMP_B[t], cst_B], writes=[PT4_B[k]])
            if it["last"]:
                S.op("dve", lambda e: e.tensor_tensor(out=PT4[k][:, w_ - 128:w_], in0=PT4[k][:, w_ - 128:w_], in1=trib, op=ALU.mult),
                     reads=[PT4_B[k], trib_B], writes=[PT4_B[k]])

        def stageC(it):
            h, i, j0, nj = it["h"], it["i"], it["j0"], it["nj"]
            hg = hp * 4 + h
            k = it["k"]
            if it["first"]:
                cnt["po"] = PSA()
            po, poB = cnt["po"]
            for jj in range(nj):
                S.op("pe", lambda e, jj=jj: e.matmul(po[:, 0:130], lhsT=PT4[k][:, jj * 128:(jj + 1) * 128], rhs=VA[:, j0 + jj, h, :],
                                                     start=(it["first"] and jj == 0), stop=(it["last"] and jj == nj - 1)),
                     reads=[PT4_B[k], VA_B], writes=[poB], inc=(jj == nj - 1))
            if it["last"]:
                S.op("dve", lambda e: e.tensor_scalar(out=sm[:, 0:1], in0=po[:, 128:129], scalar1=1e-30, scalar2=None, op0=ALU.max),
                     reads=[poB], writes=[sm_B])
                S.op("dve", lambda e: e.reciprocal(out=sm[:, 1:2], in_=sm[:, 0:1]), reads=[sm_B], writes=[sm_B])
                o = cnt["on"] % 2; cnt["on"] += 1
                S.op("dve", lambda e: e.tensor_scalar(out=ON[o], in0=po[:, 0:128], scalar1=sm[:, 1:2], scalar2=None, op0=ALU.mult),
                     reads=[poB, sm_B], writes=[ON_B[o]])
                epi.append((cnt["kk"] + 2, o, hg, i))

        def flush_epi(force=False):
            while epi and (force or epi[0][0] <= cnt["kk"]):
                _, o, hg, i = epi.pop(0)
                pt_, ptB = PST()
                S.op("pe", lambda e: e.transpose(pt_, ON[o], ident), reads=[ON_B[o], ident_B], writes=[ptB])
                evac(OFT[:, hg, i * 128:(i + 1) * 128], pt_, [ptB], [OFT_B])

        epi = []
        n_it = len(items)
        for kk in range(n_it + 3):
            cnt["kk"] = kk
            if kk < n_it:
                stageA(items[kk])
            if 0 <= kk - 1 < n_it:
                stageB(items[kk - 1])
            if 0 <= kk - 3 < n_it:
                stageC(items[kk - 3])
            flush_epi()
        flush_epi(force=True)
    S.barrier()

    if KSTOP == 4:
        return nc
    at(OFF_LOCAL)
    wp_setup(2, 8192)
    Sst = [sb("Sst%d" % h, [128, 2, 512], F32) for h in range(4)]; Sst_B = [[Buf(), Buf()] for _ in range(4)]
    Sbf = sb("Sbf", [128, 2, 512], BF16); Sbf_B = [Buf(), Buf()]
    GLR = sb("GLR", [32, NOWN], BF16); GLR_B = Buf()
    S.op("dve", lambda e: e.memset(GLR, 0.0), writes=[GLR_B])
    WG16 = sb("WG16", [128, DC, 16], BF16); WG16_B = Buf()
    S.dma("pool", WG16, w_in_v[:, :, C_GLR:C_GLR + 16], writes=[WG16_B])
    for h in range(4):
        S.op("dve", lambda e, h=h: e.memset(Sst[h], 0.0), writes=Sst_B[h])
    Lb = sb("Lb", [128, NOWN], F32); Lb_B = Buf()
    Bb = sb("Bb", [128, NOWN], F32); Bb_B = Buf()
    Db = sb("Db", [128, NOWN], F32); Db_B = Buf()
    EN = sb("EN", [128, NOWN], F32); EN_B = Buf()
    EP = sb("EP", [128, NOWN], F32); EP_B = Buf()
    EBL = sb("EBL", [128, 2, 16], F32); EBL_B = Buf()
    KHT = sb("KHT", [128, 2, NOWN], BF16); KHT_B = Buf()
    KNT = sb("KNT", [128, 2, NOWN], BF16); KNT_B = Buf()
    QPT = sb("QPT", [128, 2, NOWN], BF16); QPT_B = Buf()
    KHt = sb("KHt", [128, 9, 256], BF16); KHt_B = Buf()
    Vh = sb("Vh", [128, 9, 512], BF16); Vh_B = Buf()
    AT = sb("AT", [128, 128], BF16); AT_B = Buf()
    SG = sb("SG", [128, 4, NOWN], BF16); SG_B = Buf()
    SQ4 = sb("SQ4", [128, 512], BF16); SQ4_B = Buf()
    TT4 = sb("TT4", [128, 512], F32); TT4_B = Buf()
    onesb = sb("onesb", [128, 128], BF16); onesb_B = Buf()
    S.op("dve", lambda e: e.memset(onesb, 1.0), writes=[onesb_B])
    RR2 = [sb("RR%d" % i, [128, 128], F32) for i in range(2)]; RR2_B = [Buf(), Buf()]
    ATA = sb("ATA", [128, 9, 128], BF16); ATA_B = [Buf() for _ in range(9)]

    for (t0, ntok) in SUPER:
        load_xt(t0, ntok)
        own = (t0 == OWN0)
        nb = ntok // 128
        for (a, n) in ttiles(ntok):
            p, pB = PS()
            for dc in range(DC):
                S.op("pe", lambda e, dc=dc, p=p, a=a, n=n: e.matmul(p[0:16, 0:n], lhsT=WG16[:, dc, :], rhs=XT[:, dc, a:a + n],
                                                                  start=(dc == 0), stop=(dc == DC - 1)),
                     reads=[WG16_B, XT_B[dc // 4]], writes=[pB], inc=(dc == DC - 1))
            evac(GLR[0:16, a:a + n], p[0:16, 0:n], [pB], [GLR_B])
        if KSTOP == 51:
            S.barrier()
            return nc
        if KSTOP == 55 and own:
            S.barrier()
            return nc
        for h in range(4):
            wk, wkB = wpanel(w_in, C_GK + h * 256, 256)
            if own:
                wq, wqB = wpanel(w_in, C_GQ + h * 256, 256)
            for kc in range(2):
                col = h * 256 + kc * 128
                for (a, n) in ttiles(ntok):
                    p, pB = PS()
                    S.op("pe", lambda e, p=p, a=a, n=n, col=col: e.matmul(p[:, 0:n], lhsT=wlr[:, col:col + 128], rhs=GLR[:, a:a + n],
                                                                        start=True, stop=True),
                         reads=[wlr_B, GLR_B], writes=[pB])
                    S.op("act", lambda e, p=p, a=a, n=n, h=h, kc=kc: e.activation(out=Lb[:, a:a + n], in_=p[:, 0:n], func=AF.Exp, scale=-1.0,
                                                                                bias=nblr[:, h * 2 + kc:h * 2 + kc + 1]),
                         reads=[pB, nblr_B], writes=[Lb_B])
                if KSTOP == 511:
                    S.barrier()
                    return nc
                S.op("act", lambda e: e.activation(out=Lb[:, 0:ntok], in_=Lb[:, 0:ntok], func=AF.Ln, scale=1.0, bias=C1),
                     reads=[Lb_B, cst_B], writes=[Lb_B])
                if KSTOP == 512:
                    S.barrier()
                    return nc
                S.op("dve", lambda e: e.tensor_tensor_scan(out=Bb[:, 0:ntok], data0=reset[:, 0:ntok], data1=Lb[:, 0:ntok],
                                                           initial=0.0, op0=ALU.mult, op1=ALU.add),
                     reads=[reset_B, Lb_B], writes=[Bb_B])
                if KSTOP == 513:
                    S.barrier()
                    return nc
                B3 = Bb[:, 0:ntok].rearrange("p (b t) -> p b t", t=128)
                S.op("dve", lambda e: e.tensor_tensor(out=Db[:, 0:ntok].rearrange("p (b t) -> p b t", t=128), in0=B3,
                                                      in1=B3[:, :, 127:128].broadcast_to([128, nb, 128]), op=ALU.subtract),
                     reads=[Bb_B], writes=[Db_B])
                S.op("act", lambda e: e.activation(out=Db[:, 0:ntok], in_=Db[:, 0:ntok], func=AF.Exp, scale=1.0 / 16, bias=C0),
                     reads=[Db_B, cst_B], writes=[Db_B])
                S.op("act", lambda e, kc=kc: e.activation(out=EBL[:, kc, 0:nb], in_=B3[:, :, 127], func=AF.Exp, scale=-1.0 / 16, bias=C0),
                     reads=[Bb_B, cst_B], writes=[EBL_B])
                if KSTOP == 514:
                    S.barrier()
                    return nc
                if own:
                    S.op("act", lambda e: e.activation(out=EN[:, 0:ntok], in_=Bb[:, 0:ntok], func=AF.Exp, scale=1.0 / 16, bias=C0),
                         reads=[Bb_B, cst_B], writes=[EN_B])
                    S.op("act", lambda e: e.activation(out=EP[:, 0:ntok], in_=Bb[:, 0:ntok], func=AF.Exp, scale=-1.0 / 16, bias=C0),
                         reads=[Bb_B, cst_B], writes=[EP_B])
                for (a, n) in ttiles(ntok):
                    def kcb(p, pB, a=a, n=n, kc=kc):
                        S.op("dve", lambda e: e.tensor_tensor(out=KHT[:, kc, a:a + n], in0=p[:, 0:n], in1=Db[:, a:a + n], op=ALU.mult),
                             reads=[pB, Db_B], writes=[KHT_B])
                        if own:
                            S.op("dve", lambda e: e.tensor_tensor(out=KNT[:, kc, a:a + n], in0=p[:, 0:n], in1=EN[:, a:a + n], op=ALU.mult),
                                 reads=[pB, EN_B], writes=[KNT_B])
                    proj_fm(wk, wkB, kc * 128, XT, XT_B, a, n, kcb)
                    if own:
                        def qcb(p, pB, a=a, n=n, kc=kc):
                            S.op("dve", lambda e: e.scalar_tensor_tensor(out=QPT[:, kc, a:a + n], in0=p[:, 0:n], scalar=256.0 ** -0.5,
                                                                         in1=EP[:, a:a + n], op0=ALU.mult, op1=ALU.mult),
                                 reads=[pB, EP_B], writes=[QPT_B])
                        proj_fm(wq, wqB, kc * 128, XT, XT_B, a, n, qcb)
                if KSTOP == 515:
                    S.barrier()
                    return nc
                for tb in range(nb):
                    pt_, ptB = PST()
                    S.op("pe", lambda e, pt_=pt_, tb=tb, kc=kc: e.transpose(pt_, KHT[:, kc, tb * 128:(tb + 1) * 128], ident),
                         reads=[KHT_B, ident_B], writes=[ptB])
                    evac(KHt[:, tb, kc * 128:(kc + 1) * 128], pt_, [ptB], [KHt_B])
            if KSTOP == 52:
                S.barrier()
                return nc
            for half in range(2):
                wv, wvB = wpanel(w_in, C_GV + h * 512 + half * 256, 256)
                for tb in range(nb):
                    proj_tm(wv, wvB, 256, XT, XT_B, tb,
                            lambda p, pB, tb=tb, half=half: evac(Vh[:, tb, half * 256:(half + 1) * 256], p[:, 0:256], [pB], [Vh_B]))
            if own:
                for half in range(2):
                    wr, wrB = wpanel(w_in, C_GR + h * 512 + half * 256, 256)
                    for v2 in range(2):
                        vc = half * 2 + v2
                        for (a, n) in ttiles(ntok):
                            proj_fm(wr, wrB, v2 * 128, XT, XT_B, a, n,
                                    lambda p, pB, vc=vc, a=a, n=n: (
                                        S.op("act", lambda e: e.activation(out=SG[:, vc, a:a + n], in_=p[:, 0:n], func=AF.Silu, scale=1.0, bias=C0),
                                             reads=[pB, cst_B], writes=[SG_B]),
                                        S.op("dve", lambda e: e.tensor_scalar(out=SG[:, vc, a:a + n], in0=SG[:, vc, a:a + n], scalar1=gn[:, vc:vc + 1],
                                                                              scalar2=None, op0=ALU.mult),
                                             reads=[gn_B], writes=[SG_B])))
                S.op("act", lambda e, h=h: e.copy(out=Sbf, in_=Sst[h]), reads=Sst_B[h], writes=Sbf_B)
            if KSTOP == 53:
                S.barrier()
                return nc
            if not own:
                for tb in range(nb):
                    for kc in range(2):
                        pu, puB = PS()
                        S.op("pe", lambda e, kc=kc, tb=tb, pu=pu: e.matmul(pu[:, 0:512], lhsT=KHt[:, tb, kc * 128:(kc + 1) * 128], rhs=Vh[:, tb, :],
                                                                         start=True, stop=True),
                             reads=[KHt_B, Vh_B], writes=[puB])
                        S.op("dve", lambda e, kc=kc, tb=tb, pu=pu, h=h: e.scalar_tensor_tensor(out=Sst[h][:, kc, :], in0=Sst[h][:, kc, :],
                                                                                             scalar=EBL[:, kc, tb:tb + 1], in1=pu[:, 0:512],
                                                                                             op0=ALU.mult, op1=ALU.add),
                             reads=[EBL_B, puB], writes=[Sst_B[h][kc]])
            else:
                for tb in range(nb):
                    pa, paB = PS()
                    for kc in range(2):
                        S.op("pe", lambda e, kc=kc, tb=tb, pa=pa: e.matmul(pa[:, 0:128], lhsT=KNT[:, kc, tb * 128:(tb + 1) * 128],
                                                                         rhs=QPT[:, kc, tb * 128:(tb + 1) * 128], start=(kc == 0), stop=(kc == 1)),
                             reads=[KNT_B, QPT_B], writes=[paB], inc=(kc == 1))
                    S.op("dve", lambda e, pa=pa, tb=tb: e.tensor_tensor(out=ATA[:, tb, :], in0=pa[:, 0:128], in1=tri, op=ALU.mult),
                         reads=[paB, tri_B], writes=[ATA_B[tb]])

                def part_b(tb, po, poB):
                    r = tb % 2
                    S.op("dve", lambda e: e.reciprocal(out=RR2[r], in_=RR2[r]), reads=[RR2_B[r]], writes=[RR2_B[r]])
                    S.op("dve", lambda e: e.tensor_tensor(out=TT4.rearrange("p (v t) -> p v t", v=4),
                                                          in0=po[:, 0:512].rearrange("p (v t) -> p v t", v=4),
                                                          in1=RR2[r].unsqueeze(1).broadcast_to([128, 4, 128]), op=ALU.mult),
                         reads=[poB, RR2_B[r]], writes=[TT4_B])
                    S.op("dve", lambda e: e.tensor_tensor(out=OGT[:, h * 4:(h + 1) * 4, tb * 128:(tb + 1) * 128],
                                                          in0=TT4.rearrange("p (v t) -> p v t", v=4),
                                                          in1=SG[:, :, tb * 128:(tb + 1) * 128], op=ALU.mult),
                         reads=[TT4_B, SG_B], writes=[OGT_B])

                pend = None
                for tb in range(nb):
                    last = (tb == nb - 1)
                    if not last:
                        for kc in range(2):
                            pu, puB = PS()
                            S.op("pe", lambda e, kc=kc, tb=tb, pu=pu: e.matmul(pu[:, 0:512], lhsT=KHt[:, tb, kc * 128:(kc + 1) * 128], rhs=Vh[:, tb, :],
                                                                             start=True, stop=True),
                                 reads=[KHt_B, Vh_B], writes=[puB])
                            S.op("dve", lambda e, kc=kc, tb=tb, pu=pu: e.scalar_tensor_tensor(out=Sst[h][:, kc, :], in0=Sst[h][:, kc, :],
                                                                                            scalar=EBL[:, kc, tb:tb + 1], in1=pu[:, 0:512],
                                                                                            op0=ALU.mult, op1=ALU.add),
                                 reads=[EBL_B, puB], writes=[Sst_B[h][kc]])
                    if pend is not None:
                        part_b(*pend)
                    po, poB = PSA()
                    for vc in range(4):
                        for kc in range(2):
                            S.op("pe", lambda e, kc=kc, vc=vc, tb=tb, po=po: e.matmul(po[:, vc * 128:(vc + 1) * 128],
                                                                                    lhsT=Sbf[:, kc, vc * 128:(vc + 1) * 128],
                                                                                    rhs=QPT[:, kc, tb * 128:(tb + 1) * 128],
                                                                                    start=(kc == 0), stop=False),
                                 reads=[Sbf_B[kc], QPT_B], writes=[poB], inc=False)
                        S.op("pe", lambda e, vc=vc, tb=tb, po=po: e.matmul(po[:, vc * 128:(vc + 1) * 128], lhsT=Vh[:, tb, vc * 128:(vc + 1) * 128],
                                                                         rhs=ATA[:, tb, :], start=False, stop=True),
                             reads=[Vh_B, ATA_B[tb]], writes=[poB], inc=(vc == 3))
                    if not last:
                        for kc in range(2):
                            S.op("act", lambda e, kc=kc: e.copy(out=Sbf[:, kc, :], in_=Sst[h][:, kc, :]),
                                 reads=[Sst_B[h][kc]], writes=[Sbf_B[kc]])
                    r = tb % 2
                    pr, prB = PS()
                    S.op("act", lambda e, po=po: e.activation(out=SQ4, in_=po[:, 0:512], func=AF.Square, scale=1.0, bias=C0),
                         reads=[poB, cst_B], writes=[SQ4_B])
                    for vc in range(4):
                        S.op("pe", lambda e, vc=vc, pr=pr: e.matmul(pr[:, 0:128], lhsT=onesb, rhs=SQ4[:, vc * 128:(vc + 1) * 128],
                                                                    start=(vc == 0), stop=(vc == 3)),
                             reads=[onesb_B, SQ4_B], writes=[prB], inc=(vc == 3))
                    S.op("act", lambda e, pr=pr, r=r: e.activation(out=RR2[r], in_=pr[:, 0:128], func=AF.Sqrt, scale=1.0 / 512, bias=CEPS_RMS),
                         reads=[prB, cst_B], writes=[RR2_B[r]])
                    pend = (tb, po, poB)
                part_b(*pend)
            if KSTOP == 54:
                S.barrier()
                return nc
    S.barrier()

    if KSTOP == 5:
        return nc
    OFF_MG = OFF_LOCAL
    OFF_X1B = OFF_MG + 36864
    OFF_WP3 = OFF_X1B + 36864
    at(OFF_MG)
    MG = sb("MG", [128, DC, NOWN], BF16); MG_B = Buf()
    X1b = sb("X1b", [128, DC, NOWN], BF16); X1b_B = Buf()
    wp_setup(6, 4096)
    assert ar["off"] <= NA_BYTES
    at(OFF_X1B)
    SA = sb("SA", [128, 512], F32); SA_B = Buf()
    SB_ = sb("SB_", [128, 512], F32); SB_B = Buf()
    T1 = sb("T1", [128, 384], F32); T1_B = Buf()
    T2 = sb("T2", [128, 384], F32); T2_B = Buf()
    for dch in range(DC):
        c0 = dch * 128
        wa, waB = wpanel(w_in, C_MA + c0, 128)
        wb, wbB = wpanel(w_in, C_MB + c0, 128)
        wg_, wgB = wpanel(w_bg, c0, 128)
        wf_, wfB = wpanel(w_bf, c0, 128, kc=8)
        for (a, n) in ttiles(NOWN):
            proj_fm(wa, waB, 0, XT, XT_B, a, n,
                    lambda p, pB, n=n: S.op("act", lambda e: e.activation(out=SA[:, 0:n], in_=p[:, 0:n], func=AF.Sigmoid, scale=1.0, bias=C0),
                                            reads=[pB, cst_B], writes=[SA_B]))
            proj_fm(wb, wbB, 0, XT, XT_B, a, n,
                    lambda p, pB, n=n: S.op("act", lambda e: e.activation(out=SB_[:, 0:n], in_=p[:, 0:n], func=AF.Sigmoid, scale=1.0, bias=C0),
                                            reads=[pB, cst_B], writes=[SB_B]))
            proj_fm(wg_, wgB, 0, OGT, OGT_B, a, n,
                    lambda p, pB, n=n: S.op("dve", lambda e: e.tensor_tensor(out=T1[:, 0:n], in0=p[:, 0:n], in1=SA[:, 0:n], op=ALU.mult),
                                            reads=[pB, SA_B], writes=[T1_B]))
            proj_fm(wf_, wfB, 0, OFT, OFT_B, a, n,
                    lambda p, pB, n=n: S.op("dve", lambda e: e.tensor_tensor(out=T2[:, 0:n], in0=p[:, 0:n], in1=SB_[:, 0:n], op=ALU.mult),
                                            reads=[pB, SB_B], writes=[T2_B]), kc=8)
            S.op("dve", lambda e, a=a, n=n, dch=dch: e.tensor_tensor(out=MG[:, dch, a:a + n], in0=T1[:, 0:n], in1=T2[:, 0:n], op=ALU.add),
                 reads=[T1_B, T2_B], writes=[MG_B])
    S.barrier()

    if KSTOP == 6:
        return nc
    at(OFF_XT)
    R1 = sb("R1", [128, DC, NOWN], F32); R1_B = Buf()
    OFF_T = ar["off"]
    XF = [sb("XF%d" % i, [128, 512], F32) for i in range(2)]; XF_B = [Buf(), Buf()]
    MU = sb("MU", [128, 512], F32); MU_B = Buf()
    VR = sb("VR", [128, 512], F32); VR_B = Buf()
    SQ2 = [sb("SQ2%d" % i, [128, 512], F32) for i in range(2)]; SQ2_B = [Buf(), Buf()]
    TT2 = sb("TT2", [128, 512], F32); TT2_B = Buf()
    assert ar["off"] <= OFF_MG, ar["off"]
    xfc = {"i": 0}
    xTv = xT.rearrange("(dc p) t -> p dc t", p=128)
    for dch in range(DC):
        w, wB = wpanel(w_out, dch * 128, 128)
        for (a, n) in ttiles(NOWN):
            i = xfc["i"] % 2; xfc["i"] += 1
            S.dma("sp", XF[i][:, 0:n], xTv[:, dch, OWN0 + a:OWN0 + a + n], writes=[XF_B[i]])
            proj_fm(w, wB, 0, MG, MG_B, a, n,
                    lambda p, pB, i=i, a=a, n=n, dch=dch: S.op("dve", lambda e: e.scalar_tensor_tensor(out=R1[:, dch, a:a + n], in0=XF[i][:, 0:n], scalar=ALPHA,
                                                                                                  in1=p[:, 0:n], op0=ALU.mult, op1=ALU.add),
                                                               reads=[pB, XF_B[i]], writes=[R1_B]))

    def layer_norm(R, RB, ntok, gam, bet, outb, outbB):
        rbs = []
        for (a, n) in ttiles(ntok):
            p1, p1B = PS()
            for dc in range(DC):
                S.op("pe", lambda e, dc=dc: e.matmul(p1[:, 0:n], lhsT=ones, rhs=R[:, dc, a:a + n], start=(dc == 0), stop=(dc == DC - 1)),
                     reads=[ones_B, RB], writes=[p1B], inc=(dc == DC - 1))
            p2, p2B = PS()
            for dc in range(DC):
                i = dc % 2
                S.op("act", lambda e, dc=dc, i=i: e.activation(out=SQ2[i][:, 0:n], in_=R[:, dc, a:a + n], func=AF.Square, scale=1.0, bias=C0),
                     reads=[RB, cst_B], writes=[SQ2_B[i]])
                S.op("pe", lambda e, dc=dc, i=i: e.matmul(p2[:, 0:n], lhsT=ones, rhs=SQ2[i][:, 0:n], start=(dc == 0), stop=(dc == DC - 1)),
                     reads=[ones_B, SQ2_B[i]], writes=[p2B])
            S.op("dve", lambda e: e.tensor_scalar(out=MU[:, 0:n], in0=p1[:, 0:n], scalar1=1.0 / D, scalar2=None, op0=ALU.mult),
                 reads=[p1B], writes=[MU_B])
            S.op("dve", lambda e: e.tensor_tensor(out=TT2[:, 0:n], in0=MU[:, 0:n], in1=MU[:, 0:n], op=ALU.mult),
                 reads=[MU_B], writes=[TT2_B])
            S.op("dve", lambda e: e.scalar_tensor_tensor(out=VR[:, 0:n], in0=p2[:, 0:n], scalar=1.0 / D, in1=TT2[:, 0:n],
                                                         op0=ALU.mult, op1=ALU.subtract),
                 reads=[p2B, TT2_B], writes=[VR_B])
            S.op("act", lambda e: e.activation(out=VR[:, 0:n], in_=VR[:, 0:n], func=AF.Sqrt, scale=1.0, bias=CEPS_LN),
                 reads=[VR_B, cst_B], writes=[VR_B])
            S.op("dve", lambda e: e.reciprocal(out=VR[:, 0:n], in_=VR[:, 0:n]), reads=[VR_B], writes=[VR_B])
            for dc in range(DC):
                rb = Buf()
                rbs.append(rb)
                S.op("dve", lambda e, dc=dc: e.tensor_tensor(out=R[:, dc, a:a + n], in0=R[:, dc, a:a + n], in1=MU[:, 0:n], op=ALU.subtract),
                     reads=[RB, MU_B, VR_B], writes=[rb])
                S.op("dve", lambda e, dc=dc: e.tensor_tensor(out=R[:, dc, a:a + n], in0=R[:, dc, a:a + n], in1=VR[:, 0:n], op=ALU.mult),
                     reads=[VR_B], writes=[rb])
                S.op("act", lambda e, dc=dc: e.activation(out=outb[:, dc, a:a + n], in_=R[:, dc, a:a + n], func=AF.Identity,
                                                          scale=gam[:, dc:dc + 1], bias=bet[:, dc:dc + 1]),
                     reads=[rb, prm_B], writes=[outbB])
                S.op("act", lambda e, dc=dc: e.activation(out=R[:, dc, a:a + n], in_=R[:, dc, a:a + n], func=AF.Identity,
                                                          scale=gam[:, dc:dc + 1], bias=bet[:, dc:dc + 1]),
                     reads=[prm_B], writes=[rb])
        S.op("act", lambda e: e.copy(out=VR[:, 0:1], in_=VR[:, 0:1]), reads=rbs, writes=[RB, VR_B])

    layer_norm(R1, R1_B, NOWN, ln1g, ln1b, X1b, X1b_B)
    x1sv = x1s.rearrange("(dc p) t -> p dc t", p=128)
    x1s_B = Buf()
    for g in range(4):
        S.dma("sp", x1sv[:, 4 * g:4 * g + 4, :], R1[:, 4 * g:4 * g + 4, 128:NOWN], reads=[R1_B], writes=[x1s_B])
    S.barrier()

    if KSTOP == 7:
        return nc
    at(OFF_WP3)
    wp_setup(4, 6144)
    at(OFF_XT)
    R2 = sb("R2", [128, DC, NREAL], F32); R2_B = Buf()
    GE = sb("GE", [128, NREAL], F32); GE_B = Buf()
    XF = [sb("XFb%d" % i, [128, 512], F32) for i in range(2)]; XF_B = [Buf(), Buf()]
    HB = sb("HB", [128, FC // 2, NREAL], BF16); HB_B = Buf()
    GG = sb("GG", [128, NOWN], F32); GG_B = Buf()
    CV = sb("CV", [128, NREAL], F32); CV_B = Buf()
    assert ar["off"] <= OFF_X1B, ar["off"]
    for grp in range(2):
        for fl in range(FC // 2):
            fc = grp * (FC // 2) + fl
            wg_, wgB = wpanel(w_gate, fc * 128, 128)
            wu_, wuB = wpanel(w_up, fc * 128, 128)
            for (a, n) in ttiles(NOWN):
                proj_fm(wg_, wgB, 0, X1b, X1b_B, a, n, lambda p, pB, a=a, n=n: evac(GG[:, a:a + n], p[:, 0:n], [pB], [GG_B]))
            S.op("dve", lambda e: e.tensor_scalar(out=GG[:, 126:128], in0=GG[:, 126:128], scalar1=halo[:, 0:1], scalar2=None, op0=ALU.mult),
                 reads=[GG_B, halo_B], writes=[GG_B])
            S.op("dve", lambda e, fc=fc: e.tensor_scalar(out=CV, in0=GG[:, 126:126 + NREAL], scalar1=convw[:, fc, 0:1], scalar2=convb[:, fc:fc + 1],
                                                         op0=ALU.mult, op1=ALU.add),
                 reads=[GG_B, prm_B], writes=[CV_B])
            S.op("dve", lambda e, fc=fc: e.scalar_tensor_tensor(out=CV, in0=GG[:, 127:127 + NREAL], scalar=convw[:, fc, 1:2], in1=CV,
                                                                op0=ALU.mult, op1=ALU.add),
                 reads=[GG_B, prm_B, CV_B], writes=[CV_B])
            S.op("dve", lambda e, fc=fc: e.scalar_tensor_tensor(out=CV, in0=GG[:, 128:128 + NREAL], scalar=convw[:, fc, 2:3], in1=CV,
                                                                op0=ALU.mult, op1=ALU.add),
                 reads=[GG_B, prm_B, CV_B], writes=[CV_B])
            S.op("act", lambda e: e.activation(out=GE, in_=CV, func=AF.Gelu, scale=1.0, bias=C0), reads=[CV_B, cst_B], writes=[GE_B])
            for (a, n) in ttiles(NREAL):
                proj_fm(wu_, wuB, 0, X1b[:, :, 128:NOWN], X1b_B, a, n,
                        lambda p, pB, a=a, n=n, fl=fl: S.op("dve", lambda e: e.tensor_tensor(out=HB[:, fl, a:a + n], in0=p[:, 0:n], in1=GE[:, a:a + n], op=ALU.mult),
                                                            reads=[pB, GE_B], writes=[HB_B]))
        for dch in range(DC):
            w, wB = wpanel(w_down, dch * 128, 128, kc=FC // 2, kc0=grp * (FC // 2))
            for (a, n) in ttiles(NREAL):
                if grp == 0:
                    k = xfc["i"] % 2; xfc["i"] += 1
                    S.dma("sp", XF[k][:, 0:n], x1sv[:, dch, a:a + n], reads=[x1s_B], writes=[XF_B[k]])
                    proj_fm(w, wB, 0, HB, HB_B, a, n,
                            lambda p, pB, k=k, a=a, n=n, dch=dch: S.op("dve", lambda e: e.scalar_tensor_tensor(out=R2[:, dch, a:a + n], in0=XF[k][:, 0:n], scalar=ALPHA,
                                                                                                          in1=p[:, 0:n], op0=ALU.mult, op1=ALU.add),
                                                                       reads=[pB, XF_B[k]], writes=[R2_B]), kc=FC // 2)
                else:
                    proj_fm(w, wB, 0, HB, HB_B, a, n,
                            lambda p, pB, a=a, n=n, dch=dch: S.op("dve", lambda e: e.tensor_tensor(out=R2[:, dch, a:a + n], in0=R2[:, dch, a:a + n],
                                                                                                 in1=p[:, 0:n], op=ALU.add),
                                                                  reads=[pB, R2_B], writes=[R2_B]), kc=FC // 2)
    S.barrier()
    if KSTOP == 8:
        return nc
    at(86016)
    MU = sb("MUb", [128, 512], F32); MU_B = Buf()
    VR = sb("VRb", [128, 512], F32); VR_B = Buf()
    SQ2 = [sb("SQ2b%d" % i, [128, 512], F32) for i in range(2)]; SQ2_B = [Buf(), Buf()]
    TT2 = sb("TT2b", [128, 512], F32); TT2_B = Buf()
    X2b = X1b; X2b_B = X1b_B
    layer_norm(R2, R2_B, NREAL, ln2g, ln2b, X2b, X2b_B)

    PTb = sb("PTb", [128, 2, NREAL], BF16); PTb_B = Buf()
    S.dma("pool", PTb, pT.rearrange("(kc p) t -> p kc t", p=128), writes=[PTb_B])
    YO = [sb("YO%d" % i, [128, 512], F32) for i in range(2)]; YO_B = [Buf(), Buf()]
    SGT = sb("SGT", [128, 512], F32); SGT_B = Buf()
    yTv = yT.rearrange("(dc p) t -> p dc t", p=128)
    yc = {"i": 0}
    for dch in range(DC):
        w, wB = wpanel(w_pg, dch * 128, 128)
        w2, w2B = wpanel(w_pp, dch * 128, 128, kc=2)
        for (a, n) in ttiles(NREAL):
            proj_fm(w, wB, 0, X2b, X2b_B, a, n,
                    lambda p, pB, n=n: S.op("act", lambda e: e.activation(out=SGT[:, 0:n], in_=p[:, 0:n], func=AF.Sigmoid, scale=1.0, bias=C0),
                                            reads=[pB, cst_B], writes=[SGT_B]))
            k = yc["i"] % 2; yc["i"] += 1
            proj_fm(w2, w2B, 0, PTb, PTb_B, a, n,
                    lambda p, pB, n=n, k=k: S.op("dve", lambda e: e.tensor_tensor(out=YO[k][:, 0:n], in0=p[:, 0:n], in1=SGT[:, 0:n], op=ALU.mult),
                                                 reads=[pB, SGT_B], writes=[YO_B[k]]), kc=2)
            S.op("dve", lambda e, k=k, a=a, n=n, dch=dch: e.tensor_tensor(out=YO[k][:, 0:n], in0=YO[k][:, 0:n], in1=R2[:, dch, a:a + n], op=ALU.add),
                 reads=[YO_B[k], R2_B], writes=[YO_B[k]])
            S.dma("sp", yTv[:, dch, a:a + n], YO[k][:, 0:n], reads=[YO_B[k]])
    S.barrier()
    nc._marks = S.marks
    return nc


_CACHE = {}


def kernel(x, p, w_in, w_gla_lr, b_gla_lr, gla_norm_g, b_forget, w_branch_gla, w_branch_fox, w_out,
           ln1_g, ln1_b, w_gate, w_up, conv_w, conv_b, w_down, ln2_g, ln2_b, w_ple_gate, w_ple_proj):
    f = np.float32
    x = np.asarray(x, f); p = np.asarray(p, f)
    B = x.shape[0]

    def pc(v, n):
        return np.ascontiguousarray(np.asarray(v, f).reshape(n, 128).T)

    shared = {
        "ident": np.eye(128, dtype=f),
        "tri": np.ascontiguousarray(np.triu(np.ones((128, 128), f))),
        "reset": np.ascontiguousarray(np.tile((np.arange(NOWN) % 128 != 0).astype(f)[None, :], (128, 1))),
        "w_in": np.ascontiguousarray(np.asarray(w_in, f)[0]),
        "w_gla_lr": np.ascontiguousarray(np.asarray(w_gla_lr, f)[0]),
        "blr": pc(np.asarray(b_gla_lr)[0], 8),
        "gn": pc(np.asarray(gla_norm_g)[0], 4),
        "bfor": np.ascontiguousarray(np.tile(np.asarray(b_forget, f)[0][None, :], (128, 1))),
        "w_branch_gla": np.ascontiguousarray(np.asarray(w_branch_gla, f)[0]),
        "w_branch_fox": np.ascontiguousarray(np.asarray(w_branch_fox, f)[0]),
        "w_out": np.ascontiguousarray(np.asarray(w_out, f)[0]),
        "ln1g": pc(np.asarray(ln1_g)[0], DC), "ln1b": pc(np.asarray(ln1_b)[0], DC),
        "w_gate": np.ascontiguousarray(np.asarray(w_gate, f)[0]),
        "w_up": np.ascontiguousarray(np.asarray(w_up, f)[0]),
        "convw": np.ascontiguousarray(np.asarray(conv_w, f)[0].reshape(3, FC, 128).transpose(2, 1, 0)),
        "convb": pc(np.asarray(conv_b)[0], FC),
        "w_down": np.ascontiguousarray(np.asarray(w_down, f)[0]),
        "ln2g": pc(np.asarray(ln2_g)[0], DC), "ln2b": pc(np.asarray(ln2_b)[0], DC),
        "w_ple_gate": np.ascontiguousarray(np.asarray(w_ple_gate, f)[0]),
        "w_ple_proj": np.ascontiguousarray(np.asarray(w_ple_proj, f)[0]),
    }
    in_maps = []
    for c in range(8):
        b, j = c // 4, c % 4
        g0 = 1024 * j + 1024 - NWIN
        xw = np.zeros((NWIN, D), f)
        lo = max(g0, 0)
        xw[lo - g0:, :] = x[b, lo:1024 * j + 1024, :]
        valid = (np.arange(NWIN) + g0) >= 0
        km = np.where(valid, 0.0, -30000.0).astype(f).reshape(NBLK, 128).T
        m = dict(shared)
        m["xT"] = np.ascontiguousarray(xw.T)
        m["pT"] = np.ascontiguousarray(p[0, b, 1024 * j:1024 * j + 1024, :].T)
        m["kmask"] = np.ascontiguousarray(km)
        m["halo"] = np.full((128, 1), 0.0 if j == 0 else 1.0, f)
        in_maps.append(m)
    if "nc" not in _CACHE:
        _CACHE["nc"] = build_program()
    res = run_bass_kernel_spmd(_CACHE["nc"], in_maps, core_ids=list(range(8)))
    out = np.empty((B, S_LEN, D), f)
    for c in range(8):
        b, j = c // 4, c % 4
        out[b, 1024 * j:1024 * j + 1024, :] = res.results[c]["yT"].T
    return out
```

```python
import numpy as np
import concourse.bass as bass
import concourse.mybir as mybir
from concourse.bass_utils import run_bass_kernel_spmd

F32 = mybir.dt.float32
BF16 = mybir.dt.bfloat16
AF = mybir.ActivationFunctionType
ALU = mybir.AluOpType

D = 2048
S_LEN = 4096
NWIN = 4224
NOWN = 1152
NREAL = 1024
OWN0 = NWIN - NOWN
NBLK = NWIN // 128
DC = 16
DFF = 5632
FC = 44
ALPHA = 2.0 ** 0.25
C_GQ, C_GK, C_GV, C_GR, C_GLR = 0, 1024, 2048, 4096, 6144
C_FQ, C_FK, C_FV, C_FF, C_MA, C_MB = 6160, 7184, 8208, 9232, 9240, 11288


class Buf:
    __slots__ = ("w", "r")

    def __init__(self):
        self.w = None
        self.r = []


class Sync:
    def __init__(self, nc, n_dma_sems=20):
        self.nc = nc
        self.engs = {"pe": nc.tensor, "act": nc.scalar, "dve": nc.vector, "pool": nc.gpsimd, "sp": nc.sync}
        self.semobj = {}
        self.cnt = {}
        for k in self.engs:
            self.semobj[k] = nc.alloc_semaphore("sem_" + k)
            self.cnt[k] = 0
        self.waited = {k: {} for k in self.engs}
        self.npe = 0
        self.marks = []
        self.dq = {}
        for q in ("sp", "pool"):
            keys = []
            for i in range(n_dma_sems):
                key = "d_%s_%d" % (q, i)
                self.semobj[key] = nc.alloc_semaphore(key)
                self.cnt[key] = 0
                keys.append(key)
            self.dq[q] = [keys, 0]

    def _wait(self, eng, deps):
        need = {}
        for tok in deps:
            if tok is None:
                continue
            k, v = tok
            if k == "pe" and eng == "pe":
                continue
            if v > need.get(k, 0):
                need[k] = v
        for k, v in need.items():
            if self.waited[eng].get(k, 0) >= v:
                continue
            self.engs[eng].wait_ge(self.semobj[k], v)
            self.waited[eng][k] = v

    def _deps(self, reads, writes):
        deps = []
        for b in reads:
            deps.append(b.w)
        for b in writes:
            deps.append(b.w)
            deps.extend(b.r)
        return deps

    def op(self, eng, fn, reads=(), writes=(), inc=True):
        self._wait(eng, self._deps(reads, writes))
        inst = fn(self.engs[eng])
        if eng == "pe":
            self.npe += 1
        tok = (eng, self.cnt[eng] + 1)
        if inc:
            inst.then_inc(self.semobj[eng], 1)
            self.cnt[eng] += 1
        for b in reads:
            b.r.append(tok)
        for b in writes:
            b.w = tok
            b.r = []
        return inst

    def dma(self, q, out, in_, reads=(), writes=()):
        keys, idx = self.dq[q]
        key = keys[idx % len(keys)]
        self.dq[q][1] = idx + 1
        deps = self._deps(reads, writes)
        deps.append((key, self.cnt[key]) if self.cnt[key] else None)
        self._wait(q, deps)
        inst = self.engs[q].dma_start(out=out, in_=in_)
        inst.then_inc(self.semobj[key], 16)
        self.cnt[key] += 16
        tok = (key, self.cnt[key])
        for b in reads:
            b.r.append(tok)
        for b in writes:
            b.w = tok
            b.r = []
        return inst

    def barrier(self):
        self.marks.append(self.npe)
        toks = [(k, v) for k, v in self.cnt.items() if v > 0]
        for e in self.engs:
            self._wait(e, [t for t in toks if not (t[0] == e)])


def build_program():
    import os
    KSTOP = int(os.environ.get("KSTOP", "99"))
    nc = bass.Bass("TRN2", target_bir_lowering=False)

    def din(name, shape, dt=F32):
        return nc.dram_tensor(name, list(shape), dt, kind="ExternalInput").ap()

    xT = din("xT", [D, NWIN])
    pT = din("pT", [256, NREAL])
    kmask_d = din("kmask", [128, NBLK])
    halo_d = din("halo", [128, 1])
    ident_d = din("ident", [128, 128])
    tri_d = din("tri", [128, 128])
    reset_d = din("reset", [128, NOWN])
    w_in = din("w_in", [D, 13336])
    w_lr = din("w_gla_lr", [16, 1024])
    blr_d = din("blr", [128, 8])
    gn_d = din("gn", [128, 4])
    bf_d = din("bfor", [128, 8])
    w_bg = din("w_branch_gla", [D, D])
    w_bf = din("w_branch_fox", [1024, D])
    w_out = din("w_out", [D, D])
    ln1g_d = din("ln1g", [128, DC])
    ln1b_d = din("ln1b", [128, DC])
    w_gate = din("w_gate", [D, DFF])
    w_up = din("w_up", [D, DFF])
    convw_d = din("convw", [128, FC, 3])
    convb_d = din("convb", [128, FC])
    w_down = din("w_down", [DFF, D])
    ln2g_d = din("ln2g", [128, DC])
    ln2b_d = din("ln2b", [128, DC])
    w_pg = din("w_ple_gate", [D, D])
    w_pp = din("w_ple_proj", [256, D])
    yT = nc.dram_tensor("yT", [D, NREAL], F32, kind="ExternalOutput").ap()
    x1s = nc.dram_tensor("x1s", [D, NREAL], F32, kind="Internal").ap()

    S = Sync(nc)
    NA_BYTES = 212736
    arena = nc.alloc_sbuf_tensor("arena", [128, NA_BYTES // 4], F32)
    ar = {"off": 0}

    def at(off):
        ar["off"] = off

    def sb(name, shape, dt):
        n = 1
        for s_ in shape[1:]:
            n *= s_
        nbytes = n * (2 if dt == BF16 else 4)
        off = (ar["off"] + 31) // 32 * 32
        assert off + nbytes <= NA_BYTES, (name, off, nbytes)
        ar["off"] = off + nbytes
        v = arena[0:shape[0], off // 4:(off + nbytes) // 4]
        if dt == BF16:
            v = v.bitcast(BF16)
        if len(shape) == 3:
            v = v.rearrange("p (a b) -> p a b", a=shape[1])
        elif len(shape) == 4:
            v = v.rearrange("p (a b c) -> p a b c", a=shape[1], b=shape[2])
        return v
    NPS = 7
    ps_t = [nc.alloc_psum_tensor("ps%d" % i, [128, 512], F32) for i in range(NPS)]
    ps_b = [Buf() for _ in range(NPS)]
    pst = nc.alloc_psum_tensor("pst", [128, 1024], BF16)
    pst_b = [Buf() for _ in range(8)]
    st = {"ps": 0, "psa": 0, "pst": 0, "ev": 0}

    def PS():
        i = st["ps"] % 5
        st["ps"] += 1
        return ps_t[i], ps_b[i]

    def PSA():
        i = 5 + st["psa"] % 2
        st["psa"] += 1
        return ps_t[i], ps_b[i]

    def PST():
        i = st["ps"] % 5
        st["ps"] += 1
        return ps_t[i][:, 0:64].bitcast(BF16), ps_b[i]

    def evac(out, in_, reads, writes):
        st["ev"] += 1
        if st["ev"] % 2:
            S.op("act", lambda e: e.copy(out=out, in_=in_), reads=reads, writes=writes)
        else:
            S.op("dve", lambda e: e.tensor_copy(out=out, in_=in_), reads=reads, writes=writes)

    ident = sb("ident", [128, 128], BF16); ident_B = Buf()
    tri = sb("tri", [128, 128], F32); tri_B = Buf()
    trib = sb("trib", [128, 128], BF16); trib_B = Buf()
    ones = sb("ones", [128, 128], F32); ones_B = Buf()
    reset = sb("reset", [128, NOWN], F32); reset_B = Buf()
    kmask = sb("kmask", [128, NBLK], F32); kmask_B = Buf()
    halo = sb("halo", [128, 1], F32); halo_B = Buf()
    cst = sb("cst", [128, 4], F32); cst_B = Buf()
    nblr = sb("nblr", [128, 8], F32); nblr_B = Buf()
    gn = sb("gn", [128, 4], F32); gn_B = Buf()
    bfor = sb("bfor", [128, 8], F32); bfor_B = Buf()
    ln1g = sb("ln1g", [128, DC], F32); ln1b = sb("ln1b", [128, DC], F32)
    ln2g = sb("ln2g", [128, DC], F32); ln2b = sb("ln2b", [128, DC], F32)
    convw = sb("convw", [128, FC, 3], F32); convb = sb("convb", [128, FC], F32)
    prm_B = Buf()
    wlr = sb("wlr", [32, 1024], BF16); wlr_B = Buf()
    ones33 = sb("ones33", [128, NBLK], F32)

    S.dma("pool", ident[:], ident_d, writes=[ident_B])
    S.dma("pool", trib[:], tri_d, writes=[trib_B])
    S.op("dve", lambda e: e.memset(wlr[:], 0.0), writes=[wlr_B])
    S.dma("pool", wlr[0:16, :], w_lr, writes=[wlr_B])
    S.dma("sp", tri[:], tri_d, writes=[tri_B])
    S.dma("sp", reset[:], reset_d, writes=[reset_B])
    S.dma("sp", kmask[:], kmask_d, writes=[kmask_B])
    S.dma("sp", halo[:], halo_d, writes=[halo_B])
    S.dma("sp", nblr[:], blr_d, writes=[nblr_B])
    S.dma("sp", gn[:], gn_d, writes=[gn_B])
    S.dma("sp", bfor[:], bf_d, writes=[bfor_B])
    for t_, d_ in ((ln1g, ln1g_d), (ln1b, ln1b_d), (ln2g, ln2g_d), (ln2b, ln2b_d), (convw, convw_d), (convb, convb_d)):
        S.dma("sp", t_[:], d_, writes=[prm_B])
    S.op("dve", lambda e: e.memset(ones[:], 1.0), writes=[ones_B])
    S.op("dve", lambda e: e.memset(ones33[:], 1.0), writes=[ones_B])
    S.op("dve", lambda e: e.memset(cst[:, 0:1], 0.0), writes=[cst_B])
    S.op("dve", lambda e: e.memset(cst[:, 1:2], 1.0), writes=[cst_B])
    S.op("dve", lambda e: e.memset(cst[:, 2:3], 1e-5), writes=[cst_B])
    S.op("dve", lambda e: e.memset(cst[:, 3:4], 1e-6), writes=[cst_B])
    S.op("dve", lambda e: e.tensor_scalar(out=nblr[:], in0=nblr[:], scalar1=-1.0, scalar2=None, op0=ALU.mult),
         reads=[nblr_B], writes=[nblr_B])
    S.barrier()
    C0, C1, CEPS_LN, CEPS_RMS = cst[:, 0:1], cst[:, 1:2], cst[:, 2:3], cst[:, 3:4]
    if KSTOP == 1:
        return nc


    CONST_END = ar["off"]
    assert CONST_END <= 12288, CONST_END
    OFF_XT = 12288
    OFF_OFT = OFF_XT + 36864
    OFF_OGT = OFF_OFT + 18432
    OFF_LOCAL = OFF_OGT + 36864
    at(OFF_XT)
    XT = sb("XT", [128, DC, NOWN], BF16)
    XT_B = [Buf() for _ in range(4)]
    OFT = sb("OFT", [128, 8, NOWN], BF16); OFT_B = Buf()
    OGT = sb("OGT", [128, 16, NOWN], BF16); OGT_B = Buf()
    wp = {"bufs": [], "B": [], "i": 0}

    def wp_setup(nbuf, nbytes):
        wp["bufs"] = [sb("WP%d" % i, [128, nbytes // 2], BF16) for i in range(nbuf)]
        wp["B"] = [Buf() for _ in range(nbuf)]
        wp["i"] = 0

    def wpanel(wsrc, c0, ncols, kc=DC, kc0=0):
        i = wp["i"] % len(wp["bufs"])
        wp["i"] += 1
        src = wsrc.rearrange("(kc p) n -> p kc n", p=128)[:, kc0:kc0 + kc, c0:c0 + ncols]
        v = wp["bufs"][i][:, 0:kc * ncols].rearrange("p (k n) -> p k n", k=kc)
        S.dma("pool", v, src, writes=[wp["B"][i]])
        return v, wp["B"][i]

    def load_xt(t0, ntok):
        for g in range(4):
            src = xT.rearrange("(dc p) t -> p dc t", p=128)[:, 4 * g:4 * g + 4, t0:t0 + ntok]
            S.dma("pool", XT[:, 4 * g:4 * g + 4, 0:ntok], src, writes=[XT_B[g]])

    def ttiles(ntok):
        n = 512 if ntok % 512 == 0 else 384
        return [(a, n) for a in range(0, ntok, n)]

    def proj_fm(w, wB, wcol, act, actB, t0, n, cb, kc=DC):
        p, pB = PS()
        for dc in range(kc):
            rb = actB[dc * len(actB) // kc] if isinstance(actB, list) else actB
            S.op("pe", lambda e, dc=dc: e.matmul(p[:, 0:n], lhsT=w[:, dc, wcol:wcol + 128], rhs=act[:, dc, t0:t0 + n],
                                                  start=(dc == 0), stop=(dc == kc - 1)),
                 reads=[wB, rb], writes=[pB], inc=(dc == kc - 1))
        cb(p, pB)

    def proj_tm(w, wB, ncols, act, actB, tb, cb, kc=DC):
        p, pB = PS()
        for dc in range(kc):
            rb = actB[dc * len(actB) // kc] if isinstance(actB, list) else actB
            S.op("pe", lambda e, dc=dc: e.matmul(p[:, 0:ncols], lhsT=act[:, dc, tb * 128:(tb + 1) * 128], rhs=w[:, dc, 0:ncols],
                                                  start=(dc == 0), stop=(dc == kc - 1)),
                 reads=[wB, rb], writes=[pB], inc=(dc == kc - 1))
        cb(p, pB)

    SUPER = [(0, 1024), (1024, 1024), (2048, 1024), (OWN0, NOWN)]
    w_in_v = w_in.rearrange("(kc p) n -> p kc n", p=128)

    at(OFF_OGT)
    KT = sb("KT", [128, 4, NWIN], BF16); KT_B = Buf()
    VA = sb("VA", [128, NBLK, 4, 130], BF16); VA_B = Buf()
    QT = sb("QT", [128, 4, NOWN], BF16); QT_B = Buf()
    LF = sb("LF", [128, NBLK, 8], F32); LF_B = Buf()
    CK = sb("CK", [128, NBLK, 8], F32); CK_B = Buf()
    TOT = sb("TOT", [128, NBLK, 8], F32); TOT_B = Buf()
    INC = sb("INC", [128, NBLK, 8], F32); INC_B = Buf()
    WFF = sb("WFF", [128, DC, 8], BF16); WFF_B = Buf()
    BI = [sb("BI%d" % i, [128, NBLK], F32) for i in range(2)]; BI_B = [Buf(), Buf()]
    PT4 = [sb("PT4%d" % i, [128, 512], BF16) for i in range(4)]; PT4_B = [Buf() for _ in range(4)]
    TMP = [sb("TMP%d" % i, [128, 512], F32) for i in range(2)]; TMP_B = [Buf(), Buf()]
    trineg = sb("trineg", [128, 128], F32); trineg_B = Buf()
    S.op("dve", lambda e: e.tensor_scalar(out=trineg, in0=tri, scalar1=-1.0, scalar2=30000.0, op0=ALU.add, op1=ALU.mult),
         reads=[tri_B], writes=[trineg_B])
    ON = [sb("ON%d" % i, [128, 128], BF16) for i in range(2)]; ON_B = [Buf(), Buf()]
    sm = sb("sm", [128, 8], F32); sm_B = Buf()
    t8 = sb("t8", [128, 8], F32); t8_B = Buf()
    wp_setup(3, 16384)
    S.dma("pool", WFF, w_in_v[:, :, C_FF:C_FF + 8], writes=[WFF_B])
    S.op("dve", lambda e: e.memset(VA[:, :, :, 128:130], 1.0), writes=[VA_B])
    SCL = 128.0 ** -0.5
    cnt = {"pt": 0, "on": 0, "bi": 0, "tm": 0, "cur_bi": 0, "po": None}

    XTF = [XT.rearrange("p a b -> p (a b)")[:, i * DC * 512:(i + 1) * DC * 512].rearrange("p (a b) -> p a b", a=DC) for i in range(2)]
    XTF_B = [[Buf() for _ in range(4)] for _ in range(2)]
    FT = [(512 * k_, 512) for k_ in range(8)] + [(4096, 128)]
    xT_v = xT.rearrange("(dc p) t -> p dc t", p=128)
    for hp in range(2):
        wK, wKB = wpanel(w_in, C_FK + hp * 512, 512)
        wV, wVB = wpanel(w_in, C_FV + hp * 512, 512)
        wQ, wQB = wpanel(w_in, C_FQ + hp * 512, 512)
        for ti, (t0, n) in enumerate(FT):
            b = ti % 2
            X_, XB_ = XTF[b], XTF_B[b]
            for g in range(4):
                S.dma("pool", X_[:, 4 * g:4 * g + 4, 0:n], xT_v[:, 4 * g:4 * g + 4, t0:t0 + n], writes=[XB_[g]])
            for h in range(4):
                proj_fm(wK, wKB, h * 128, X_, XB_, 0, n,
                        lambda p, pB, h=h, n=n, t0=t0: evac(KT[:, h, t0:t0 + n], p[:, 0:n], [pB], [KT_B]))
            for tb in range(n // 128):
                blk = t0 // 128 + tb
                proj_tm(wV, wVB, 512, X_, XB_, tb,
                        lambda p, pB, blk=blk: evac(VA[:, blk, :, 0:128], p[:, 0:512].rearrange("p (h d) -> p h d", h=4), [pB], [VA_B]))
            if hp == 0:
                for tb in range(n // 128):
                    blk = t0 // 128 + tb

                    def ffcb(p, pB, blk=blk):
                        S.op("dve", lambda e: e.tensor_tensor(out=t8, in0=p[:, 0:8], in1=bfor, op=ALU.add),
                             reads=[pB, bfor_B], writes=[t8_B])
                        S.op("act", lambda e: e.activation(out=t8, in_=t8, func=AF.Exp, scale=-1.0, bias=C0),
                             reads=[t8_B, cst_B], writes=[t8_B])
                        S.op("act", lambda e: e.activation(out=LF[:, blk, :], in_=t8, func=AF.Ln, scale=1.0, bias=C1),
                             reads=[t8_B, cst_B], writes=[LF_B])
                    proj_tm(WFF, WFF_B, 8, X_, XB_, tb, ffcb)
            if t0 >= OWN0:
                for h in range(4):
                    proj_fm(wQ, wQB, h * 128, X_, XB_, 0, n,
                            lambda p, pB, h=h, n=n, t0=t0: evac(QT[:, h, t0 - OWN0:t0 - OWN0 + n], p[:, 0:n], [pB], [QT_B]))
        if hp == 0:
            LFf = LF.rearrange("p b h -> p (b h)")
            p1, p1B = PS()
            S.op("pe", lambda e: e.matmul(p1[:, 0:NBLK * 8], lhsT=tri, rhs=LFf, start=True, stop=True),
                 reads=[tri_B, LF_B], writes=[p1B])
            p2, p2B = PS()
            S.op("pe", lambda e: e.matmul(p2[:, 0:NBLK * 8], lhsT=ones, rhs=LFf, start=True, stop=True),
                 reads=[ones_B, LF_B], writes=[p2B])
            S.op("dve", lambda e: e.tensor_copy(out=TOT.rearrange("p b h -> p (b h)"), in_=p2[:, 0:NBLK * 8]),
                 reads=[p2B], writes=[TOT_B])
            for h in range(8):
                S.op("dve", lambda e, h=h: e.tensor_tensor_scan(out=INC[:, :, h], data0=ones33, data1=TOT[:, :, h],
                                                                 initial=0.0, op0=ALU.mult, op1=ALU.add),
                     reads=[TOT_B, ones_B], writes=[INC_B])
            S.op("dve", lambda e: e.tensor_tensor(out=CK, in0=INC, in1=TOT, op=ALU.subtract),
                 reads=[INC_B, TOT_B], writes=[CK_B])
            S.op("dve", lambda e: e.tensor_tensor(out=CK.rearrange("p b h -> p (b h)"), in0=CK.rearrange("p b h -> p (b h)"),
                                                  in1=p1[:, 0:NBLK * 8], op=ALU.add),
                 reads=[CK_B, p1B], writes=[CK_B])
            S.op("dve", lambda e: e.tensor_tensor(out=CK, in0=CK, in1=kmask.unsqueeze(2).broadcast_to([128, NBLK, 8]), op=ALU.add),
                 reads=[CK_B, kmask_B], writes=[CK_B])
        if KSTOP == 3:
            S.barrier()
            return nc
        items = []
        for h in range(4):
            for i in range(NOWN // 128):
                qb = OWN0 // 128 + i
                js = list(range(0, qb + 1, 4))
                for gi, j0 in enumerate(js):
                    items.append({"h": h, "i": i, "qb": qb, "j0": j0, "nj": min(4, qb + 1 - j0),
                                  "first": gi == 0, "last": gi == len(js) - 1})

        def stageA(it):
            p, pB = PS()
            it["p"], it["pB"] = p, pB
            h, i, j0, nj = it["h"], it["i"], it["j0"], it["nj"]
            for jj in range(nj):
                S.op("pe", lambda e, jj=jj: e.matmul(p[:, jj * 128:(jj + 1) * 128], lhsT=KT[:, h, (j0 + jj) * 128:(j0 + jj + 1) * 128],
                                                     rhs=QT[:, h, i * 128:(i + 1) * 128], start=True, stop=True),
                     reads=[KT_B, QT_B], writes=[pB], inc=(jj == nj - 1))

        def stageB(it):
            h, i, j0, nj, qb = it["h"], it["i"], it["j0"], it["nj"], it["qb"]
            hg = hp * 4 + h
            if it["first"]:
                bi = cnt["bi"] % 2; cnt["bi"] += 1
                cnt["cur_bi"] = bi
                S.op("dve", lambda e: e.tensor_scalar(out=BI[bi], in0=CK[:, :, hg], scalar1=INC[:, qb, hg:hg + 1],
                                                      scalar2=None, op0=ALU.subtract),
                     reads=[CK_B, INC_B], writes=[BI_B[bi]])
            bi = cnt["cur_bi"]
            p, pB = it["p"], it["pB"]
            t = cnt["tm"] % 2; cnt["tm"] += 1
            k = cnt["pt"] % 4; cnt["pt"] += 1
            it["k"] = k
            w_ = nj * 128
            S.op("dve", lambda e: e.scalar_tensor_tensor(out=TMP[t][:, 0:w_].rearrange("p (j t) -> p j t", j=nj),
                                                         in0=p[:, 0:w_].rearrange("p (j t) -> p j t", j=nj), scalar=SCL,
                                                         in1=BI[bi][:, j0:j0 + nj].unsqueeze(2).broadcast_to([128, nj, 128]),
                                                         op0=ALU.mult, op1=ALU.add),
                 reads=[pB, BI_B[bi]], writes=[TMP_B[t]])
            if it["last"]:
                S.op("dve", lambda e: e.tensor_tensor(out=TMP[t][:, w_ - 128:w_], in0=TMP[t][:, w_ - 128:w_], in1=trineg, op=ALU.add),
                     reads=[trineg_B], writes=[TMP_B[t]])
            S.op("act", lambda e: e.activation(out=PT4[k][:, 0:w_], in_=TMP[t][:, 0:w_], func=AF.Exp, scale=1.0, bias=C0),
                 reads=[TMP_B[t], cst_B], writes=[PT4_B[k]])

        def stageC(it):
            h, i, j0, nj = it["h"], it["i"], it["j0"], it["nj"]
            hg = hp * 4 + h
            k = it["k"]
            if it["first"]:
                cnt["po"] = PSA()
            po, poB = cnt["po"]
            for jj in range(nj):
                S.op("pe", lambda e, jj=jj: e.matmul(po[:, 0:130], lhsT=PT4[k][:, jj * 128:(jj + 1) * 128], rhs=VA[:, j0 + jj, h, :],
                                                     start=(it["first"] and jj == 0), stop=(it["last"] and jj == nj - 1)),
                     reads=[PT4_B[k], VA_B], writes=[poB], inc=(jj == nj - 1))
            if it["last"]:
                S.op("dve", lambda e: e.tensor_scalar(out=sm[:, 0:1], in0=po[:, 128:129], scalar1=1e-30, scalar2=None, op0=ALU.max),
                     reads=[poB], writes=[sm_B])
                S.op("dve", lambda e: e.reciprocal(out=sm[:, 1:2], in_=sm[:, 0:1]), reads=[sm_B], writes=[sm_B])
                o = cnt["on"] % 2; cnt["on"] += 1
                S.op("dve", lambda e: e.tensor_scalar(out=ON[o], in0=po[:, 0:128], scalar1=sm[:, 1:2], scalar2=None, op0=ALU.mult),
                     reads=[poB, sm_B], writes=[ON_B[o]])
                epi.append((cnt["kk"] + 2, o, hg, i))

        def flush_epi(force=False):
            while epi and (force or epi[0][0] <= cnt["kk"]):
                _, o, hg, i = epi.pop(0)
                pt_, ptB = PST()
                S.op("pe", lambda e: e.transpose(pt_, ON[o], ident), reads=[ON_B[o], ident_B], writes=[ptB])
                evac(OFT[:, hg, i * 128:(i + 1) * 128], pt_, [ptB], [OFT_B])

        epi = []
        n_it = len(items)
        for kk in range(n_it + 3):
            cnt["kk"] = kk
            if kk < n_it:
                stageA(items[kk])
            if 0 <= kk - 1 < n_it:
                stageB(items[kk - 1])
            if 0 <= kk - 3 < n_it:
                stageC(items[kk - 3])
            flush_epi()
        flush_epi(force=True)
    S.barrier()

    if KSTOP == 4:
        return nc
    at(OFF_LOCAL)
    wp_setup(2, 8192)
    Sst = [sb("Sst%d" % h, [128, 2, 512], F32) for h in range(4)]; Sst_B = [[Buf(), Buf()] for _ in range(4)]
    Sbf = sb("Sbf", [128, 2, 512], BF16); Sbf_B = [Buf(), Buf()]
    GLR = sb("GLR", [32, NOWN], BF16); GLR_B = Buf()
    S.op("dve", lambda e: e.memset(GLR, 0.0), writes=[GLR_B])
    WG16 = sb("WG16", [128, DC, 16], BF16); WG16_B = Buf()
    S.dma("pool", WG16, w_in_v[:, :, C_GLR:C_GLR + 16], writes=[WG16_B])
    for h in range(4):
        S.op("dve", lambda e, h=h: e.memset(Sst[h], 0.0), writes=Sst_B[h])
    Lb = sb("Lb", [128, NOWN], F32); Lb_B = Buf()
    Bb = sb("Bb", [128, NOWN], F32); Bb_B = Buf()
    Db = sb("Db", [128, NOWN], F32); Db_B = Buf()
    EN = sb("EN", [128, NOWN], F32); EN_B = Buf()
    EP = sb("EP", [128, NOWN], F32); EP_B = Buf()
    EBL = sb("EBL", [128, 2, 16], F32); EBL_B = Buf()
    KHT = sb("KHT", [128, 2, NOWN], BF16); KHT_B = Buf()
    KNT = sb("KNT", [128, 2, NOWN], BF16); KNT_B = Buf()
    QPT = sb("QPT", [128, 2, NOWN], BF16); QPT_B = Buf()
    KHt = sb("KHt", [128, 9, 256], BF16); KHt_B = Buf()
    Vh = sb("Vh", [128, 9, 512], BF16); Vh_B = Buf()
    AT = sb("AT", [128, 128], BF16); AT_B = Buf()
    SG = sb("SG", [128, 4, NOWN], BF16); SG_B = Buf()
    SQ4 = sb("SQ4", [128, 512], BF16); SQ4_B = Buf()
    TT4 = sb("TT4", [128, 512], F32); TT4_B = Buf()
    onesb = sb("onesb", [128, 128], BF16); onesb_B = Buf()
    S.op("dve", lambda e: e.memset(onesb, 1.0), writes=[onesb_B])
    RR2 = [sb("RR%d" % i, [128, 128], F32) for i in range(2)]; RR2_B = [Buf(), Buf()]
    ATA = sb("ATA", [128, 9, 128], BF16); ATA_B = [Buf() for _ in range(9)]

    for (t0, ntok) in SUPER:
        load_xt(t0, ntok)
        own = (t0 == OWN0)
        nb = ntok // 128
        for (a, n) in ttiles(ntok):
            p, pB = PS()
            for dc in range(DC):
                S.op("pe", lambda e, dc=dc, p=p, a=a, n=n: e.matmul(p[0:16, 0:n], lhsT=WG16[:, dc, :], rhs=XT[:, dc, a:a + n],
                                                                  start=(dc == 0), stop=(dc == DC - 1)),
                     reads=[WG16_B, XT_B[dc // 4]], writes=[pB], inc=(dc == DC - 1))
            evac(GLR[0:16, a:a + n], p[0:16, 0:n], [pB], [GLR_B])
        if KSTOP == 51:
            S.barrier()
            return nc
        if KSTOP == 55 and own:
            S.barrier()
            return nc
        for h in range(4):
            wk, wkB = wpanel(w_in, C_GK + h * 256, 256)
            if own:
                wq, wqB = wpanel(w_in, C_GQ + h * 256, 256)
            for kc in range(2):
                col = h * 256 + kc * 128
                for (a, n) in ttiles(ntok):
                    p, pB = PS()
                    S.op("pe", lambda e, p=p, a=a, n=n, col=col: e.matmul(p[:, 0:n], lhsT=wlr[:, col:col + 128], rhs=GLR[:, a:a + n],
                                                                        start=True, stop=True),
                         reads=[wlr_B, GLR_B], writes=[pB])
                    S.op("act", lambda e, p=p, a=a, n=n, h=h, kc=kc: e.activation(out=Lb[:, a:a + n], in_=p[:, 0:n], func=AF.Exp, scale=-1.0,
                                                                                bias=nblr[:, h * 2 + kc:h * 2 + kc + 1]),
                         reads=[pB, nblr_B], writes=[Lb_B])
                if KSTOP == 511:
                    S.barrier()
                    return nc
                S.op("act", lambda e: e.activation(out=Lb[:, 0:ntok], in_=Lb[:, 0:ntok], func=AF.Ln, scale=1.0, bias=C1),
                     reads=[Lb_B, cst_B], writes=[Lb_B])
                if KSTOP == 512:
                    S.barrier()
                    return nc
                S.op("dve", lambda e: e.tensor_tensor_scan(out=Bb[:, 0:ntok], data0=reset[:, 0:ntok], data1=Lb[:, 0:ntok],
                                                           initial=0.0, op0=ALU.mult, op1=ALU.add),
                     reads=[reset_B, Lb_B], writes=[Bb_B])
                if KSTOP == 513:
                    S.barrier()
                    return nc
                B3 = Bb[:, 0:ntok].rearrange("p (b t) -> p b t", t=128)
                S.op("dve", lambda e: e.tensor_tensor(out=Db[:, 0:ntok].rearrange("p (b t) -> p b t", t=128), in0=B3,
                                                      in1=B3[:, :, 127:128].broadcast_to([128, nb, 128]), op=ALU.subtract),
                     reads=[Bb_B], writes=[Db_B])
                S.op("act", lambda e: e.activation(out=Db[:, 0:ntok], in_=Db[:, 0:ntok], func=AF.Exp, scale=1.0 / 16, bias=C0),
                     reads=[Db_B, cst_B], writes=[Db_B])
                S.op("act", lambda e, kc=kc: e.activation(out=EBL[:, kc, 0:nb], in_=B3[:, :, 127], func=AF.Exp, scale=-1.0 / 16, bias=C0),
                     reads=[Bb_B, cst_B], writes=[EBL_B])
                if KSTOP == 514:
                    S.barrier()
                    return nc
                if own:
                    S.op("act", lambda e: e.activation(out=EN[:, 0:ntok], in_=Bb[:, 0:ntok], func=AF.Exp, scale=1.0 / 16, bias=C0),
                         reads=[Bb_B, cst_B], writes=[EN_B])
                    S.op("act", lambda e: e.activation(out=EP[:, 0:ntok], in_=Bb[:, 0:ntok], func=AF.Exp, scale=-1.0 / 16, bias=C0),
                         reads=[Bb_B, cst_B], writes=[EP_B])
                for (a, n) in ttiles(ntok):
                    def kcb(p, pB, a=a, n=n, kc=kc):
                        S.op("dve", lambda e: e.tensor_tensor(out=KHT[:, kc, a:a + n], in0=p[:, 0:n], in1=Db[:, a:a + n], op=ALU.mult),
                             reads=[pB, Db_B], writes=[KHT_B])
                        if own:
                            S.op("dve", lambda e: e.tensor_tensor(out=KNT[:, kc, a:a + n], in0=p[:, 0:n], in1=EN[:, a:a + n], op=ALU.mult),
                                 reads=[pB, EN_B], writes=[KNT_B])
                    proj_fm(wk, wkB, kc * 128, XT, XT_B, a, n, kcb)
                    if own:
                        def qcb(p, pB, a=a, n=n, kc=kc):
                            S.op("dve", lambda e: e.scalar_tensor_tensor(out=QPT[:, kc, a:a + n], in0=p[:, 0:n], scalar=256.0 ** -0.5,
                                                                         in1=EP[:, a:a + n], op0=ALU.mult, op1=ALU.mult),
                                 reads=[pB, EP_B], writes=[QPT_B])
                        proj_fm(wq, wqB, kc * 128, XT, XT_B, a, n, qcb)
                if KSTOP == 515:
                    S.barrier()
                    return nc
                for tb in range(nb):
                    pt_, ptB = PST()
                    S.op("pe", lambda e, pt_=pt_, tb=tb, kc=kc: e.transpose(pt_, KHT[:, kc, tb * 128:(tb + 1) * 128], ident),
                         reads=[KHT_B, ident_B], writes=[ptB])
                    evac(KHt[:, tb, kc * 128:(kc + 1) * 128], pt_, [ptB], [KHt_B])
            if KSTOP == 52:
                S.barrier()
                return nc
            for half in range(2):
                wv, wvB = wpanel(w_in, C_GV + h * 512 + half * 256, 256)
                for tb in range(nb):
                    proj_tm(wv, wvB, 256, XT, XT_B, tb,
                            lambda p, pB, tb=tb, half=half: evac(Vh[:, tb, half * 256:(half + 1) * 256], p[:, 0:256], [pB], [Vh_B]))
            if own:
                for half in range(2):
                    wr, wrB = wpanel(w_in, C_GR + h * 512 + half * 256, 256)
                    for v2 in range(2):
                        vc = half * 2 + v2
                        for (a, n) in ttiles(ntok):
                            proj_fm(wr, wrB, v2 * 128, XT, XT_B, a, n,
                                    lambda p, pB, vc=vc, a=a, n=n: (
                                        S.op("act", lambda e: e.activation(out=SG[:, vc, a:a + n], in_=p[:, 0:n], func=AF.Silu, scale=1.0, bias=C0),
                                             reads=[pB, cst_B], writes=[SG_B]),
                                        S.op("dve", lambda e: e.tensor_scalar(out=SG[:, vc, a:a + n], in0=SG[:, vc, a:a + n], scalar1=gn[:, vc:vc + 1],
                                                                              scalar2=None, op0=ALU.mult),
                                             reads=[gn_B], writes=[SG_B])))
                S.op("act", lambda e, h=h: e.copy(out=Sbf, in_=Sst[h]), reads=Sst_B[h], writes=Sbf_B)
            if KSTOP == 53:
                S.barrier()
                return nc
            if not own:
                for tb in range(nb):
                    for kc in range(2):
                        pu, puB = PS()
                        S.op("pe", lambda e, kc=kc, tb=tb, pu=pu: e.matmul(pu[:, 0:512], lhsT=KHt[:, tb, kc * 128:(kc + 1) * 128], rhs=Vh[:, tb, :],
                                                                         start=True, stop=True),
                             reads=[KHt_B, Vh_B], writes=[puB])
                        S.op("dve", lambda e, kc=kc, tb=tb, pu=pu, h=h: e.scalar_tensor_tensor(out=Sst[h][:, kc, :], in0=Sst[h][:, kc, :],
                                                                                             scalar=EBL[:, kc, tb:tb + 1], in1=pu[:, 0:512],
                                                                                             op0=ALU.mult, op1=ALU.add),
                             reads=[EBL_B, puB], writes=[Sst_B[h][kc]])
            else:
                for tb in range(nb):
                    pa, paB = PS()
                    for kc in range(2):
                        S.op("pe", lambda e, kc=kc, tb=tb, pa=pa: e.matmul(pa[:, 0:128], lhsT=KNT[:, kc, tb * 128:(tb + 1) * 128],
                                                                         rhs=QPT[:, kc, tb * 128:(tb + 1) * 128], start=(kc == 0), stop=(kc == 1)),
                             reads=[KNT_B, QPT_B], writes=[paB], inc=(kc == 1))
                    S.op("dve", lambda e, pa=pa, tb=tb: e.tensor_tensor(out=ATA[:, tb, :], in0=pa[:, 0:128], in1=tri, op=ALU.mult),
                         reads=[paB, tri_B], writes=[ATA_B[tb]])

                def part_b(tb, po, poB):
                    r = tb % 2
                    S.op("dve", lambda e: e.reciprocal(out=RR2[r], in_=RR2[r]), reads=[RR2_B[r]], writes=[RR2_B[r]])
                    S.op("dve", lambda e: e.tensor_tensor(out=TT4.rearrange("p (v t) -> p v t", v=4),
                                                          in0=po[:, 0:512].rearrange("p (v t) -> p v t", v=4),
                                                          in1=RR2[r].unsqueeze(1).broadcast_to([128, 4, 128]), op=ALU.mult),
                         reads=[poB, RR2_B[r]], writes=[TT4_B])
                    S.op("dve", lambda e: e.tensor_tensor(out=OGT[:, h * 4:(h + 1) * 4, tb * 128:(tb + 1) * 128],
                                                          in0=TT4.rearrange("p (v t) -> p v t", v=4),
                                                          in1=SG[:, :, tb * 128:(tb + 1) * 128], op=ALU.mult),
                         reads=[TT4_B, SG_B], writes=[OGT_B])

                pend = None
                for tb in range(nb):
                    last = (tb == nb - 1)
                    if not last:
                        for kc in range(2):
                            pu, puB = PS()
                            S.op("pe", lambda e, kc=kc, tb=tb, pu=pu: e.matmul(pu[:, 0:512], lhsT=KHt[:, tb, kc * 128:(kc + 1) * 128], rhs=Vh[:, tb, :],
                                                                             start=True, stop=True),
                                 reads=[KHt_B, Vh_B], writes=[puB])
                            S.op("dve", lambda e, kc=kc, tb=tb, pu=pu: e.scalar_tensor_tensor(out=Sst[h][:, kc, :], in0=Sst[h][:, kc, :],
                                                                                            scalar=EBL[:, kc, tb:tb + 1], in1=pu[:, 0:512],
                                                                                            op0=ALU.mult, op1=ALU.add),
                                 reads=[EBL_B, puB], writes=[Sst_B[h][kc]])
                    if pend is not None:
                        part_b(*pend)
                    po, poB = PSA()
                    for vc in range(4):
                        for kc in range(2):
                            S.op("pe", lambda e, kc=kc, vc=vc, tb=tb, po=po: e.matmul(po[:, vc * 128:(vc + 1) * 128],
                                                                                    lhsT=Sbf[:, kc, vc * 128:(vc + 1) * 128],
                                                                                    rhs=QPT[:, kc, tb * 128:(tb + 1) * 128],
                                                                                    start=(kc == 0), stop=False),
                                 reads=[Sbf_B[kc], QPT_B], writes=[poB], inc=False)
                        S.op("pe", lambda e, vc=vc, tb=tb, po=po: e.matmul(po[:, vc * 128:(vc + 1) * 128], lhsT=Vh[:, tb, vc * 128:(vc + 1) * 128],
                                                                         rhs=ATA[:, tb, :], start=False, stop=True),
                             reads=[Vh_B, ATA_B[tb]], writes=[poB], inc=(vc == 3))
                    if not last:
                        for kc in range(2):
                            S.op("act", lambda e, kc=kc: e.copy(out=Sbf[:, kc, :], in_=Sst[h][:, kc, :]),
                                 reads=[Sst_B[h][kc]], writes=[Sbf_B[kc]])
                    r = tb % 2
                    pr, prB = PS()
                    S.op("act", lambda e, po=po: e.activation(out=SQ4, in_=po[:, 0:512], func=AF.Square, scale=1.0, bias=C0),
                         reads=[poB, cst_B], writes=[SQ4_B])
                    for vc in range(4):
                        S.op("pe", lambda e, vc=vc, pr=pr: e.matmul(pr[:, 0:128], lhsT=onesb, rhs=SQ4[:, vc * 128:(vc + 1) * 128],
                                                                    start=(vc == 0), stop=(vc == 3)),
                             reads=[onesb_B, SQ4_B], writes=[prB], inc=(vc == 3))
                    S.op("act", lambda e, pr=pr, r=r: e.activation(out=RR2[r], in_=pr[:, 0:128], func=AF.Sqrt, scale=1.0 / 512, bias=CEPS_RMS),
                         reads=[prB, cst_B], writes=[RR2_B[r]])
                    pend = (tb, po, poB)
                part_b(*pend)
            if KSTOP == 54:
                S.barrier()
                return nc
    S.barrier()

    if KSTOP == 5:
        return nc
    OFF_MG = OFF_LOCAL
    OFF_X1B = OFF_MG + 36864
    OFF_WP3 = OFF_X1B + 36864
    at(OFF_MG)
    MG = sb("MG", [128, DC, NOWN], BF16); MG_B = Buf()
    X1b = sb("X1b", [128, DC, NOWN], BF16); X1b_B = Buf()
    wp_setup(6, 4096)
    assert ar["off"] <= NA_BYTES
    at(OFF_X1B)
    SA = sb("SA", [128, 512], F32); SA_B = Buf()
    SB_ = sb("SB_", [128, 512], F32); SB_B = Buf()
    T1 = sb("T1", [128, 384], F32); T1_B = Buf()
    T2 = sb("T2", [128, 384], F32); T2_B = Buf()
    for dch in range(DC):
        c0 = dch * 128
        wa, waB = wpanel(w_in, C_MA + c0, 128)
        wb, wbB = wpanel(w_in, C_MB + c0, 128)
        wg_, wgB = wpanel(w_bg, c0, 128)
        wf_, wfB = wpanel(w_bf, c0, 128, kc=8)
        for (a, n) in ttiles(NOWN):
            proj_fm(wa, waB, 0, XT, XT_B, a, n,
                    lambda p, pB, n=n: S.op("act", lambda e: e.activation(out=SA[:, 0:n], in_=p[:, 0:n], func=AF.Sigmoid, scale=1.0, bias=C0),
                                            reads=[pB, cst_B], writes=[SA_B]))
            proj_fm(wb, wbB, 0, XT, XT_B, a, n,
                    lambda p, pB, n=n: S.op("act", lambda e: e.activation(out=SB_[:, 0:n], in_=p[:, 0:n], func=AF.Sigmoid, scale=1.0, bias=C0),
                                            reads=[pB, cst_B], writes=[SB_B]))
            proj_fm(wg_, wgB, 0, OGT, OGT_B, a, n,
                    lambda p, pB, n=n: S.op("dve", lambda e: e.tensor_tensor(out=T1[:, 0:n], in0=p[:, 0:n], in1=SA[:, 0:n], op=ALU.mult),
                                            reads=[pB, SA_B], writes=[T1_B]))
            proj_fm(wf_, wfB, 0, OFT, OFT_B, a, n,
                    lambda p, pB, n=n: S.op("dve", lambda e: e.tensor_tensor(out=T2[:, 0:n], in0=p[:, 0:n], in1=SB_[:, 0:n], op=ALU.mult),
                                            reads=[pB, SB_B], writes=[T2_B]), kc=8)
            S.op("dve", lambda e, a=a, n=n, dch=dch: e.tensor_tensor(out=MG[:, dch, a:a + n], in0=T1[:, 0:n], in1=T2[:, 0:n], op=ALU.add),
                 reads=[T1_B, T2_B], writes=[MG_B])
    S.barrier()

    if KSTOP == 6:
        return nc
    at(OFF_XT)
    R1 = sb("R1", [128, DC, NOWN], F32); R1_B = Buf()
    OFF_T = ar["off"]
    XF = [sb("XF%d" % i, [128, 512], F32) for i in range(2)]; XF_B = [Buf(), Buf()]
    MU = sb("MU", [128, 512], F32); MU_B = Buf()
    VR = sb("VR", [128, 512], F32); VR_B = Buf()
    SQ2 = [sb("SQ2%d" % i, [128, 512], F32) for i in range(2)]; SQ2_B = [Buf(), Buf()]
    TT2 = sb("TT2", [128, 512], F32); TT2_B = Buf()
    assert ar["off"] <= OFF_MG, ar["off"]
    xfc = {"i": 0}
    xTv = xT.rearrange("(dc p) t -> p dc t", p=128)
    for dch in range(DC):
        w, wB = wpanel(w_out, dch * 128, 128)
        for (a, n) in ttiles(NOWN):
            i = xfc["i"] % 2; xfc["i"] += 1
            S.dma("sp", XF[i][:, 0:n], xTv[:, dch, OWN0 + a:OWN0 + a + n], writes=[XF_B[i]])
            proj_fm(w, wB, 0, MG, MG_B, a, n,
                    lambda p, pB, i=i, a=a, n=n, dch=dch: S.op("dve", lambda e: e.scalar_tensor_tensor(out=R1[:, dch, a:a + n], in0=XF[i][:, 0:n], scalar=ALPHA,
                                                                                                  in1=p[:, 0:n], op0=ALU.mult, op1=ALU.add),
                                                               reads=[pB, XF_B[i]], writes=[R1_B]))

    def layer_norm(R, RB, ntok, gam, bet, outb, outbB):
        rbs = []
        for (a, n) in ttiles(ntok):
            p1, p1B = PS()
            for dc in range(DC):
                S.op("pe", lambda e, dc=dc: e.matmul(p1[:, 0:n], lhsT=ones, rhs=R[:, dc, a:a + n], start=(dc == 0), stop=(dc == DC - 1)),
                     reads=[ones_B, RB], writes=[p1B], inc=(dc == DC - 1))
            p2, p2B = PS()
            for dc in range(DC):
                i = dc % 2
                S.op("act", lambda e, dc=dc, i=i: e.activation(out=SQ2[i][:, 0:n], in_=R[:, dc, a:a + n], func=AF.Square, scale=1.0, bias=C0),
                     reads=[RB, cst_B], writes=[SQ2_B[i]])
                S.op("pe", lambda e, dc=dc, i=i: e.matmul(p2[:, 0:n], lhsT=ones, rhs=SQ2[i][:, 0:n], start=(dc == 0), stop=(dc == DC - 1)),
                     reads=[ones_B, SQ2_B[i]], writes=[p2B])
            S.op("dve", lambda e: e.tensor_scalar(out=MU[:, 0:n], in0=p1[:, 0:n], scalar1=1.0 / D, scalar2=None, op0=ALU.mult),
                 reads=[p1B], writes=[MU_B])
            S.op("dve", lambda e: e.tensor_tensor(out=TT2[:, 0:n], in0=MU[:, 0:n], in1=MU[:, 0:n], op=ALU.mult),
                 reads=[MU_B], writes=[TT2_B])
            S.op("dve", lambda e: e.scalar_tensor_tensor(out=VR[:, 0:n], in0=p2[:, 0:n], scalar=1.0 / D, in1=TT2[:, 0:n],
                                                         op0=ALU.mult, op1=ALU.subtract),
                 reads=[p2B, TT2_B], writes=[VR_B])
            S.op("act", lambda e: e.activation(out=VR[:, 0:n], in_=VR[:, 0:n], func=AF.Sqrt, scale=1.0, bias=CEPS_LN),
                 reads=[VR_B, cst_B], writes=[VR_B])
            S.op("dve", lambda e: e.reciprocal(out=VR[:, 0:n], in_=VR[:, 0:n]), reads=[VR_B], writes=[VR_B])
            for dc in range(DC):
                rb = Buf()
                rbs.append(rb)
                S.op("dve", lambda e, dc=dc: e.tensor_tensor(out=R[:, dc, a:a + n], in0=R[:, dc, a:a + n], in1=MU[:, 0:n], op=ALU.subtract),
                     reads=[RB, MU_B, VR_B], writes=[rb])
                S.op("dve", lambda e, dc=dc: e.tensor_tensor(out=R[:, dc, a:a + n], in0=R[:, dc, a:a + n], in1=VR[:, 0:n], op=ALU.mult),
                     reads=[VR_B], writes=[rb])
                S.op("act", lambda e, dc=dc: e.activation(out=outb[:, dc, a:a + n], in_=R[:, dc, a:a + n], func=AF.Identity,
                                                          scale=gam[:, dc:dc + 1], bias=bet[:, dc:dc + 1]),
                     reads=[rb, prm_B], writes=[outbB])
                S.op("act", lambda e, dc=dc: e.activation(out=R[:, dc, a:a + n], in_=R[:, dc, a:a + n], func=AF.Identity,
                                                          scale=gam[:, dc:dc + 1], bias=bet[:, dc:dc + 1]),
                     reads=[prm_B], writes=[rb])
        S.op("act", lambda e: e.copy(out=VR[:, 0:1], in_=VR[:, 0:1]), reads=rbs, writes=[RB, VR_B])

    layer_norm(R1, R1_B, NOWN, ln1g, ln1b, X1b, X1b_B)
    x1sv = x1s.rearrange("(dc p) t -> p dc t", p=128)
    x1s_B = Buf()
    for g in range(4):
        S.dma("sp", x1sv[:, 4 * g:4 * g + 4, :], R1[:, 4 * g:4 * g + 4, 128:NOWN], reads=[R1_B], writes=[x1s_B])
    S.barrier()

    if KSTOP == 7:
        return nc
    at(OFF_WP3)
    wp_setup(4, 6144)
    at(OFF_XT)
    R2 = sb("R2", [128, DC, NREAL], F32); R2_B = Buf()
    GE = sb("GE", [128, NREAL], F32); GE_B = Buf()
    XF = [sb("XFb%d" % i, [128, 512], F32) for i in range(2)]; XF_B = [Buf(), Buf()]
    HB = sb("HB", [128, FC // 2, NREAL], BF16); HB_B = Buf()
    GG = sb("GG", [128, NOWN], F32); GG_B = Buf()
    CV = sb("CV", [128, NREAL], F32); CV_B = Buf()
    assert ar["off"] <= OFF_X1B, ar["off"]
    for grp in range(2):
        for fl in range(FC // 2):
            fc = grp * (FC // 2) + fl
            wg_, wgB = wpanel(w_gate, fc * 128, 128)
            wu_, wuB = wpanel(w_up, fc * 128, 128)
            for (a, n) in ttiles(NOWN):
                proj_fm(wg_, wgB, 0, X1b, X1b_B, a, n, lambda p, pB, a=a, n=n: evac(GG[:, a:a + n], p[:, 0:n], [pB], [GG_B]))
            S.op("dve", lambda e: e.tensor_scalar(out=GG[:, 126:128], in0=GG[:, 126:128], scalar1=halo[:, 0:1], scalar2=None, op0=ALU.mult),
                 reads=[GG_B, halo_B], writes=[GG_B])
            S.op("dve", lambda e, fc=fc: e.tensor_scalar(out=CV, in0=GG[:, 126:126 + NREAL], scalar1=convw[:, fc, 0:1], scalar2=convb[:, fc:fc + 1],
                                                         op0=ALU.mult, op1=ALU.add),
                 reads=[GG_B, prm_B], writes=[CV_B])
            S.op("dve", lambda e, fc=fc: e.scalar_tensor_tensor(out=CV, in0=GG[:, 127:127 + NREAL], scalar=convw[:, fc, 1:2], in1=CV,
                                                                op0=ALU.mult, op1=ALU.add),
                 reads=[GG_B, prm_B, CV_B], writes=[CV_B])
            S.op("dve", lambda e, fc=fc: e.scalar_tensor_tensor(out=CV, in0=GG[:, 128:128 + NREAL], scalar=convw[:, fc, 2:3], in1=CV,
                                                                op0=ALU.mult, op1=ALU.add),
                 reads=[GG_B, prm_B, CV_B], writes=[CV_B])
            S.op("act", lambda e: e.activation(out=GE, in_=CV, func=AF.Gelu, scale=1.0, bias=C0), reads=[CV_B, cst_B], writes=[GE_B])
            for (a, n) in ttiles(NREAL):
                proj_fm(wu_, wuB, 0, X1b[:, :, 128:NOWN], X1b_B, a, n,
                        lambda p, pB, a=a, n=n, fl=fl: S.op("dve", lambda e: e.tensor_tensor(out=HB[:, fl, a:a + n], in0=p[:, 0:n], in1=GE[:, a:a + n], op=ALU.mult),
                                                            reads=[pB, GE_B], writes=[HB_B]))
        for dch in range(DC):
            w, wB = wpanel(w_down, dch * 128, 128, kc=FC // 2, kc0=grp * (FC // 2))
            for (a, n) in ttiles(NREAL):
                if grp == 0:
                    k = xfc["i"] % 2; xfc["i"] += 1
                    S.dma("sp", XF[k][:, 0:n], x1sv[:, dch, a:a + n], reads=[x1s_B], writes=[XF_B[k]])
                    proj_fm(w, wB, 0, HB, HB_B, a, n,
                            lambda p, pB, k=k, a=a, n=n, dch=dch: S.op("dve", lambda e: e.scalar_tensor_tensor(out=R2[:, dch, a:a + n], in0=XF[k][:, 0:n], scalar=ALPHA,
                                                                                                          in1=p[:, 0:n], op0=ALU.mult, op1=ALU.add),
                                                                       reads=[pB, XF_B[k]], writes=[R2_B]), kc=FC // 2)
                else:
                    proj_fm(w, wB, 0, HB, HB_B, a, n,
                            lambda p, pB, a=a, n=n, dch=dch: S.op("dve", lambda e: e.tensor_tensor(out=R2[:, dch, a:a + n], in0=R2[:, dch, a:a + n],
                                                                                                 in1=p[:, 0:n], op=ALU.add),
                                                                  reads=[pB, R2_B], writes=[R2_B]), kc=FC // 2)
    S.barrier()
    if KSTOP == 8:
        return nc
    at(86016)
    MU = sb("MUb", [128, 512], F32); MU_B = Buf()
    VR = sb("VRb", [128, 512], F32); VR_B = Buf()
    SQ2 = [sb("SQ2b%d" % i, [128, 512], F32) for i in range(2)]; SQ2_B = [Buf(), Buf()]
    TT2 = sb("TT2b", [128, 512], F32); TT2_B = Buf()
    X2b = X1b; X2b_B = X1b_B
    layer_norm(R2, R2_B, NREAL, ln2g, ln2b, X2b, X2b_B)

    PTb = sb("PTb", [128, 2, NREAL], BF16); PTb_B = Buf()
    S.dma("pool", PTb, pT.rearrange("(kc p) t -> p kc t", p=128), writes=[PTb_B])
    YO = [sb("YO%d" % i, [128, 512], F32) for i in range(2)]; YO_B = [Buf(), Buf()]
    SGT = sb("SGT", [128, 512], F32); SGT_B = Buf()
    yTv = yT.rearrange("(dc p) t -> p dc t", p=128)
    yc = {"i": 0}
    for dch in range(DC):
        w, wB = wpanel(w_pg, dch * 128, 128)
        w2, w2B = wpanel(w_pp, dch * 128, 128, kc=2)
        for (a, n) in ttiles(NREAL):
            proj_fm(w, wB, 0, X2b, X2b_B, a, n,
                    lambda p, pB, n=n: S.op("act", lambda e: e.activation(out=SGT[:, 0:n], in_=p[:, 0:n], func=AF.Sigmoid, scale=1.0, bias=C0),
                                            reads=[pB, cst_B], writes=[SGT_B]))
            k = yc["i"] % 2; yc["i"] += 1
            proj_fm(w2, w2B, 0, PTb, PTb_B, a, n,
                    lambda p, pB, n=n, k=k: S.op("dve", lambda e: e.tensor_tensor(out=YO[k][:, 0:n], in0=p[:, 0:n], in1=SGT[:, 0:n], op=ALU.mult),
                                                 reads=[pB, SGT_B], writes=[YO_B[k]]), kc=2)
            S.op("dve", lambda e, k=k, a=a, n=n, dch=dch: e.tensor_tensor(out=YO[k][:, 0:n], in0=YO[k][:, 0:n], in1=R2[:, dch, a:a + n], op=ALU.add),
                 reads=[YO_B[k], R2_B], writes=[YO_B[k]])
            S.dma("sp", yTv[:, dch, a:a + n], YO[k][:, 0:n], reads=[YO_B[k]])
    S.barrier()
    nc._marks = S.marks
    return nc


_CACHE = {}


def kernel(x, p, w_in, w_gla_lr, b_gla_lr, gla_norm_g, b_forget, w_branch_gla, w_branch_fox, w_out,
           ln1_g, ln1_b, w_gate, w_up, conv_w, conv_b, w_down, ln2_g, ln2_b, w_ple_gate, w_ple_proj):
    f = np.float32
    x = np.asarray(x, f); p = np.asarray(p, f)
    B = x.shape[0]

    def pc(v, n):
        return np.ascontiguousarray(np.asarray(v, f).reshape(n, 128).T)

    shared = {
        "ident": np.eye(128, dtype=f),
        "tri": np.ascontiguousarray(np.triu(np.ones((128, 128), f))),
        "reset": np.ascontiguousarray(np.tile((np.arange(NOWN) % 128 != 0).astype(f)[None, :], (128, 1))),
        "w_in": np.ascontiguousarray(np.asarray(w_in, f)[0]),
        "w_gla_lr": np.ascontiguousarray(np.asarray(w_gla_lr, f)[0]),
        "blr": pc(np.asarray(b_gla_lr)[0], 8),
        "gn": pc(np.asarray(gla_norm_g)[0], 4),
        "bfor": np.ascontiguousarray(np.tile(np.asarray(b_forget, f)[0][None, :], (128, 1))),
        "w_branch_gla": np.ascontiguousarray(np.asarray(w_branch_gla, f)[0]),
        "w_branch_fox": np.ascontiguousarray(np.asarray(w_branch_fox, f)[0]),
        "w_out": np.ascontiguousarray(np.asarray(w_out, f)[0]),
        "ln1g": pc(np.asarray(ln1_g)[0], DC), "ln1b": pc(np.asarray(ln1_b)[0], DC),
        "w_gate": np.ascontiguousarray(np.asarray(w_gate, f)[0]),
        "w_up": np.ascontiguousarray(np.asarray(w_up, f)[0]),
        "convw": np.ascontiguousarray(np.asarray(conv_w, f)[0].reshape(3, FC, 128).transpose(2, 1, 0)),
        "convb": pc(np.asarray(conv_b)[0], FC),
        "w_down": np.ascontiguousarray(np.asarray(w_down, f)[0]),
        "ln2g": pc(np.asarray(ln2_g)[0], DC), "ln2b": pc(np.asarray(ln2_b)[0], DC),
        "w_ple_gate": np.ascontiguousarray(np.asarray(w_ple_gate, f)[0]),
        "w_ple_proj": np.ascontiguousarray(np.asarray(w_ple_proj, f)[0]),
    }
    in_maps = []
    for c in range(8):
        b, j = c // 4, c % 4
        g0 = 1024 * j + 1024 - NWIN
        xw = np.zeros((NWIN, D), f)
        lo = max(g0, 0)
        xw[lo - g0:, :] = x[b, lo:1024 * j + 1024, :]
        valid = (np.arange(NWIN) + g0) >= 0
        km = np.where(valid, 0.0, -30000.0).astype(f).reshape(NBLK, 128).T
        m = dict(shared)
        m["xT"] = np.ascontiguousarray(xw.T)
        m["pT"] = np.ascontiguousarray(p[0, b, 1024 * j:1024 * j + 1024, :].T)
        m["kmask"] = np.ascontiguousarray(km)
        m["halo"] = np.full((128, 1), 0.0 if j == 0 else 1.0, f)
        in_maps.append(m)
    if "nc" not in _CACHE:
        _CACHE["nc"] = build_program()
    res = run_bass_kernel_spmd(_CACHE["nc"], in_maps, core_ids=list(range(8)))
    out = np.empty((B, S_LEN, D), f)
    for c in range(8):
        b, j = c // 4, c % 4
        out[b, 1024 * j:1024 * j + 1024, :] = res.results[c]["yT"].T
    return out
```

```python
import numpy as np
import concourse.bass as bass
import concourse.mybir as mybir
from concourse.bass_utils import run_bass_kernel_spmd

F32 = mybir.dt.float32
BF16 = mybir.dt.bfloat16
AF = mybir.ActivationFunctionType
ALU = mybir.AluOpType

D = 2048
S_LEN = 4096
NWIN = 4224
NOWN = 1152
NREAL = 1024
OWN0 = NWIN - NOWN
NBLK = NWIN // 128
DC = 16
DFF = 5632
FC = 44
ALPHA = 2.0 ** 0.25
C_GQ, C_GK, C_GV, C_GR, C_GLR = 0, 1024, 2048, 4096, 6144
C_FQ, C_FK, C_FV, C_FF, C_MA, C_MB = 6160, 7184, 8208, 9232, 9240, 11288


class Buf:
    __slots__ = ("w", "r")

    def __init__(self):
        self.w = None
        self.r = []


class Sync:
    def __init__(self, nc, n_dma_sems=20):
        self.nc = nc
        self.engs = {"pe": nc.tensor, "act": nc.scalar, "dve": nc.vector, "pool": nc.gpsimd, "sp": nc.sync}
        self.semobj = {}
        self.cnt = {}
        for k in self.engs:
            self.semobj[k] = nc.alloc_semaphore("sem_" + k)
            self.cnt[k] = 0
        self.waited = {k: {} for k in self.engs}
        self.npe = 0
        self.marks = []
        self.dq = {}
        for q in ("sp", "pool"):
            keys = []
            for i in range(n_dma_sems):
                key = "d_%s_%d" % (q, i)
                self.semobj[key] = nc.alloc_semaphore(key)
                self.cnt[key] = 0
                keys.append(key)
            self.dq[q] = [keys, 0]

    def _wait(self, eng, deps):
        need = {}
        for tok in deps:
            if tok is None:
                continue
            k, v = tok
            if k == "pe" and eng == "pe":
                continue
            if v > need.get(k, 0):
                need[k] = v
        for k, v in need.items():
            if self.waited[eng].get(k, 0) >= v:
                continue
            self.engs[eng].wait_ge(self.semobj[k], v)
            self.waited[eng][k] = v

    def _deps(self, reads, writes):
        deps = []
        for b in reads:
            deps.append(b.w)
        for b in writes:
            deps.append(b.w)
            deps.extend(b.r)
        return deps

    def op(self, eng, fn, reads=(), writes=(), inc=True):
        self._wait(eng, self._deps(reads, writes))
        inst = fn(self.engs[eng])
        if eng == "pe":
            self.npe += 1
        tok = (eng, self.cnt[eng] + 1)
        if inc:
            inst.then_inc(self.semobj[eng], 1)
            self.cnt[eng] += 1
        for b in reads:
            b.r.append(tok)
        for b in writes:
            b.w = tok
            b.r = []
        return inst

    def dma(self, q, out, in_, reads=(), writes=()):
        keys, idx = self.dq[q]
        key = keys[idx % len(keys)]
        self.dq[q][1] = idx + 1
        deps = self._deps(reads, writes)
        deps.append((key, self.cnt[key]) if self.cnt[key] else None)
        self._wait(q, deps)
        inst = self.engs[q].dma_start(out=out, in_=in_)
        inst.then_inc(self.semobj[key], 16)
        self.cnt[key] += 16
        tok = (key, self.cnt[key])
        for b in reads:
            b.r.append(tok)
        for b in writes:
            b.w = tok
            b.r = []
        return inst

    def barrier(self):
        self.marks.append(self.npe)
        toks = [(k, v) for k, v in self.cnt.items() if v > 0]
        for e in self.engs:
            self._wait(e, [t for t in toks if not (t[0] == e)])


def build_program():
    import os
    KSTOP = int(os.environ.get("KSTOP", "99"))
    nc = bass.Bass("TRN2", target_bir_lowering=False)

    def din(name, shape, dt=F32):
        return nc.dram_tensor(name, list(shape), dt, kind="ExternalInput").ap()

    xT = din("xT", [D, NWIN])
    pT = din("pT", [256, NREAL])
    kmask_d = din("kmask", [128, NBLK])
    halo_d = din("halo", [128, 1])
    ident_d = din("ident", [128, 128])
    tri_d = din("tri", [128, 128])
    reset_d = din("reset", [128, NOWN])
    w_in = din("w_in", [D, 13336])
    w_lr = din("w_gla_lr", [16, 1024])
    blr_d = din("blr", [128, 8])
    gn_d = din("gn", [128, 4])
    bf_d = din("bfor", [128, 8])
    w_bg = din("w_branch_gla", [D, D])
    w_bf = din("w_branch_fox", [1024, D])
    w_out = din("w_out", [D, D])
    ln1g_d = din("ln1g", [128, DC])
    ln1b_d = din("ln1b", [128, DC])
    w_gate = din("w_gate", [D, DFF])
    w_up = din("w_up", [D, DFF])
    convw_d = din("convw", [128, FC, 3])
    convb_d = din("convb", [128, FC])
    w_down = din("w_down", [DFF, D])
    ln2g_d = din("ln2g", [128, DC])
    ln2b_d = din("ln2b", [128, DC])
    w_pg = din("w_ple_gate", [D, D])
    w_pp = din("w_ple_proj", [256, D])
    yT = nc.dram_tensor("yT", [D, NREAL], F32, kind="ExternalOutput").ap()
    x1s = nc.dram_tensor("x1s", [D, NREAL], F32, kind="Internal").ap()

    S = Sync(nc)
    NA_BYTES = 212736
    arena = nc.alloc_sbuf_tensor("arena", [128, NA_BYTES // 4], F32)
    ar = {"off": 0}

    def at(off):
        ar["off"] = off

    def sb(name, shape, dt):
        n = 1
        for s_ in shape[1:]:
            n *= s_
        nbytes = n * (2 if dt == BF16 else 4)
        off = (ar["off"] + 31) // 32 * 32
        assert off + nbytes <= NA_BYTES, (name, off, nbytes)
        ar["off"] = off + nbytes
        v = arena[0:shape[0], off // 4:(off + nbytes) // 4]
        if dt == BF16:
            v = v.bitcast(BF16)
        if len(shape) == 3:
            v = v.rearrange("p (a b) -> p a b", a=shape[1])
        elif len(shape) == 4:
            v = v.rearrange("p (a b c) -> p a b c", a=shape[1], b=shape[2])
        return v
    NPS = 7
    ps_t = [nc.alloc_psum_tensor("ps%d" % i, [128, 512], F32) for i in range(NPS)]
    ps_b = [Buf() for _ in range(NPS)]
    pst = nc.alloc_psum_tensor("pst", [128, 1024], BF16)
    pst_b = [Buf() for _ in range(8)]
    st = {"ps": 0, "psa": 0, "pst": 0, "ev": 0}

    def PS():
        i = st["ps"] % 5
        st["ps"] += 1
        return ps_t[i], ps_b[i]

    def PSA():
        i = 5 + st["psa"] % 2
        st["psa"] += 1
        return ps_t[i], ps_b[i]

    def PST():
        i = st["ps"] % 5
        st["ps"] += 1
        return ps_t[i][:, 0:64].bitcast(BF16), ps_b[i]

    def evac(out, in_, reads, writes):
        st["ev"] += 1
        if st["ev"] % 2:
            S.op("act", lambda e: e.copy(out=out, in_=in_), reads=reads, writes=writes)
        else:
            S.op("dve", lambda e: e.tensor_copy(out=out, in_=in_), reads=reads, writes=writes)

    ident = sb("ident", [128, 128], BF16); ident_B = Buf()
    tri = sb("tri", [128, 128], F32); tri_B = Buf()
    trib = sb("trib", [128, 128], BF16); trib_B = Buf()
    ones = sb("ones", [128, 128], F32); ones_B = Buf()
    reset = sb("reset", [128, NOWN], F32); reset_B = Buf()
    kmask = sb("kmask", [128, NBLK], F32); kmask_B = Buf()
    halo = sb("halo", [128, 1], F32); halo_B = Buf()
    cst = sb("cst", [128, 4], F32); cst_B = Buf()
    nblr = sb("nblr", [128, 8], F32); nblr_B = Buf()
    gn = sb("gn", [128, 4], F32); gn_B = Buf()
    bfor = sb("bfor", [128, 8], F32); bfor_B = Buf()
    ln1g = sb("ln1g", [128, DC], F32); ln1b = sb("ln1b", [128, DC], F32)
    ln2g = sb("ln2g", [128, DC], F32); ln2b = sb("ln2b", [128, DC], F32)
    convw = sb("convw", [128, FC, 3], F32); convb = sb("convb", [128, FC], F32)
    prm_B = Buf()
    wlr = sb("wlr", [32, 1024], BF16); wlr_B = Buf()
    ones33 = sb("ones33", [128, NBLK], F32)

    S.dma("pool", ident[:], ident_d, writes=[ident_B])
    S.dma("pool", trib[:], tri_d, writes=[trib_B])
    S.op("dve", lambda e: e.memset(wlr[:], 0.0), writes=[wlr_B])
    S.dma("pool", wlr[0:16, :], w_lr, writes=[wlr_B])
    S.dma("sp", tri[:], tri_d, writes=[tri_B])
    S.dma("sp", reset[:], reset_d, writes=[reset_B])
    S.dma("sp", kmask[:], kmask_d, writes=[kmask_B])
    S.dma("sp", halo[:], halo_d, writes=[halo_B])
    S.dma("sp", nblr[:], blr_d, writes=[nblr_B])
    S.dma("sp", gn[:], gn_d, writes=[gn_B])
    S.dma("sp", bfor[:], bf_d, writes=[bfor_B])
    for t_, d_ in ((ln1g, ln1g_d), (ln1b, ln1b_d), (ln2g, ln2g_d), (ln2b, ln2b_d), (convw, convw_d), (convb, convb_d)):
        S.dma("sp", t_[:], d_, writes=[prm_B])
    S.op("dve", lambda e: e.memset(ones[:], 1.0), writes=[ones_B])
    S.op("dve", lambda e: e.memset(ones33[:], 1.0), writes=[ones_B])
    S.op("dve", lambda e: e.memset(cst[:, 0:1], 0.0), writes=[cst_B])
    S.op("dve", lambda e: e.memset(cst[:, 1:2], 1.0), writes=[cst_B])
    S.op("dve", lambda e: e.memset(cst[:, 2:3], 1e-5), writes=[cst_B])
    S.op("dve", lambda e: e.memset(cst[:, 3:4], 1e-6), writes=[cst_B])
    S.op("dve", lambda e: e.tensor_scalar(out=nblr[:], in0=nblr[:], scalar1=-1.0, scalar2=None, op0=ALU.mult),
         reads=[nblr_B], writes=[nblr_B])
    S.barrier()
    C0, C1, CEPS_LN, CEPS_RMS = cst[:, 0:1], cst[:, 1:2], cst[:, 2:3], cst[:, 3:4]
    if KSTOP == 1:
        return nc


    CONST_END = ar["off"]
    assert CONST_END <= 12288, CONST_END
    OFF_XT = 12288
    OFF_OFT = OFF_XT + 36864
    OFF_OGT = OFF_OFT + 18432
    OFF_LOCAL = OFF_OGT + 36864
    at(OFF_XT)
    XT = sb("XT", [128, DC, NOWN], BF16)
    XT_B = [Buf() for _ in range(4)]
    OFT = sb("OFT", [128, 8, NOWN], BF16); OFT_B = Buf()
    OGT = sb("OGT", [128, 16, NOWN], BF16); OGT_B = Buf()
    wp = {"bufs": [], "B": [], "i": 0}

    def wp_setup(nbuf, nbytes):
        wp["bufs"] = [sb("WP%d" % i, [128, nbytes // 2], BF16) for i in range(nbuf)]
        wp["B"] = [Buf() for _ in range(nbuf)]
        wp["i"] = 0

    def wpanel(wsrc, c0, ncols, kc=DC, kc0=0):
        i = wp["i"] % len(wp["bufs"])
        wp["i"] += 1
        src = wsrc.rearrange("(kc p) n -> p kc n", p=128)[:, kc0:kc0 + kc, c0:c0 + ncols]
        v = wp["bufs"][i][:, 0:kc * ncols].rearrange("p (k n) -> p k n", k=kc)
        S.dma("pool", v, src, writes=[wp["B"][i]])
        return v, wp["B"][i]

    def load_xt(t0, ntok):
        for g in range(4):
            src = xT.rearrange("(dc p) t -> p dc t", p=128)[:, 4 * g:4 * g + 4, t0:t0 + ntok]
            S.dma("pool", XT[:, 4 * g:4 * g + 4, 0:ntok], src, writes=[XT_B[g]])

    def ttiles(ntok):
        n = 512 if ntok % 512 == 0 else 384
        return [(a, n) for a in range(0, ntok, n)]

    def proj_fm(w, wB, wcol, act, actB, t0, n, cb, kc=DC):
        p, pB = PS()
        for dc in range(kc):
            rb = actB[dc * len(actB) // kc] if isinstance(actB, list) else actB
            S.op("pe", lambda e, dc=dc: e.matmul(p[:, 0:n], lhsT=w[:, dc, wcol:wcol + 128], rhs=act[:, dc, t0:t0 + n],
                                                  start=(dc == 0), stop=(dc == kc - 1)),
                 reads=[wB, rb], writes=[pB], inc=(dc == kc - 1))
        cb(p, pB)

    def proj_tm(w, wB, ncols, act, actB, tb, cb, kc=DC):
        p, pB = PS()
        for dc in range(kc):
            rb = actB[dc * len(actB) // kc] if isinstance(actB, list) else actB
            S.op("pe", lambda e, dc=dc: e.matmul(p[:, 0:ncols], lhsT=act[:, dc, tb * 128:(tb + 1) * 128], rhs=w[:, dc, 0:ncols],
                                                  start=(dc == 0), stop=(dc == kc - 1)),
                 reads=[wB, rb], writes=[pB], inc=(dc == kc - 1))
        cb(p, pB)

    SUPER = [(0, 1024), (1024, 1024), (2048, 1024), (OWN0, NOWN)]
    w_in_v = w_in.rearrange("(kc p) n -> p kc n", p=128)

    at(OFF_OGT)
    KT = sb("KT", [128, 4, NWIN], BF16); KT_B = Buf()
    VA = sb("VA", [128, NBLK, 4, 130], BF16); VA_B = Buf()
    QT = sb("QT", [128, 4, NOWN], BF16); QT_B = Buf()
    LF = sb("LF", [128, NBLK, 8], F32); LF_B = Buf()
    CK = sb("CK", [128, NBLK, 8], F32); CK_B = Buf()
    TOT = sb("TOT", [128, NBLK, 8], F32); TOT_B = Buf()
    INC = sb("INC", [128, NBLK, 8], F32); INC_B = Buf()
    WFF = sb("WFF", [128, DC, 8], BF16); WFF_B = Buf()
    BI = [sb("BI%d" % i, [128, NBLK], F32) for i in range(2)]; BI_B = [Buf(), Buf()]
    PT4 = [sb("PT4%d" % i, [128, 512], BF16) for i in range(4)]; PT4_B = [Buf() for _ in range(4)]
    TMP = [sb("TMP%d" % i, [128, 512], F32) for i in range(2)]; TMP_B = [Buf(), Buf()]
    trineg = sb("trineg", [128, 128], F32); trineg_B = Buf()
    S.op("dve", lambda e: e.tensor_scalar(out=trineg, in0=tri, scalar1=-1.0, scalar2=30000.0, op0=ALU.add, op1=ALU.mult),
         reads=[tri_B], writes=[trineg_B])
    ON = [sb("ON%d" % i, [128, 128], BF16) for i in range(2)]; ON_B = [Buf(), Buf()]
    sm = sb("sm", [128, 8], F32); sm_B = Buf()
    t8 = sb("t8", [128, 8], F32); t8_B = Buf()
    wp_setup(3, 16384)
    S.dma("pool", WFF, w_in_v[:, :, C_FF:C_FF + 8], writes=[WFF_B])
    S.op("dve", lambda e: e.memset(VA[:, :, :, 128:130], 1.0), writes=[VA_B])
    SCL = 128.0 ** -0.5
    cnt = {"pt": 0, "on": 0, "bi": 0, "tm": 0, "cur_bi": 0, "po": None}

    XTF = [XT.rearrange("p a b -> p (a b)")[:, i * DC * 512:(i + 1) * DC * 512].rearrange("p (a b) -> p a b", a=DC) for i in range(2)]
    XTF_B = [[Buf() for _ in range(4)] for _ in range(2)]
    FT = [(512 * k_, 512) for k_ in range(8)] + [(4096, 128)]
    xT_v = xT.rearrange("(dc p) t -> p dc t", p=128)
    for hp in range(2):
        wK, wKB = wpanel(w_in, C_FK + hp * 512, 512)
        wV, wVB = wpanel(w_in, C_FV + hp * 512, 512)
        wQ, wQB = wpanel(w_in, C_FQ + hp * 512, 512)
        for ti, (t0, n) in enumerate(FT):
            b = ti % 2
            X_, XB_ = XTF[b], XTF_B[b]
            for g in range(4):
                S.dma("pool", X_[:, 4 * g:4 * g + 4, 0:n], xT_v[:, 4 * g:4 * g + 4, t0:t0 + n], writes=[XB_[g]])
            for h in range(4):
                proj_fm(wK, wKB, h * 128, X_, XB_, 0, n,
                        lambda p, pB, h=h, n=n, t0=t0: evac(KT[:, h, t0:t0 + n], p[:, 0:n], [pB], [KT_B]))
            for tb in range(n // 128):
                blk = t0 // 128 + tb
                proj_tm(wV, wVB, 512, X_, XB_, tb,
                        lambda p, pB, blk=blk: evac(VA[:, blk, :, 0:128], p[:, 0:512].rearrange("p (h d) -> p h d", h=4), [pB], [VA_B]))
            if hp == 0:
                for tb in range(n // 128):
                    blk = t0 // 128 + tb

                    def ffcb(p, pB, blk=blk):
                        S.op("dve", lambda e: e.tensor_tensor(out=t8, in0=p[:, 0:8], in1=bfor, op=ALU.add),
                             reads=[pB, bfor_B], writes=[t8_B])
                        S.op("act", lambda e: e.activation(out=t8, in_=t8, func=AF.Exp, scale=-1.0, bias=C0),
                             reads=[t8_B, cst_B], writes=[t8_B])
                        S.op("act", lambda e: e.activation(out=LF[:, blk, :], in_=t8, func=AF.Ln, scale=1.0, bias=C1),
                             reads=[t8_B, cst_B], writes=[LF_B])
                    proj_tm(WFF, WFF_B, 8, X_, XB_, tb, ffcb)
            if t0 >= OWN0:
                for h in range(4):
                    proj_fm(wQ, wQB, h * 128, X_, XB_, 0, n,
                            lambda p, pB, h=h, n=n, t0=t0: evac(QT[:, h, t0 - OWN0:t0 - OWN0 + n], p[:, 0:n], [pB], [QT_B]))
        if hp == 0:
            LFf = LF.rearrange("p b h -> p (b h)")
            p1, p1B = PS()
            S.op("pe", lambda e: e.matmul(p1[:, 0:NBLK * 8], lhsT=tri, rhs=LFf, start=True, stop=True),
                 reads=[tri_B, LF_B], writes=[p1B])
            p2, p2B = PS()
            S.op("pe", lambda e: e.matmul(p2[:, 0:NBLK * 8], lhsT=ones, rhs=LFf, start=True, stop=True),
                 reads=[ones_B, LF_B], writes=[p2B])
            S.op("dve", lambda e: e.tensor_copy(out=TOT.rearrange("p b h -> p (b h)"), in_=p2[:, 0:NBLK * 8]),
                 reads=[p2B], writes=[TOT_B])
            for h in range(8):
                S.op("dve", lambda e, h=h: e.tensor_tensor_scan(out=INC[:, :, h], data0=ones33, data1=TOT[:, :, h],
                                                                 initial=0.0, op0=ALU.mult, op1=ALU.add),
                     reads=[TOT_B, ones_B], writes=[INC_B])
            S.op("dve", lambda e: e.tensor_tensor(out=CK, in0=INC, in1=TOT, op=ALU.subtract),
                 reads=[INC_B, TOT_B], writes=[CK_B])
            S.op("dve", lambda e: e.tensor_tensor(out=CK.rearrange("p b h -> p (b h)"), in0=CK.rearrange("p b h -> p (b h)"),
                                                  in1=p1[:, 0:NBLK * 8], op=ALU.add),
                 reads=[CK_B, p1B], writes=[CK_B])
            S.op("dve", lambda e: e.tensor_tensor(out=CK, in0=CK, in1=kmask.unsqueeze(2).broadcast_to([128, NBLK, 8]), op=ALU.add),
                 reads=[CK_B, kmask_B], writes=[CK_B])
        if KSTOP == 3:
            S.barrier()
            return nc
        items = []
        for h in range(4):
            for i in range(NOWN // 128):
                qb = OWN0 // 128 + i
                js = list(range(0, qb + 1, 4))
                for gi, j0 in enumerate(js):
                    items.append({"h": h, "i": i, "qb": qb, "j0": j0, "nj": min(4, qb + 1 - j0),
                                  "first": gi == 0, "last": gi == len(js) - 1})

        def stageA(it):
            p, pB = PS()
            it["p"], it["pB"] = p, pB
            h, i, j0, nj = it["h"], it["i"], it["j0"], it["nj"]
            for jj in range(nj):
                S.op("pe", lambda e, jj=jj: e.matmul(p[:, jj * 128:(jj + 1) * 128], lhsT=KT[:, h, (j0 + jj) * 128:(j0 + jj + 1) * 128],
                                                     rhs=QT[:, h, i * 128:(i + 1) * 128], start=True, stop=True),
                     reads=[KT_B, QT_B], writes=[pB], inc=(jj == nj - 1))

        def stageB(it):
            h, i, j0, nj, qb = it["h"], it["i"], it["j0"], it["nj"], it["qb"]
            hg = hp * 4 + h
            if it["first"]:
                bi = cnt["bi"] % 2; cnt["bi"] += 1
                cnt["cur_bi"] = bi
                S.op("dve", lambda e: e.tensor_scalar(out=BI[bi], in0=CK[:, :, hg], scalar1=INC[:, qb, hg:hg + 1],
                                                      scalar2=None, op0=ALU.subtract),
                     reads=[CK_B, INC_B], writes=[BI_B[bi]])
            bi = cnt["cur_bi"]
            p, pB = it["p"], it["pB"]
            t = cnt["tm"] % 2; cnt["tm"] += 1
            k = cnt["pt"] % 4; cnt["pt"] += 1
            it["k"] = k
            w_ = nj * 128
            S.op("dve", lambda e: e.scalar_tensor_tensor(out=TMP[t][:, 0:w_].rearrange("p (j t) -> p j t", j=nj),
                                                         in0=p[:, 0:w_].rearrange("p (j t) -> p j t", j=nj), scalar=SCL,
                                                         in1=BI[bi][:, j0:j0 + nj].unsqueeze(2).broadcast_to([128, nj, 128]),
                                                         op0=ALU.mult, op1=ALU.add),
                 reads=[pB, BI_B[bi]], writes=[TMP_B[t]])
            if it["last"]:
                S.op("dve", lambda e: e.tensor_tensor(out=TMP[t][:, w_ - 128:w_], in0=TMP[t][:, w_ - 128:w_], in1=trineg, op=ALU.add),
                     reads=[trineg_B], writes=[TMP_B[t]])
            S.op("act", lambda e: e.activation(out=PT4[k][:, 0:w_], in_=TMP[t][:, 0:w_], func=AF.Exp, scale=1.0, bias=C0),
                 reads=[TMP_B[t], cst_B], writes=[PT4_B[k]])

        def stageC(it):
            h, i, j0, nj = it["h"], it["i"], it["j0"], it["nj"]
            hg = hp * 4 + h
            k = it["k"]
            if it["first"]:
                cnt["po"] = PSA()
            po, poB = cnt["po"]
            for jj in range(nj):
                S.op("pe", lambda e, jj=jj: e.matmul(po[:, 0:130], lhsT=PT4[k][:, jj * 128:(jj + 1) * 128], rhs=VA[:, j0 + jj, h, :],
                                                     start=(it["first"] and jj == 0), stop=(it["last"] and jj == nj - 1)),
                     reads=[PT4_B[k], VA_B], writes=[poB], inc=(jj == nj - 1))
            if it["last"]:
                epd.append((cnt["kk"] + 1, po, poB, hg, i))

        def flush_epd(force=False):
            while epd and (force or epd[0][0] <= cnt["kk"]):
                _, po, poB, hg, i = epd.pop(0)
                S.op("dve", lambda e: e.tensor_scalar(out=sm[:, 0:1], in0=po[:, 128:129], scalar1=1e-30, scalar2=None, op0=ALU.max),
                     reads=[poB], writes=[sm_B])
                S.op("dve", lambda e: e.reciprocal(out=sm[:, 1:2], in_=sm[:, 0:1]), reads=[sm_B], writes=[sm_B])
                o = cnt["on"] % 2; cnt["on"] += 1
                S.op("dve", lambda e: e.tensor_scalar(out=ON[o], in0=po[:, 0:128], scalar1=sm[:, 1:2], scalar2=None, op0=ALU.mult),
                     reads=[poB, sm_B], writes=[ON_B[o]])
                epi.append((cnt["kk"] + 2, o, hg, i))

        def flush_epi(force=False):
            while epi and (force or epi[0][0] <= cnt["kk"]):
                _, o, hg, i = epi.pop(0)
                pt_, ptB = PST()
                S.op("pe", lambda e: e.transpose(pt_, ON[o], ident), reads=[ON_B[o], ident_B], writes=[ptB])
                evac(OFT[:, hg, i * 128:(i + 1) * 128], pt_, [ptB], [OFT_B])

        epi = []
        epd = []
        n_it = len(items)
        for kk in range(n_it + 3):
            cnt["kk"] = kk
            if kk < n_it:
                stageA(items[kk])
            if 0 <= kk - 1 < n_it:
                stageB(items[kk - 1])
            if 0 <= kk - 3 < n_it:
                stageC(items[kk - 3])
            flush_epd()
            flush_epi()
        flush_epd(force=True)
        flush_epi(force=True)
    S.barrier()

    if KSTOP == 4:
        return nc
    at(OFF_LOCAL)
    wp_setup(2, 8192)
    Sst = [sb("Sst%d" % h, [128, 2, 512], F32) for h in range(4)]; Sst_B = [[Buf(), Buf()] for _ in range(4)]
    Sbf = sb("Sbf", [128, 2, 512], BF16); Sbf_B = [Buf(), Buf()]
    GLR = sb("GLR", [32, NOWN], BF16); GLR_B = Buf()
    S.op("dve", lambda e: e.memset(GLR, 0.0), writes=[GLR_B])
    WG16 = sb("WG16", [128, DC, 16], BF16); WG16_B = Buf()
    S.dma("pool", WG16, w_in_v[:, :, C_GLR:C_GLR + 16], writes=[WG16_B])
    for h in range(4):
        S.op("dve", lambda e, h=h: e.memset(Sst[h], 0.0), writes=Sst_B[h])
    Lb = sb("Lb", [128, NOWN], F32); Lb_B = Buf()
    Bb = sb("Bb", [128, NOWN], F32); Bb_B = Buf()
    Db = sb("Db", [128, NOWN], F32); Db_B = Buf()
    EN = sb("EN", [128, NOWN], F32); EN_B = Buf()
    EP = sb("EP", [128, NOWN], F32); EP_B = Buf()
    EBL = sb("EBL", [128, 2, 16], F32); EBL_B = Buf()
    KHT = sb("KHT", [128, 2, NOWN], BF16); KHT_B = Buf()
    KNT = sb("KNT", [128, 2, NOWN], BF16); KNT_B = Buf()
    QPT = sb("QPT", [128, 2, NOWN], BF16); QPT_B = Buf()
    KHt = sb("KHt", [128, 9, 256], BF16); KHt_B = Buf()
    Vh = sb("Vh", [128, 9, 512], BF16); Vh_B = Buf()
    AT = sb("AT", [128, 128], BF16); AT_B = Buf()
    SG = sb("SG", [128, 4, NOWN], BF16); SG_B = Buf()
    SQ4 = sb("SQ4", [128, 512], BF16); SQ4_B = Buf()
    TT4 = sb("TT4", [128, 512], F32); TT4_B = Buf()
    onesb = sb("onesb", [128, 128], BF16); onesb_B = Buf()
    S.op("dve", lambda e: e.memset(onesb, 1.0), writes=[onesb_B])
    RR2 = [sb("RR%d" % i, [128, 128], F32) for i in range(2)]; RR2_B = [Buf(), Buf()]
    ATA = sb("ATA", [128, 9, 128], BF16); ATA_B = [Buf() for _ in range(9)]

    for (t0, ntok) in SUPER:
        load_xt(t0, ntok)
        own = (t0 == OWN0)
        nb = ntok // 128
        for (a, n) in ttiles(ntok):
            p, pB = PS()
            for dc in range(DC):
                S.op("pe", lambda e, dc=dc, p=p, a=a, n=n: e.matmul(p[0:16, 0:n], lhsT=WG16[:, dc, :], rhs=XT[:, dc, a:a + n],
                                                                  start=(dc == 0), stop=(dc == DC - 1)),
                     reads=[WG16_B, XT_B[dc // 4]], writes=[pB], inc=(dc == DC - 1))
            evac(GLR[0:16, a:a + n], p[0:16, 0:n], [pB], [GLR_B])
        if KSTOP == 51:
            S.barrier()
            return nc
        if KSTOP == 55 and own:
            S.barrier()
            return nc
        for h in range(4):
            wk, wkB = wpanel(w_in, C_GK + h * 256, 256)
            if own:
                wq, wqB = wpanel(w_in, C_GQ + h * 256, 256)
            for kc in range(2):
                col = h * 256 + kc * 128
                for (a, n) in ttiles(ntok):
                    p, pB = PS()
                    S.op("pe", lambda e, p=p, a=a, n=n, col=col: e.matmul(p[:, 0:n], lhsT=wlr[:, col:col + 128], rhs=GLR[:, a:a + n],
                                                                        start=True, stop=True),
                         reads=[wlr_B, GLR_B], writes=[pB])
                    S.op("act", lambda e, p=p, a=a, n=n, h=h, kc=kc: e.activation(out=Lb[:, a:a + n], in_=p[:, 0:n], func=AF.Exp, scale=-1.0,
                                                                                bias=nblr[:, h * 2 + kc:h * 2 + kc + 1]),
                         reads=[pB, nblr_B], writes=[Lb_B])
                if KSTOP == 511:
                    S.barrier()
                    return nc
                S.op("act", lambda e: e.activation(out=Lb[:, 0:ntok], in_=Lb[:, 0:ntok], func=AF.Ln, scale=1.0, bias=C1),
                     reads=[Lb_B, cst_B], writes=[Lb_B])
                if KSTOP == 512:
                    S.barrier()
                    return nc
                S.op("dve", lambda e: e.tensor_tensor_scan(out=Bb[:, 0:ntok], data0=reset[:, 0:ntok], data1=Lb[:, 0:ntok],
                                                           initial=0.0, op0=ALU.mult, op1=ALU.add),
                     reads=[reset_B, Lb_B], writes=[Bb_B])
                if KSTOP == 513:
                    S.barrier()
                    return nc
                B3 = Bb[:, 0:ntok].rearrange("p (b t) -> p b t", t=128)
                S.op("dve", lambda e: e.tensor_tensor(out=Db[:, 0:ntok].rearrange("p (b t) -> p b t", t=128), in0=B3,
                                                      in1=B3[:, :, 127:128].broadcast_to([128, nb, 128]), op=ALU.subtract),
                     reads=[Bb_B], writes=[Db_B])
                S.op("act", lambda e: e.activation(out=Db[:, 0:ntok], in_=Db[:, 0:ntok], func=AF.Exp, scale=1.0 / 16, bias=C0),
                     reads=[Db_B, cst_B], writes=[Db_B])
                S.op("act", lambda e, kc=kc: e.activation(out=EBL[:, kc, 0:nb], in_=B3[:, :, 127], func=AF.Exp, scale=-1.0 / 16, bias=C0),
                     reads=[Bb_B, cst_B], writes=[EBL_B])
                if KSTOP == 514:
                    S.barrier()
                    return nc
                if own:
                    S.op("act", lambda e: e.activation(out=EN[:, 0:ntok], in_=Bb[:, 0:ntok], func=AF.Exp, scale=1.0 / 16, bias=C0),
                         reads=[Bb_B, cst_B], writes=[EN_B])
                    S.op("act", lambda e: e.activation(out=EP[:, 0:ntok], in_=Bb[:, 0:ntok], func=AF.Exp, scale=-1.0 / 16, bias=C0),
                         reads=[Bb_B, cst_B], writes=[EP_B])
                for (a, n) in ttiles(ntok):
                    def kcb(p, pB, a=a, n=n, kc=kc):
                        S.op("dve", lambda e: e.tensor_tensor(out=KHT[:, kc, a:a + n], in0=p[:, 0:n], in1=Db[:, a:a + n], op=ALU.mult),
                             reads=[pB, Db_B], writes=[KHT_B])
                        if own:
                            S.op("dve", lambda e: e.tensor_tensor(out=KNT[:, kc, a:a + n], in0=p[:, 0:n], in1=EN[:, a:a + n], op=ALU.mult),
                                 reads=[pB, EN_B], writes=[KNT_B])
                    proj_fm(wk, wkB, kc * 128, XT, XT_B, a, n, kcb)
                    if own:
                        def qcb(p, pB, a=a, n=n, kc=kc):
                            S.op("dve", lambda e: e.scalar_tensor_tensor(out=QPT[:, kc, a:a + n], in0=p[:, 0:n], scalar=256.0 ** -0.5,
                                                                         in1=EP[:, a:a + n], op0=ALU.mult, op1=ALU.mult),
                                 reads=[pB, EP_B], writes=[QPT_B])
                        proj_fm(wq, wqB, kc * 128, XT, XT_B, a, n, qcb)
                if KSTOP == 515:
                    S.barrier()
                    return nc
                for tb in range(nb):
                    pt_, ptB = PST()
                    S.op("pe", lambda e, pt_=pt_, tb=tb, kc=kc: e.transpose(pt_, KHT[:, kc, tb * 128:(tb + 1) * 128], ident),
                         reads=[KHT_B, ident_B], writes=[ptB])
                    evac(KHt[:, tb, kc * 128:(kc + 1) * 128], pt_, [ptB], [KHt_B])
            if KSTOP == 52:
                S.barrier()
                return nc
            for half in range(2):
                wv, wvB = wpanel(w_in, C_GV + h * 512 + half * 256, 256)
                for tb in range(nb):
                    proj_tm(wv, wvB, 256, XT, XT_B, tb,
                            lambda p, pB, tb=tb, half=half: evac(Vh[:, tb, half * 256:(half + 1) * 256], p[:, 0:256], [pB], [Vh_B]))
            if own:
                for half in range(2):
                    wr, wrB = wpanel(w_in, C_GR + h * 512 + half * 256, 256)
                    for v2 in range(2):
                        vc = half * 2 + v2
                        for (a, n) in ttiles(ntok):
                            proj_fm(wr, wrB, v2 * 128, XT, XT_B, a, n,
                                    lambda p, pB, vc=vc, a=a, n=n: (
                                        S.op("act", lambda e: e.activation(out=SG[:, vc, a:a + n], in_=p[:, 0:n], func=AF.Silu, scale=1.0, bias=C0),
                                             reads=[pB, cst_B], writes=[SG_B]),
                                        S.op("dve", lambda e: e.tensor_scalar(out=SG[:, vc, a:a + n], in0=SG[:, vc, a:a + n], scalar1=gn[:, vc:vc + 1],
                                                                              scalar2=None, op0=ALU.mult),
                                             reads=[gn_B], writes=[SG_B])))
                S.op("act", lambda e, h=h: e.copy(out=Sbf, in_=Sst[h]), reads=Sst_B[h], writes=Sbf_B)
            if KSTOP == 53:
                S.barrier()
                return nc
            if not own:
                for tb in range(nb):
                    for kc in range(2):
                        pu, puB = PS()
                        S.op("pe", lambda e, kc=kc, tb=tb, pu=pu: e.matmul(pu[:, 0:512], lhsT=KHt[:, tb, kc * 128:(kc + 1) * 128], rhs=Vh[:, tb, :],
                                                                         start=True, stop=True),
                             reads=[KHt_B, Vh_B], writes=[puB])
                        S.op("dve", lambda e, kc=kc, tb=tb, pu=pu, h=h: e.scalar_tensor_tensor(out=Sst[h][:, kc, :], in0=Sst[h][:, kc, :],
                                                                                             scalar=EBL[:, kc, tb:tb + 1], in1=pu[:, 0:512],
                                                                                             op0=ALU.mult, op1=ALU.add),
                             reads=[EBL_B, puB], writes=[Sst_B[h][kc]])
            else:
                for tb in range(nb):
                    pa, paB = PS()
                    for kc in range(2):
                        S.op("pe", lambda e, kc=kc, tb=tb, pa=pa: e.matmul(pa[:, 0:128], lhsT=KNT[:, kc, tb * 128:(tb + 1) * 128],
                                                                         rhs=QPT[:, kc, tb * 128:(tb + 1) * 128], start=(kc == 0), stop=(kc == 1)),
                             reads=[KNT_B, QPT_B], writes=[paB], inc=(kc == 1))
                    S.op("dve", lambda e, pa=pa, tb=tb: e.tensor_tensor(out=ATA[:, tb, :], in0=pa[:, 0:128], in1=tri, op=ALU.mult),
                         reads=[paB, tri_B], writes=[ATA_B[tb]])

                def part_b(tb, po, poB):
                    r = tb % 2
                    S.op("dve", lambda e: e.reciprocal(out=RR2[r], in_=RR2[r]), reads=[RR2_B[r]], writes=[RR2_B[r]])
                    S.op("dve", lambda e: e.tensor_tensor(out=TT4.rearrange("p (v t) -> p v t", v=4),
                                                          in0=po[:, 0:512].rearrange("p (v t) -> p v t", v=4),
                                                          in1=RR2[r].unsqueeze(1).broadcast_to([128, 4, 128]), op=ALU.mult),
                         reads=[poB, RR2_B[r]], writes=[TT4_B])
                    S.op("dve", lambda e: e.tensor_tensor(out=OGT[:, h * 4:(h + 1) * 4, tb * 128:(tb + 1) * 128],
                                                          in0=TT4.rearrange("p (v t) -> p v t", v=4),
                                                          in1=SG[:, :, tb * 128:(tb + 1) * 128], op=ALU.mult),
                         reads=[TT4_B, SG_B], writes=[OGT_B])

                pend = None
                for tb in range(nb):
                    last = (tb == nb - 1)
                    if not last:
                        for kc in range(2):
                            pu, puB = PS()
                            S.op("pe", lambda e, kc=kc, tb=tb, pu=pu: e.matmul(pu[:, 0:512], lhsT=KHt[:, tb, kc * 128:(kc + 1) * 128], rhs=Vh[:, tb, :],
                                                                             start=True, stop=True),
                                 reads=[KHt_B, Vh_B], writes=[puB])
                            S.op("dve", lambda e, kc=kc, tb=tb, pu=pu: e.scalar_tensor_tensor(out=Sst[h][:, kc, :], in0=Sst[h][:, kc, :],
                                                                                            scalar=EBL[:, kc, tb:tb + 1], in1=pu[:, 0:512],
                                                                                            op0=ALU.mult, op1=ALU.add),
                                 reads=[EBL_B, puB], writes=[Sst_B[h][kc]])
                    if pend is not None:
                        part_b(*pend)
                    po, poB = PSA()
                    for vc in range(4):
                        for kc in range(2):
                            S.op("pe", lambda e, kc=kc, vc=vc, tb=tb, po=po: e.matmul(po[:, vc * 128:(vc + 1) * 128],
                                                                                    lhsT=Sbf[:, kc, vc * 128:(vc + 1) * 128],
                                                                                    rhs=QPT[:, kc, tb * 128:(tb + 1) * 128],
                                                                                    start=(kc == 0), stop=False),
                                 reads=[Sbf_B[kc], QPT_B], writes=[poB], inc=False)
                        S.op("pe", lambda e, vc=vc, tb=tb, po=po: e.matmul(po[:, vc * 128:(vc + 1) * 128], lhsT=Vh[:, tb, vc * 128:(vc + 1) * 128],
                                                                         rhs=ATA[:, tb, :], start=False, stop=True),
                             reads=[Vh_B, ATA_B[tb]], writes=[poB], inc=(vc == 3))
                    if not last:
                        for kc in range(2):
                            S.op("act", lambda e, kc=kc: e.copy(out=Sbf[:, kc, :], in_=Sst[h][:, kc, :]),
                                 reads=[Sst_B[h][kc]], writes=[Sbf_B[kc]])
                    r = tb % 2
                    pr, prB = PS()
                    S.op("act", lambda e, po=po: e.activation(out=SQ4, in_=po[:, 0:512], func=AF.Square, scale=1.0, bias=C0),
                         reads=[poB, cst_B], writes=[SQ4_B])
                    for vc in range(4):
                        S.op("pe", lambda e, vc=vc, pr=pr: e.matmul(pr[:, 0:128], lhsT=onesb, rhs=SQ4[:, vc * 128:(vc + 1) * 128],
                                                                    start=(vc == 0), stop=(vc == 3)),
                             reads=[onesb_B, SQ4_B], writes=[prB], inc=(vc == 3))
                    S.op("act", lambda e, pr=pr, r=r: e.activation(out=RR2[r], in_=pr[:, 0:128], func=AF.Sqrt, scale=1.0 / 512, bias=CEPS_RMS),
                         reads=[prB, cst_B], writes=[RR2_B[r]])
                    pend = (tb, po, poB)
                part_b(*pend)
            if KSTOP == 54:
                S.barrier()
                return nc
    S.barrier()

    if KSTOP == 5:
        return nc
    OFF_MG = OFF_LOCAL
    OFF_X1B = OFF_MG + 36864
    OFF_WP3 = OFF_X1B + 36864
    at(OFF_MG)
    MG = sb("MG", [128, DC, NOWN], BF16); MG_B = Buf()
    X1b = sb("X1b", [128, DC, NOWN], BF16); X1b_B = Buf()
    wp_setup(6, 4096)
    assert ar["off"] <= NA_BYTES
    at(OFF_X1B)
    SA = sb("SA", [128, 512], F32); SA_B = Buf()
    SB_ = sb("SB_", [128, 512], F32); SB_B = Buf()
    T1 = sb("T1", [128, 384], F32); T1_B = Buf()
    T2 = sb("T2", [128, 384], F32); T2_B = Buf()
    for dch in range(DC):
        c0 = dch * 128
        wa, waB = wpanel(w_in, C_MA + c0, 128)
        wb, wbB = wpanel(w_in, C_MB + c0, 128)
        wg_, wgB = wpanel(w_bg, c0, 128)
        wf_, wfB = wpanel(w_bf, c0, 128, kc=8)
        for (a, n) in ttiles(NOWN):
            proj_fm(wa, waB, 0, XT, XT_B, a, n,
                    lambda p, pB, n=n: S.op("act", lambda e: e.activation(out=SA[:, 0:n], in_=p[:, 0:n], func=AF.Sigmoid, scale=1.0, bias=C0),
                                            reads=[pB, cst_B], writes=[SA_B]))
            proj_fm(wb, wbB, 0, XT, XT_B, a, n,
                    lambda p, pB, n=n: S.op("act", lambda e: e.activation(out=SB_[:, 0:n], in_=p[:, 0:n], func=AF.Sigmoid, scale=1.0, bias=C0),
                                            reads=[pB, cst_B], writes=[SB_B]))
            proj_fm(wg_, wgB, 0, OGT, OGT_B, a, n,
                    lambda p, pB, n=n: S.op("dve", lambda e: e.tensor_tensor(out=T1[:, 0:n], in0=p[:, 0:n], in1=SA[:, 0:n], op=ALU.mult),
                                            reads=[pB, SA_B], writes=[T1_B]))
            proj_fm(wf_, wfB, 0, OFT, OFT_B, a, n,
                    lambda p, pB, n=n: S.op("dve", lambda e: e.tensor_tensor(out=T2[:, 0:n], in0=p[:, 0:n], in1=SB_[:, 0:n], op=ALU.mult),
                                            reads=[pB, SB_B], writes=[T2_B]), kc=8)
            S.op("dve", lambda e, a=a, n=n, dch=dch: e.tensor_tensor(out=MG[:, dch, a:a + n], in0=T1[:, 0:n], in1=T2[:, 0:n], op=ALU.add),
                 reads=[T1_B, T2_B], writes=[MG_B])
    S.barrier()

    if KSTOP == 6:
        return nc
    at(OFF_XT)
    R1 = sb("R1", [128, DC, NOWN], F32); R1_B = Buf()
    OFF_T = ar["off"]
    XF = [sb("XF%d" % i, [128, 512], F32) for i in range(2)]; XF_B = [Buf(), Buf()]
    MU = sb("MU", [128, 512], F32); MU_B = Buf()
    VR = sb("VR", [128, 512], F32); VR_B = Buf()
    SQ2 = [sb("SQ2%d" % i, [128, 512], F32) for i in range(2)]; SQ2_B = [Buf(), Buf()]
    TT2 = sb("TT2", [128, 512], F32); TT2_B = Buf()
    assert ar["off"] <= OFF_MG, ar["off"]
    xfc = {"i": 0}
    xTv = xT.rearrange("(dc p) t -> p dc t", p=128)
    for dch in range(DC):
        w, wB = wpanel(w_out, dch * 128, 128)
        for (a, n) in ttiles(NOWN):
            i = xfc["i"] % 2; xfc["i"] += 1
            S.dma("sp", XF[i][:, 0:n], xTv[:, dch, OWN0 + a:OWN0 + a + n], writes=[XF_B[i]])
            proj_fm(w, wB, 0, MG, MG_B, a, n,
                    lambda p, pB, i=i, a=a, n=n, dch=dch: S.op("dve", lambda e: e.scalar_tensor_tensor(out=R1[:, dch, a:a + n], in0=XF[i][:, 0:n], scalar=ALPHA,
                                                                                                  in1=p[:, 0:n], op0=ALU.mult, op1=ALU.add),
                                                               reads=[pB, XF_B[i]], writes=[R1_B]))

    def layer_norm(R, RB, ntok, gam, bet, outb, outbB):
        rbs = []
        for (a, n) in ttiles(ntok):
            p1, p1B = PS()
            for dc in range(DC):
                S.op("pe", lambda e, dc=dc: e.matmul(p1[:, 0:n], lhsT=ones, rhs=R[:, dc, a:a + n], start=(dc == 0), stop=(dc == DC - 1)),
                     reads=[ones_B, RB], writes=[p1B], inc=(dc == DC - 1))
            p2, p2B = PS()
            for dc in range(DC):
                i = dc % 2
                S.op("act", lambda e, dc=dc, i=i: e.activation(out=SQ2[i][:, 0:n], in_=R[:, dc, a:a + n], func=AF.Square, scale=1.0, bias=C0),
                     reads=[RB, cst_B], writes=[SQ2_B[i]])
                S.op("pe", lambda e, dc=dc, i=i: e.matmul(p2[:, 0:n], lhsT=ones, rhs=SQ2[i][:, 0:n], start=(dc == 0), stop=(dc == DC - 1)),
                     reads=[ones_B, SQ2_B[i]], writes=[p2B])
            S.op("dve", lambda e: e.tensor_scalar(out=MU[:, 0:n], in0=p1[:, 0:n], scalar1=1.0 / D, scalar2=None, op0=ALU.mult),
                 reads=[p1B], writes=[MU_B])
            S.op("dve", lambda e: e.tensor_tensor(out=TT2[:, 0:n], in0=MU[:, 0:n], in1=MU[:, 0:n], op=ALU.mult),
                 reads=[MU_B], writes=[TT2_B])
            S.op("dve", lambda e: e.scalar_tensor_tensor(out=VR[:, 0:n], in0=p2[:, 0:n], scalar=1.0 / D, in1=TT2[:, 0:n],
                                                         op0=ALU.mult, op1=ALU.subtract),
                 reads=[p2B, TT2_B], writes=[VR_B])
            S.op("act", lambda e: e.activation(out=VR[:, 0:n], in_=VR[:, 0:n], func=AF.Sqrt, scale=1.0, bias=CEPS_LN),
                 reads=[VR_B, cst_B], writes=[VR_B])
            S.op("dve", lambda e: e.reciprocal(out=VR[:, 0:n], in_=VR[:, 0:n]), reads=[VR_B], writes=[VR_B])
            for dc in range(DC):
                rb = Buf()
                rbs.append(rb)
                S.op("dve", lambda e, dc=dc: e.tensor_tensor(out=R[:, dc, a:a + n], in0=R[:, dc, a:a + n], in1=MU[:, 0:n], op=ALU.subtract),
                     reads=[RB, MU_B, VR_B], writes=[rb])
                S.op("dve", lambda e, dc=dc: e.tensor_tensor(out=R[:, dc, a:a + n], in0=R[:, dc, a:a + n], in1=VR[:, 0:n], op=ALU.mult),
                     reads=[VR_B], writes=[rb])
                S.op("act", lambda e, dc=dc: e.activation(out=outb[:, dc, a:a + n], in_=R[:, dc, a:a + n], func=AF.Identity,
                                                          scale=gam[:, dc:dc + 1], bias=bet[:, dc:dc + 1]),
                     reads=[rb, prm_B], writes=[outbB])
                S.op("act", lambda e, dc=dc: e.activation(out=R[:, dc, a:a + n], in_=R[:, dc, a:a + n], func=AF.Identity,
                                                          scale=gam[:, dc:dc + 1], bias=bet[:, dc:dc + 1]),
                     reads=[prm_B], writes=[rb])
        S.op("act", lambda e: e.copy(out=VR[:, 0:1], in_=VR[:, 0:1]), reads=rbs, writes=[RB, VR_B])

    layer_norm(R1, R1_B, NOWN, ln1g, ln1b, X1b, X1b_B)
    x1sv = x1s.rearrange("(dc p) t -> p dc t", p=128)
    x1s_B = Buf()
    for g in range(4):
        S.dma("sp", x1sv[:, 4 * g:4 * g + 4, :], R1[:, 4 * g:4 * g + 4, 128:NOWN], reads=[R1_B], writes=[x1s_B])
    S.barrier()

    if KSTOP == 7:
        return nc
    at(OFF_WP3)
    wp_setup(4, 6144)
    at(OFF_XT)
    R2 = sb("R2", [128, DC, NREAL], F32); R2_B = Buf()
    GE = sb("GE", [128, NREAL], F32); GE_B = Buf()
    XF = [sb("XFb%d" % i, [128, 512], F32) for i in range(2)]; XF_B = [Buf(), Buf()]
    HB = sb("HB", [128, FC // 2, NREAL], BF16); HB_B = Buf()
    GG = sb("GG", [128, NOWN], F32); GG_B = Buf()
    CV = sb("CV", [128, NREAL], F32); CV_B = Buf()
    assert ar["off"] <= OFF_X1B, ar["off"]
    for grp in range(2):
        for fl in range(FC // 2):
            fc = grp * (FC // 2) + fl
            wg_, wgB = wpanel(w_gate, fc * 128, 128)
            wu_, wuB = wpanel(w_up, fc * 128, 128)
            for (a, n) in ttiles(NOWN):
                proj_fm(wg_, wgB, 0, X1b, X1b_B, a, n, lambda p, pB, a=a, n=n: evac(GG[:, a:a + n], p[:, 0:n], [pB], [GG_B]))
            S.op("dve", lambda e: e.tensor_scalar(out=GG[:, 126:128], in0=GG[:, 126:128], scalar1=halo[:, 0:1], scalar2=None, op0=ALU.mult),
                 reads=[GG_B, halo_B], writes=[GG_B])
            S.op("dve", lambda e, fc=fc: e.tensor_scalar(out=CV, in0=GG[:, 126:126 + NREAL], scalar1=convw[:, fc, 0:1], scalar2=convb[:, fc:fc + 1],
                                                         op0=ALU.mult, op1=ALU.add),
                 reads=[GG_B, prm_B], writes=[CV_B])
            S.op("dve", lambda e, fc=fc: e.scalar_tensor_tensor(out=CV, in0=GG[:, 127:127 + NREAL], scalar=convw[:, fc, 1:2], in1=CV,
                                                                op0=ALU.mult, op1=ALU.add),
                 reads=[GG_B, prm_B, CV_B], writes=[CV_B])
            S.op("dve", lambda e, fc=fc: e.scalar_tensor_tensor(out=CV, in0=GG[:, 128:128 + NREAL], scalar=convw[:, fc, 2:3], in1=CV,
                                                                op0=ALU.mult, op1=ALU.add),
                 reads=[GG_B, prm_B, CV_B], writes=[CV_B])
            S.op("act", lambda e: e.activation(out=GE, in_=CV, func=AF.Gelu, scale=1.0, bias=C0), reads=[CV_B, cst_B], writes=[GE_B])
            for (a, n) in ttiles(NREAL):
                proj_fm(wu_, wuB, 0, X1b[:, :, 128:NOWN], X1b_B, a, n,
                        lambda p, pB, a=a, n=n, fl=fl: S.op("dve", lambda e: e.tensor_tensor(out=HB[:, fl, a:a + n], in0=p[:, 0:n], in1=GE[:, a:a + n], op=ALU.mult),
                                                            reads=[pB, GE_B], writes=[HB_B]))
        for dch in range(DC):
            w, wB = wpanel(w_down, dch * 128, 128, kc=FC // 2, kc0=grp * (FC // 2))
            for (a, n) in ttiles(NREAL):
                if grp == 0:
                    k = xfc["i"] % 2; xfc["i"] += 1
                    S.dma("sp", XF[k][:, 0:n], x1sv[:, dch, a:a + n], reads=[x1s_B], writes=[XF_B[k]])
                    proj_fm(w, wB, 0, HB, HB_B, a, n,
                            lambda p, pB, k=k, a=a, n=n, dch=dch: S.op("dve", lambda e: e.scalar_tensor_tensor(out=R2[:, dch, a:a + n], in0=XF[k][:, 0:n], scalar=ALPHA,
                                                                                                          in1=p[:, 0:n], op0=ALU.mult, op1=ALU.add),
                                                                       reads=[pB, XF_B[k]], writes=[R2_B]), kc=FC // 2)
                else:
                    proj_fm(w, wB, 0, HB, HB_B, a, n,
                            lambda p, pB, a=a, n=n, dch=dch: S.op("dve", lambda e: e.tensor_tensor(out=R2[:, dch, a:a + n], in0=R2[:, dch, a:a + n],
                                                                                                 in1=p[:, 0:n], op=ALU.add),
                                                                  reads=[pB, R2_B], writes=[R2_B]), kc=FC // 2)
    S.barrier()
    if KSTOP == 8:
        return nc
    at(86016)
    MU = sb("MUb", [128, 512], F32); MU_B = Buf()
    VR = sb("VRb", [128, 512], F32); VR_B = Buf()
    SQ2 = [sb("SQ2b%d" % i, [128, 512], F32) for i in range(2)]; SQ2_B = [Buf(), Buf()]
    TT2 = sb("TT2b", [128, 512], F32); TT2_B = Buf()
    X2b = X1b; X2b_B = X1b_B
    layer_norm(R2, R2_B, NREAL, ln2g, ln2b, X2b, X2b_B)

    PTb = sb("PTb", [128, 2, NREAL], BF16); PTb_B = Buf()
    S.dma("pool", PTb, pT.rearrange("(kc p) t -> p kc t", p=128), writes=[PTb_B])
    YO = [sb("YO%d" % i, [128, 512], F32) for i in range(2)]; YO_B = [Buf(), Buf()]
    SGT = sb("SGT", [128, 512], F32); SGT_B = Buf()
    yTv = yT.rearrange("(dc p) t -> p dc t", p=128)
    yc = {"i": 0}
    for dch in range(DC):
        w, wB = wpanel(w_pg, dch * 128, 128)
        w2, w2B = wpanel(w_pp, dch * 128, 128, kc=2)
        for (a, n) in ttiles(NREAL):
            proj_fm(w, wB, 0, X2b, X2b_B, a, n,
                    lambda p, pB, n=n: S.op("act", lambda e: e.activation(out=SGT[:, 0:n], in_=p[:, 0:n], func=AF.Sigmoid, scale=1.0, bias=C0),
                                            reads=[pB, cst_B], writes=[SGT_B]))
            k = yc["i"] % 2; yc["i"] += 1
            proj_fm(w2, w2B, 0, PTb, PTb_B, a, n,
                    lambda p, pB, n=n, k=k: S.op("dve", lambda e: e.tensor_tensor(out=YO[k][:, 0:n], in0=p[:, 0:n], in1=SGT[:, 0:n], op=ALU.mult),
                                                 reads=[pB, SGT_B], writes=[YO_B[k]]), kc=2)
            S.op("dve", lambda e, k=k, a=a, n=n, dch=dch: e.tensor_tensor(out=YO[k][:, 0:n], in0=YO[k][:, 0:n], in1=R2[:, dch, a:a + n], op=ALU.add),
                 reads=[YO_B[k], R2_B], writes=[YO_B[k]])
            S.dma("sp", yTv[:, dch, a:a + n], YO[k][:, 0:n], reads=[YO_B[k]])
    S.barrier()
    nc._marks = S.marks
    return nc


_CACHE = {}


def kernel(x, p, w_in, w_gla_lr, b_gla_lr, gla_norm_g, b_forget, w_branch_gla, w_branch_fox, w_out,
           ln1_g, ln1_b, w_gate, w_up, conv_w, conv_b, w_down, ln2_g, ln2_b, w_ple_gate, w_ple_proj):
    f = np.float32
    x = np.asarray(x, f); p = np.asarray(p, f)
    B = x.shape[0]

    def pc(v, n):
        return np.ascontiguousarray(np.asarray(v, f).reshape(n, 128).T)

    shared = {
        "ident": np.eye(128, dtype=f),
        "tri": np.ascontiguousarray(np.triu(np.ones((128, 128), f))),
        "reset": np.ascontiguousarray(np.tile((np.arange(NOWN) % 128 != 0).astype(f)[None, :], (128, 1))),
        "w_in": np.ascontiguousarray(np.asarray(w_in, f)[0]),
        "w_gla_lr": np.ascontiguousarray(np.asarray(w_gla_lr, f)[0]),
        "blr": pc(np.asarray(b_gla_lr)[0], 8),
        "gn": pc(np.asarray(gla_norm_g)[0], 4),
        "bfor": np.ascontiguousarray(np.tile(np.asarray(b_forget, f)[0][None, :], (128, 1))),
        "w_branch_gla": np.ascontiguousarray(np.asarray(w_branch_gla, f)[0]),
        "w_branch_fox": np.ascontiguousarray(np.asarray(w_branch_fox, f)[0]),
        "w_out": np.ascontiguousarray(np.asarray(w_out, f)[0]),
        "ln1g": pc(np.asarray(ln1_g)[0], DC), "ln1b": pc(np.asarray(ln1_b)[0], DC),
        "w_gate": np.ascontiguousarray(np.asarray(w_gate, f)[0]),
        "w_up": np.ascontiguousarray(np.asarray(w_up, f)[0]),
        "convw": np.ascontiguousarray(np.asarray(conv_w, f)[0].reshape(3, FC, 128).transpose(2, 1, 0)),
        "convb": pc(np.asarray(conv_b)[0], FC),
        "w_down": np.ascontiguousarray(np.asarray(w_down, f)[0]),
        "ln2g": pc(np.asarray(ln2_g)[0], DC), "ln2b": pc(np.asarray(ln2_b)[0], DC),
        "w_ple_gate": np.ascontiguousarray(np.asarray(w_ple_gate, f)[0]),
        "w_ple_proj": np.ascontiguousarray(np.asarray(w_ple_proj, f)[0]),
    }
    in_maps = []
    for c in range(8):
        b, j = c // 4, c % 4
        g0 = 1024 * j + 1024 - NWIN
        xw = np.zeros((NWIN, D), f)
        lo = max(g0, 0)
        xw[lo - g0:, :] = x[b, lo:1024 * j + 1024, :]
        valid = (np.arange(NWIN) + g0) >= 0
        km = np.where(valid, 0.0, -30000.0).astype(f).reshape(NBLK, 128).T
        m = dict(shared)
        m["xT"] = np.ascontiguousarray(xw.T)
        m["pT"] = np.ascontiguousarray(p[0, b, 1024 * j:1024 * j + 1024, :].T)
        m["kmask"] = np.ascontiguousarray(km)
        m["halo"] = np.full((128, 1), 0.0 if j == 0 else 1.0, f)
        in_maps.append(m)
    if "nc" not in _CACHE:
        _CACHE["nc"] = build_program()
    res = run_bass_kernel_spmd(_CACHE["nc"], in_maps, core_ids=list(range(8)))
    out = np.empty((B, S_LEN, D), f)
    for c in range(8):
        b, j = c // 4, c % 4
        out[b, 1024 * j:1024 * j + 1024, :] = res.results[c]["yT"].T
    return out
```
